# Optimizing a Trainium2 kernel written in Bass

```python
import math
import jax
import jax.numpy as jnp
from jax import lax
import numpy as np

D_MODEL = 2048
BATCH = 16
SEQ = 2048
DEPTH = 4
DEC_BATCH = 32
DEC_SEQ = 32
PAST_LEN = 1024

CHUNK = 64
A_DK = 128
A_DV = 128
A_W = D_MODEL // 2
A_HEADS = A_W // A_DK
B_W = D_MODEL // 2
B_HEADDIM = 64
B_HEADS = B_W // B_HEADDIM
B_GROUPS = 2
B_DSTATE = 128
CONV_W = 4
B_CONV_DIM = B_W + 2 * B_GROUPS * B_DSTATE
C_W = D_MODEL // 2
C_GROUPS = 4
CMLP_CHUNK = 128
N_BRANCH = 3
D_FF = 4 * D_MODEL
IN_SIZES = (A_W, A_W, A_W, A_W, B_W, B_CONV_DIM, B_HEADS, C_W, C_W, N_BRANCH * D_MODEL)
IN_TOTAL = 4 * A_W + B_W + B_CONV_DIM + B_HEADS + 2 * C_W + N_BRANCH * D_MODEL
NORM_EPS = 1e-6
LB_FLOOR = 1e-30

kernel_name = 'hybrid_hgrn2_ssd_gmlp_stream_step'


def rms_norm(x, g):
    xf = x.astype(jnp.float32)
    y = xf * lax.rsqrt(jnp.mean(xf * xf, axis=-1, keepdims=True) + NORM_EPS)
    return (y * g.astype(jnp.float32)).astype(x.dtype)


def group_rms_norm(x, g, groups):
    shp = x.shape
    xg = x.astype(jnp.float32).reshape(shp[:-1] + (groups, shp[-1] // groups))
    y = xg * lax.rsqrt(jnp.mean(xg * xg, axis=-1, keepdims=True) + NORM_EPS)
    return y.reshape(shp) * g.astype(jnp.float32)


def layer_norm(x, g, b):
    xf = x.astype(jnp.float32)
    mu = jnp.mean(xf, axis=-1, keepdims=True)
    var = jnp.mean(jnp.square(xf - mu), axis=-1, keepdims=True)
    y = (xf - mu) * lax.rsqrt(var + NORM_EPS) * g.astype(jnp.float32) + b.astype(jnp.float32)
    return y.astype(x.dtype)


def masked_exp(diff, mask):
    return jnp.where(mask, jnp.exp(jnp.where(mask, diff, 0.0)), 0.0)


def segsum_exp(cs):
    t = cs.shape[-1]
    mask = jnp.tril(jnp.ones((t, t), dtype=bool))
    return masked_exp(cs[..., :, None] - cs[..., None, :], mask)


def gla_chunked(q, k, v, log_f, s0, qc):
    bsz, L, H, _ = q.shape
    n = L // qc
    causal = jnp.tril(jnp.ones((qc, qc), dtype=bool))[None, :, :, None, None]

    def to_blocks(t):
        return jnp.moveaxis(t.reshape((bsz, n, qc) + t.shape[2:]), 1, 0)

    def step(s, inp):
        qb, kb, vb, gb = inp
        b = jnp.cumsum(gb, axis=1)
        o_inter = jnp.einsum('bthk,bhkv->bthv', qb * jnp.exp(b), s)
        decay = masked_exp(b[:, :, None] - b[:, None, :], causal)
        att = jnp.einsum('bthk,bshk,btshk->bhts', qb, kb, decay)
        o_intra = jnp.einsum('bhts,bshv->bthv', att, vb)
        b_last = b[:, -1]
        s_new = jnp.exp(b_last)[..., None] * s + jnp.einsum(
            'bshk,bshv->bhkv', kb * jnp.exp(b_last[:, None] - b), vb)
        return s_new, o_inter + o_intra

    s_fin, o = lax.scan(step, s0, (to_blocks(q), to_blocks(k), to_blocks(v), to_blocks(log_f)))
    o = jnp.moveaxis(o, 0, 1).reshape(bsz, L, H, v.shape[-1])
    return o, s_fin


def ssd_chunked(xh, dt, a_neg, bh, ch, s0, qc):
    bsz, L, H, P = xh.shape
    nc = L // qc

    def blk(t):
        return t.reshape((bsz, nc, qc) + t.shape[2:])

    x_, dt_, b_, c_ = blk(xh), blk(dt), blk(bh), blk(ch)
    a = jnp.moveaxis(dt_ * a_neg, -1, 1)
    a_cum = jnp.cumsum(a, axis=-1)
    lmat = segsum_exp(a_cum)
    xdt = x_ * dt_[..., None]
    y_diag = jnp.einsum('bclhn,bcshn,bhcls,bcshp->bclhp', c_, b_, lmat, xdt)
    decay_states = jnp.exp(a_cum[..., -1:] - a_cum)
    states = jnp.einsum('bcshn,bhcs,bcshp->bchpn', b_, decay_states, xdt)
    states = jnp.concatenate([s0[:, None], states], axis=1)
    chunk_cs = jnp.cumsum(jnp.pad(a_cum[..., -1], ((0, 0), (0, 0), (1, 0))), axis=-1)
    decay_chunk = segsum_exp(chunk_cs)
    new_states = jnp.einsum('bhzc,bchpn->bzhpn', decay_chunk, states)
    states_in, s_fin = new_states[:, :-1], new_states[:, -1]
    y_off = jnp.einsum('bclhn,bchpn,bhcl->bclhp', c_, states_in, jnp.exp(a_cum))
    return (y_diag + y_off).reshape(bsz, L, H, P), s_fin


def chunk_token_mix(v, ws, bs):
    bsz, L, W = v.shape
    lc = min(L, CMLP_CHUNK)
    n = L // lc
    w = jnp.tril(ws)[:, :lc, :lc]
    vv = v.reshape(bsz, n, lc, C_GROUPS, W // C_GROUPS)
    out = jnp.einsum('gts,bnsgd->bntgd', w, vv) + bs[:, :lc].T[None, None, :, :, None]
    return out.reshape(bsz, L, W)


def sq_relu_mlp(h, w_up, w_down):
    return jnp.square(jax.nn.relu(h @ w_up)) @ w_down


def mixer_block(h, s_hgrn, s_ssm, conv_buf, lb, w_in, hgrn_onorm_g, ssm_conv_w, ssm_conv_b,
                ssm_dt_bias, ssm_a_log, ssm_d, ssm_onorm_g, cmlp_ln_g, cmlp_ln_b, cmlp_ws, cmlp_bs,
                w_branch, w_out):
    bsz, L, _ = h.shape
    dtp = h.dtype
    f32 = jnp.float32
    qc = min(L, CHUNK)
    split_at = [int(i) for i in np.cumsum(IN_SIZES)[:-1]]
    a_q, a_f, a_i, a_g, b_z, b_xbc, b_dt, c_u, c_v, gate_in = jnp.split(h @ w_in, split_at, axis=-1)

    lbf = lb.astype(f32).reshape(A_HEADS, A_DK)
    zf = a_f.astype(f32).reshape(bsz, L, A_HEADS, A_DK)
    log_lb = jnp.log(jnp.maximum(lbf, LB_FLOOR))
    log_f = jnp.logaddexp(log_lb, jnp.log1p(-lbf) + jax.nn.log_sigmoid(zf))
    k_in = (1.0 - lbf) * jax.nn.sigmoid(-zf)
    q = jax.nn.silu(a_q.astype(f32)).reshape(bsz, L, A_HEADS, A_DK)
    i_v = a_i.astype(f32).reshape(bsz, L, A_HEADS, A_DV)
    o_a, s_hgrn_new = gla_chunked(q, k_in, i_v, log_f, s_hgrn.astype(f32), qc)
    y_a = group_rms_norm(o_a.reshape(bsz, L, A_W), hgrn_onorm_g, A_HEADS) * jax.nn.silu(a_g.astype(f32))

    xpad = jnp.concatenate([conv_buf.astype(dtp), b_xbc], axis=1)
    conv = ssm_conv_b.astype(dtp)
    for j in range(CONV_W):
        conv = conv + xpad[:, j:j + L] * ssm_conv_w[j]
    xbc = jax.nn.silu(conv)
    conv_new = xpad[:, L:]
    x_s, b_s, c_s = jnp.split(xbc, [B_W, B_W + B_GROUPS * B_DSTATE], axis=-1)
    rep = B_HEADS // B_GROUPS
    xh = x_s.astype(f32).reshape(bsz, L, B_HEADS, B_HEADDIM)
    bh = jnp.repeat(b_s.astype(f32).reshape(bsz, L, B_GROUPS, B_DSTATE), rep, axis=2)
    ch = jnp.repeat(c_s.astype(f32).reshape(bsz, L, B_GROUPS, B_DSTATE), rep, axis=2)
    dt = jax.nn.softplus(b_dt.astype(f32) + ssm_dt_bias.astype(f32))
    a_neg = -jnp.exp(ssm_a_log.astype(f32))
    y_b, s_ssm_new = ssd_chunked(xh, dt, a_neg, bh, ch, s_ssm.astype(f32), qc)
    y_b = (y_b + ssm_d.astype(f32)[:, None] * xh).reshape(bsz, L, B_W)
    y_b = group_rms_norm(y_b * jax.nn.silu(b_z.astype(f32)), ssm_onorm_g, B_GROUPS)

    u = jax.nn.gelu(c_u, approximate=False)
    v = layer_norm(jax.nn.gelu(c_v, approximate=False), cmlp_ln_g, cmlp_ln_b)
    y_c = u * chunk_token_mix(v, cmlp_ws, cmlp_bs)

    gates = jax.nn.sigmoid(gate_in).reshape(bsz, L, N_BRANCH, D_MODEL)
    merged = (gates[:, :, 0] * (y_a.astype(dtp) @ w_branch[:A_W])
              + gates[:, :, 1] * (y_b.astype(dtp) @ w_branch[A_W:A_W + B_W])
              + gates[:, :, 2] * (y_c @ w_branch[A_W + B_W:]))
    return (merged @ w_out, s_hgrn_new, s_ssm_new, conv_new, v)


def run_trunk(x, c, st_hgrn, st_ssm, st_conv, lbs, keep_chunk_rows, norm1_g, norm2_g, w_mod, b_mod,
              w_in, hgrn_onorm_g, ssm_conv_w, ssm_conv_b, ssm_dt_bias, ssm_a_log, ssm_d, ssm_onorm_g,
              cmlp_ln_g, cmlp_ln_b, cmlp_ws, cmlp_bs, w_branch, w_out, w_up, w_down, final_g):
    cs = jax.nn.silu(c)
    hgrn_out, ssm_out, conv_out, v_out = [], [], [], []
    for l in range(DEPTH):
        mod = cs @ w_mod[l] + b_mod[l]
        sh1, sc1, g1, sh2, sc2, g2 = [m[:, None, :] for m in jnp.split(mod, 6, axis=-1)]
        h = rms_norm(x, norm1_g[l]) * (1 + sc1) + sh1
        y, s_h, s_s, cb, v = mixer_block(
            h, st_hgrn[l], st_ssm[l], st_conv[l], lbs[l], w_in[l], hgrn_onorm_g[l], ssm_conv_w[l],
            ssm_conv_b[l], ssm_dt_bias[l], ssm_a_log[l], ssm_d[l], ssm_onorm_g[l], cmlp_ln_g[l],
            cmlp_ln_b[l], cmlp_ws[l], cmlp_bs[l], w_branch[l], w_out[l])
        x = x + g1 * y
        h = rms_norm(x, norm2_g[l]) * (1 + sc2) + sh2
        x = x + g2 * sq_relu_mlp(h, w_up[l], w_down[l])
        hgrn_out.append(s_h.astype(x.dtype))
        ssm_out.append(s_s.astype(x.dtype))
        conv_out.append(cb.astype(x.dtype))
        if keep_chunk_rows:
            v_out.append(v)
    y = rms_norm(x, final_g)
    v_stack = jnp.stack(v_out) if keep_chunk_rows else None
    return (y, jnp.stack(hgrn_out), jnp.stack(ssm_out), jnp.stack(conv_out), v_stack)


def setup_inputs(seed: int = 0) -> dict:
    key = jax.random.key(seed)
    ks = jax.random.split(key, 32)
    f32 = jnp.float32

    def nrm(k, shape, s):
        return s * jax.random.normal(k, shape, f32)

    dt0 = jnp.exp(jax.random.uniform(ks[16], (DEPTH, B_HEADS), f32, math.log(1e-3), math.log(1e-1)))
    return {
        'x_prompt': nrm(ks[0], (BATCH, SEQ, D_MODEL), 1.0),
        'x_sample': nrm(ks[1], (DEC_BATCH, DEC_SEQ, D_MODEL), 1.0),
        'state_hgrn': nrm(ks[2], (DEPTH, DEC_BATCH, A_HEADS, A_DK, A_DV), 0.5),
        'state_ssm': nrm(ks[3], (DEPTH, DEC_BATCH, B_HEADS, B_HEADDIM, B_DSTATE), 0.1),
        'state_conv': nrm(ks[4], (DEPTH, DEC_BATCH, CONV_W - 1, B_CONV_DIM), 1.0),
        'c_prompt': nrm(ks[5], (BATCH, D_MODEL), 1.0),
        'c_sample': nrm(ks[6], (DEC_BATCH, D_MODEL), 1.0),
        'norm1_g': 1.0 + nrm(ks[7], (DEPTH, D_MODEL), 0.05),
        'norm2_g': 1.0 + nrm(ks[8], (DEPTH, D_MODEL), 0.05),
        'w_mod': nrm(ks[9], (DEPTH, D_MODEL, 6 * D_MODEL), 0.5 * D_MODEL ** -0.5),
        'b_mod': nrm(ks[10], (DEPTH, 6 * D_MODEL), 0.1),
        'w_in': nrm(ks[11], (DEPTH, D_MODEL, IN_TOTAL), D_MODEL ** -0.5),
        'hgrn_lb': nrm(ks[12], (DEPTH, A_W), 0.5),
        'hgrn_onorm_g': 1.0 + nrm(ks[13], (DEPTH, A_W), 0.05),
        'ssm_conv_w': nrm(ks[14], (DEPTH, CONV_W, B_CONV_DIM), CONV_W ** -0.5),
        'ssm_conv_b': nrm(ks[15], (DEPTH, B_CONV_DIM), 0.02),
        'ssm_dt_bias': dt0 + jnp.log(-jnp.expm1(-dt0)),
        'ssm_a_log': jnp.log(jax.random.uniform(ks[17], (DEPTH, B_HEADS), f32, 1.0, 16.0)),
        'ssm_d': 1.0 + nrm(ks[18], (DEPTH, B_HEADS), 0.1),
        'ssm_onorm_g': 1.0 + nrm(ks[19], (DEPTH, B_W), 0.05),
        'cmlp_ln_g': 1.0 + nrm(ks[20], (DEPTH, C_W), 0.05),
        'cmlp_ln_b': nrm(ks[21], (DEPTH, C_W), 0.02),
        'cmlp_ws': nrm(ks[22], (DEPTH, C_GROUPS, CMLP_CHUNK, CMLP_CHUNK), CMLP_CHUNK ** -0.5),
        'cmlp_bs': 1.0 + nrm(ks[23], (DEPTH, C_GROUPS, CMLP_CHUNK), 0.1),
        'w_branch': nrm(ks[24], (DEPTH, A_W + B_W + C_W, D_MODEL), A_W ** -0.5),
        'w_out': nrm(ks[25], (DEPTH, D_MODEL, D_MODEL), D_MODEL ** -0.5),
        'w_up': nrm(ks[26], (DEPTH, D_MODEL, D_FF), D_MODEL ** -0.5),
        'w_down': nrm(ks[27], (DEPTH, D_FF, D_MODEL), D_FF ** -0.5),
        'final_g': 1.0 + nrm(ks[28], (D_MODEL,), 0.05),
    }


def reference(x_prompt, x_sample, state_hgrn, state_ssm, state_conv, c_prompt, c_sample, norm1_g,
              norm2_g, w_mod, b_mod, w_in, hgrn_lb, hgrn_onorm_g, ssm_conv_w, ssm_conv_b, ssm_dt_bias,
              ssm_a_log, ssm_d, ssm_onorm_g, cmlp_ln_g, cmlp_ln_b, cmlp_ws, cmlp_bs, w_branch, w_out,
              w_up, w_down, final_g):
    p = jax.nn.softmax(hgrn_lb.astype(jnp.float32), axis=0)
    lbs = jnp.cumsum(p, axis=0) - p[0]
    weights = (norm1_g, norm2_g, w_mod, b_mod, w_in, hgrn_onorm_g, ssm_conv_w, ssm_conv_b, ssm_dt_bias,
               ssm_a_log, ssm_d, ssm_onorm_g, cmlp_ln_g, cmlp_ln_b, cmlp_ws, cmlp_bs, w_branch, w_out,
               w_up, w_down, final_g)
    zero_hgrn = jnp.zeros((DEPTH, BATCH, A_HEADS, A_DK, A_DV), jnp.float32)
    zero_ssm = jnp.zeros((DEPTH, BATCH, B_HEADS, B_HEADDIM, B_DSTATE), jnp.float32)
    zero_conv = jnp.zeros((DEPTH, BATCH, CONV_W - 1, B_CONV_DIM), x_prompt.dtype)
    y_prompt, hgrn_p, ssm_p, conv_p, _ = run_trunk(
        x_prompt, c_prompt, zero_hgrn, zero_ssm, zero_conv, lbs, False, *weights)
    y_sample, hgrn_s, ssm_s, conv_s, cmlp_v_s = run_trunk(
        x_sample, c_sample, state_hgrn, state_ssm, state_conv, lbs, True, *weights)
    return (y_prompt, y_sample, hgrn_p, ssm_p, conv_p, hgrn_s, ssm_s, conv_s, cmlp_v_s)
```

```python
import numpy as np
from contextlib import ExitStack
import concourse.bass as bass
import concourse.mybir as mybir
from concourse.bass_utils import run_bass_kernel_spmd

F32 = mybir.dt.float32
BF16 = mybir.dt.bfloat16
U32 = mybir.dt.uint32
AF = mybir.ActivationFunctionType
ALU = mybir.AluOpType

D = 2048
KC = 16
A_W = 1024
IN_TOTAL = 14864
OFF_Q, OFF_F, OFF_I, OFF_G, OFF_Z, OFF_XBC, OFF_DT, OFF_U, OFF_V, OFF_GATE = (
    0, 1024, 2048, 3072, 4096, 5120, 6656, 6672, 7696, 8720)
EPS = 1e-6
NEG = -30000.0


class Sched:
    ENGS = ("pe", "act", "dve", "pool", "sp")

    def __init__(self, nc):
        self.nc = nc
        self.ops = []
        self.last_w = {}
        self.readers = {}
        self.chan_last = {}
        self.chans = []
        self.tag = ""
        self.pe_log = None

    def add(self, eng, fn, r=(), w=(), dma=None):
        deps = set()
        for k in r:
            if k in self.last_w:
                deps.add(self.last_w[k])
        for k in w:
            if k in self.last_w:
                deps.add(self.last_w[k])
            deps.update(self.readers.get(k, ()))
        if dma is not None:
            if dma in self.chan_last:
                deps.add(self.chan_last[dma])
            else:
                self.chans.append(dma)
        i = len(self.ops)
        self.ops.append(dict(eng=eng, fn=fn, deps=deps, dma=dma, sig=False, tag=self.tag))
        for k in r:
            self.readers.setdefault(k, []).append(i)
        for k in w:
            self.last_w[k] = i
            self.readers[k] = []
        if dma is not None:
            self.chan_last[dma] = i
        return i

    def emit(self):
        nc = self.nc
        ops = self.ops
        for o in ops:
            for d in o["deps"]:
                ops[d]["sig"] = True
        cnt = {e: 0 for e in self.ENGS}
        ccnt = {c: 0 for c in self.chans}
        for o in ops:
            if o["dma"] is not None:
                ccnt[o["dma"]] += 16
                o["semk"] = ("c", o["dma"])
                o["val"] = ccnt[o["dma"]]
            elif o["sig"]:
                cnt[o["eng"]] += 1
                o["semk"] = ("e", o["eng"])
                o["val"] = cnt[o["eng"]]
        with ExitStack() as es:
            sems = {}
            for e in self.ENGS:
                sems[("e", e)] = es.enter_context(nc.semaphore("s_" + e))
            for c in self.chans:
                sems[("c", c)] = es.enter_context(nc.semaphore("d_" + str(c)))
            block = es.enter_context(nc.Block())
            per = {e: [o for o in ops if o["eng"] == e] for e in self.ENGS}
            final = [(("c", c), ccnt[c]) for c in self.chans]

            def run(e, h):
                waited = {}
                for o in per[e]:
                    need = {}
                    for d in o["deps"]:
                        p = ops[d]
                        k = p["semk"]
                        if p["val"] > need.get(k, 0):
                            need[k] = p["val"]
                    for k, v in need.items():
                        if waited.get(k, 0) < v:
                            h.wait_ge(sems[k], v)
                            waited[k] = v
                    if e == "pe" and self.pe_log is not None:
                        n0 = nc.n_instructions()
                        ins = o["fn"](h)
                        self.pe_log.append((o["tag"], nc.n_instructions() - n0))
                    else:
                        ins = o["fn"](h)
                    if o["dma"] is not None:
                        ins.then_inc(sems[o["semk"]], 16)
                    elif o["sig"]:
                        ins.then_inc(sems[o["semk"]], 1)
                if e == "sp":
                    for k, v in final:
                        if v > 0 and waited.get(k, 0) < v:
                            h.wait_ge(sems[k], v)

            @block.tensor
            def _(h):
                run("pe", h)

            @block.scalar
            def _(h):
                run("act", h)

            @block.vector
            def _(h):
                run("dve", h)

            @block.gpsimd
            def _(h):
                run("pool", h)

            @block.sync
            def _(h):
                run("sp", h)


def cst_layout(depth):
    items = [("ident", 128), ("n1g", depth * 16), ("n2g", depth * 16), ("fg", 16), ("bmod", depth * 96),
             ("lbraw", depth * 8), ("gon", depth * 8), ("convw", depth * 48), ("convb", depth * 12),
             ("dtb", depth * 16), ("alog", depth * 16), ("Dp", depth * 8), ("gbn", depth * 8),
             ("maskA", 8 * 64), ("blkm", 512),
             ("tri_p", 128), ("tri_s", 128), ("segb_p", 128), ("segb_s", 128),
             ("sel_s0", 128), ("sel_s1", 128), ("neg_p", 512), ("neg_s", 512), ("triu", 128), ("triu_s", 128)]
    lay = {}
    o = 0
    for n, c in items:
        lay[n] = (o, c)
        o += c
    return lay, o


def build(cfg):
    depth, LP, NPS = cfg["depth"], cfg["LP"], cfg["NPS"]
    NSS = 4
    TP = min(512, LP)
    lay, NCST = cst_layout(depth)
    nc = bass.Bass("TRN2", target_bir_lowering=False)

    def din(n, s, dt=F32):
        return nc.dram_tensor(n, list(s), dt, kind="ExternalInput").ap()

    def dout(n, s):
        return nc.dram_tensor(n, list(s), F32, kind="ExternalOutput").ap()

    xp = din("xp", [NPS, D, LP])
    xs = din("xs", [D, 256])
    cT = din("cT", [128, 16, NPS + 4])
    sh_in = din("sh_in", [depth, NSS, 8, 128, 128])
    ss_in = din("ss_in", [depth, NSS, 128, 16, 64])
    sc_in = din("sc_in", [depth, NSS, 128, 12, 3])
    w_mod = din("w_mod", [depth, D, 6 * D])
    w_in = din("w_in", [depth, D, IN_TOTAL])
    w_br = din("w_br", [depth, 3072, D])
    w_out = din("w_out", [depth, D, D])
    w_up = din("w_up", [depth, D, 4 * D])
    w_dn = din("w_dn", [depth, 4 * D, D])
    cst_d = din("cst", [128, NCST])
    lnp_d = din("lnp", [depth, 2, 1024])
    wsT_d = din("wsT", [depth, 128, 4, 128])
    bs_d = din("bs", [depth, 4, 128])

    yp = dout("yp", [NPS, D, LP])
    ys = dout("ys", [D, 256])
    hp = dout("hp", [depth, NPS, 8, 128, 128])
    sp_o = dout("sp_o", [depth, NPS, 128, 16, 64])
    cp = dout("cp", [depth, NPS, 128, 12, 3])
    hs = dout("hs", [depth, NSS, 8, 128, 128])
    sso = dout("sso", [depth, NSS, 128, 16, 64])
    cso = dout("cso", [depth, NSS, 128, 12, 3])
    vso = dout("vso", [depth, 256, 1024])

    WSPEC = {"w_in": (w_in, D, IN_TOTAL), "w_br": (w_br, 3072, D), "w_out": (w_out, D, D),
             "w_up": (w_up, D, 4 * D), "w_dn": (w_dn, 4 * D, D)}
    scr = {n_: nc.dram_tensor("scr_" + n_, [depth, k_, c_], BF16, kind="Internal").ap() for n_, (_, k_, c_) in WSPEC.items()}
    scr_keys = {}

    es = ExitStack()
    _n = [0]

    def sb(shape, dt, name=None):
        _n[0] += 1
        return es.enter_context(nc.sbuf_tensor(name or ("t%d" % _n[0]), list(shape), dt))

    S = Sched(nc)
    A = S.add
    QM = "pool"

    cst = sb([128, NCST], F32, "cst_sb")
    xT = sb([128, 16, 512], F32, "xT")
    hb = sb([128, 16, 512], BF16, "hb")
    W = [sb([128, 8192], BF16, "w%d" % i) for i in range(2)]
    NSEQ = NPS + NSS
    mgp = sb([128, 16, 512], BF16, "mg")
    modT = sb([128, depth, 96, NSEQ], F32, "modT")
    A1 = sb([128, depth, 16, NSEQ], F32, "A1")
    A2 = sb([128, depth, 16, NSEQ], F32, "A2")
    lb = sb([128, depth, 8], F32, "lb")
    oml = sb([128, depth, 8], F32, "oml")
    aneg = sb([128, depth, 16], F32, "aneg")
    idb = sb([128, 128], BF16, "idb")
    on11 = sb([128, 128], BF16, "on11")
    on7 = sb([128, 128], BF16, "on7")
    on9 = sb([128, 128], BF16, "on9")
    onf = sb([128, 128], F32, "onf")
    onrow = sb([1, 128], BF16, "onrow")
    WT = sb([128, 4, 128], BF16, "WT")
    brow = sb([1, 4, 128], BF16, "brow")
    convst = sb([128, depth, 12, 3], F32, "convst")
    csb = sb([128, 16, NSEQ], BF16, "csb")
    rstd = sb([128, 512], F32, "rstd")
    ftmp = [sb([128, 512], F32, "ftmp%d" % i) for i in range(2)]
    PS = [es.enter_context(nc.psum_tensor("ps%d" % i, [128, 512], F32)) for i in range(8)]
    ARN = 34 * 1024
    AR = sb([128, ARN], BF16, "arena")

    def C(name, l=None, n=None):
        o, c = lay[name]
        if l is None:
            return cst[:, o:o + c]
        return cst[:, o + l * n:o + (l + 1) * n]

    class Arena:
        def __init__(self):
            self.o = 0

        def reset(self):
            self.o = 0

        def get(self, nel, dt):
            nb = nel * (2 if dt == F32 else 1)
            nb = (nb + 15) // 16 * 16
            a = self.o
            self.o += nb
            assert self.o <= ARN, ("arena overflow", self.o)
            ap = AR[:, a:a + nb]
            if dt == F32:
                ap = ap.bitcast(F32)
            ap = ap[:, 0:nel]
            keys = [("ar", g) for g in range(a // 1024, (a + nb - 1) // 1024 + 1)]
            return ap, keys

    ar = Arena()
    _ps = [0]

    def psum():
        i = _ps[0] % 8
        _ps[0] += 1
        return PS[i], ("ps", i)

    _wi = [0]

    def wslot():
        i = _wi[0] % 2
        _wi[0] += 1
        return W[i], ("w", i), "wd%d" % i

    _ft = [0]

    def ft():
        i = _ft[0] % 2
        _ft[0] += 1
        return ftmp[i], ("ft", i)

    def act(out, in_, func, r, w, bias=None, scale=None):
        kw = {}
        if bias is not None:
            kw["bias"] = bias
        if scale is not None:
            kw["scale"] = scale
        return A("act", lambda h: h.activation(out=out, in_=in_, func=func, **kw), r=r, w=w)

    def tt(out, in0, in1, op, r, w, eng="dve"):
        return A(eng, lambda h: h.tensor_tensor(out=out, in0=in0, in1=in1, op=op), r=r, w=w)

    def ts(out, in0, s1, s2, op0, op1, r, w):
        if op1 is None:
            return A("dve", lambda h: h.tensor_scalar(out=out, in0=in0, scalar1=s1, scalar2=None, op0=op0), r=r, w=w)
        return A("dve", lambda h: h.tensor_scalar(out=out, in0=in0, scalar1=s1, scalar2=s2, op0=op0, op1=op1), r=r, w=w)

    def stt(out, in0, sc, in1, op0, op1, r, w):
        return A("dve", lambda h: h.scalar_tensor_tensor(out=out, in0=in0, scalar=sc, in1=in1, op0=op0, op1=op1), r=r, w=w)

    def dma(eng, out, in_, r, w, chan):
        return A(eng, lambda h: h.dma_start(out=out, in_=in_), r=r, w=w, dma=chan)

    def mms(ps_ap, pairs, r, w):
        def f(h):
            ins = None
            n = len(pairs)
            for i, (l_, r_) in enumerate(pairs):
                ins = h.matmul(ps_ap, lhsT=l_, rhs=r_, start=(i == 0), stop=(i == n - 1))
            return ins
        return A("pe", f, r=r, w=w)

    def load_w(src, kc, ncols, eng="sp"):
        src_ap, rkeys = src
        wt, wk, ch = wslot()
        v = wt[:, 0:kc * ncols].rearrange("p (k n) -> p k n", k=kc)
        dma(eng, v, src_ap, r=list(rkeys), w=[wk], chan=ch)
        return v, wk

    def wsrc(name, l, row0, kc, col0, ncols):
        return (scr[name][l, row0:row0 + kc * 128, col0:col0 + ncols].rearrange("(k p) n -> p k n", p=128),
                scr_keys[(name, l)])

    def linear_fm(name, l, row0, kc, col0, ncols, xin, xkeys, T, cb, gc=None):
        gc = gc or (8192 // kc)
        for g0 in range(0, ncols, gc):
            g = min(gc, ncols - g0)
            v, wk = load_w(wsrc(name, l, row0, kc, col0 + g0, g), kc, g)
            for j in range(0, g, 128):
                m = min(128, g - j)
                ps, pk = psum()
                mms(ps[0:m, 0:T], [(v[:, k, j:j + m], xin[:, k, 0:T]) for k in range(kc)],
                    r=[wk] + list(xkeys), w=[pk])
                cb((g0 + j) // 128, ps[0:m, 0:T], pk, m)

    def linear_tm(name, l, col0, ncols, xin, xkeys, T, cb):
        for g0 in range(0, ncols, 512):
            g = min(512, ncols - g0)
            v, wk = load_w(wsrc(name, l, 0, KC, col0 + g0, g), KC, g)
            for tc in range(T // 128):
                ps, pk = psum()
                mms(ps[:, 0:g], [(xin[:, k, tc * 128:(tc + 1) * 128], v[:, k, 0:g]) for k in range(KC)],
                    r=[wk] + list(xkeys), w=[pk])
                cb(tc, g0, g, ps[:, 0:g], pk)

    S.tag = "prologue"
    if cfg.get("pe_log") is not None:
        S.pe_log = cfg["pe_log"]
    dma("sp", cst[:], cst_d, r=[], w=["cst"], chan="cst")
    ctmp = sb([128, 16, NSEQ], F32, "ctmp")
    dma("sp", ctmp[:], cT, r=[], w=["ctmp"], chan="ct")
    act(csb[:], ctmp[:], AF.Silu, r=["ctmp"], w=["csb"])
    A("dve", lambda h: h.tensor_copy(out=idb[:], in_=C("ident")), r=["cst"], w=["idb"])
    A("dve", lambda h: h.memset(on11[:], 1.0 / 2048), w=["on11"])
    A("dve", lambda h: h.memset(on7[:], 1.0 / 128), w=["on7"])
    A("dve", lambda h: h.memset(on9[:], 1.0 / 512), w=["on9"])
    A("dve", lambda h: h.memset(onf[:], 1.0), w=["onf"])
    A("dve", lambda h: h.memset(onrow[:], 1.0), w=["onrow"])
    A("dve", lambda h: h.memset(convst[:], 0.0), w=["convst"])
    lbe = sb([128, depth, 8], F32, "lbe")
    lbs_ = sb([128, 8], F32, "lbs")
    act(lbe[:], C("lbraw").rearrange("p (l h) -> p l h", l=depth), AF.Exp, r=["cst"], w=["lbe"])
    A("dve", lambda h: h.tensor_copy(out=lbs_[:], in_=lbe[:, 0, :]), r=["lbe"], w=["lbs"])
    for l in range(1, depth):
        tt(lbs_[:], lbs_[:], lbe[:, l, :], ALU.add, r=["lbe", "lbs"], w=["lbs"])
    A("dve", lambda h: h.reciprocal(out=lbs_[:], in_=lbs_[:]), r=["lbs"], w=["lbs"])
    A("dve", lambda h: h.memset(lb[:], 0.0), w=["lb"])
    for l in range(1, depth):
        tt(lbe[:, l, :], lbe[:, l, :], lbs_[:], ALU.mult, r=["lbe", "lbs"], w=["lbe"])
        tt(lb[:, l, :], lb[:, l - 1, :], lbe[:, l, :], ALU.add, r=["lbe", "lb"], w=["lb"])
    ts(oml[:], lb[:], -1.0, 1.0, ALU.mult, ALU.add, r=["lb"], w=["oml"])
    act(aneg[:], C("alog").rearrange("p (l h) -> p l h", l=depth), AF.Exp, r=["cst"], w=["aneg"])
    ts(aneg[:], aneg[:], -1.0, None, ALU.mult, None, r=["aneg"], w=["aneg"])

    _cv = [0]

    def convert_layer(l):
        for n_, (wd_, k_, c_) in WSPEC.items():
            keys = []
            for r0 in range(0, k_, 512):
                key = ("scr", n_, l, r0)
                keys.append(key)
                ch = "cv%d" % (_cv[0] % 8)
                _cv[0] += 1
                dma("pool", scr[n_][l, r0:r0 + 512, :], wd_[l, r0:r0 + 512, :], r=[], w=[key], chan=ch)
            scr_keys[(n_, l)] = keys
    convert_layer(0)
    for l in range(depth):
        for g0 in range(0, 6 * D, 512):
            v, wk = load_w((w_mod[l, :, g0:g0 + 512].rearrange("(k p) n -> p k n", p=128), []), KC, 512, eng="pool")
            ps, pk = psum()
            for j in range(4):
                mms(ps[:, j * NSEQ:(j + 1) * NSEQ], [(v[:, k, j * 128:(j + 1) * 128], csb[:, k, :]) for k in range(KC)],
                    r=[wk, "csb"], w=[pk])
            c0 = g0 // 128
            tt(modT[:, l, c0:c0 + 4, :], ps[:, 0:4 * NSEQ].rearrange("p (j b) -> p j b", j=4),
               C("bmod", l, 96)[:, c0:c0 + 4].unsqueeze(2).broadcast_to([128, 4, NSEQ]), ALU.add,
               r=[pk, "cst"], w=["modT"])
        stt(A1[:, l, :, :], modT[:, l, 16:32, :], 1.0, C("n1g", l, 16).unsqueeze(2).broadcast_to([128, 16, NSEQ]),
            ALU.add, ALU.mult, r=["modT", "cst"], w=["A1"])
        stt(A2[:, l, :, :], modT[:, l, 64:80, :], 1.0, C("n2g", l, 16).unsqueeze(2).broadcast_to([128, 16, NSEQ]),
            ALU.add, ALU.mult, r=["modT", "cst"], w=["A2"])

    for l in range(1, depth):
        convert_layer(l)

    tiles = []
    for p in range(NPS):
        for t0 in range(0, LP, TP):
            tiles.append(dict(kind="p", T=TP, Cv=64, t0=t0, first=(t0 == 0), last=(t0 + TP >= LP),
                              segs=[dict(b=p, q=p, c0=0, L=TP)]))
    tiles.append(dict(kind="s", T=256, Cv=32, t0=0, first=True, last=True,
                      segs=[dict(b=NPS + j, q=j, c0=64 * j, L=32) for j in range(NSS)]))

    def rms_to(sq_ap, sq_keys, src_chunks, src_keys, ones, T, nch):
        ps, pk = psum()
        for c in range(nch):
            act(sq_ap[:, c, 0:T], src_chunks[c], AF.Square, r=src_keys, w=sq_keys)
        mms(ps[:, 0:T], [(ones[:], sq_ap[:, c, 0:T]) for c in range(nch)], r=sq_keys + ["ones"], w=[pk])
        act(rstd[:, 0:T], ps[:, 0:T], AF.Sqrt, r=[pk], w=["rstd"], bias=EPS)
        A("dve", lambda h: h.reciprocal(out=rstd[:, 0:T], in_=rstd[:, 0:T]), r=["rstd"], w=["rstd"])

    def norm_mod(tl, l, Atab, shoff, out_b, out_k):
        T = tl["T"]
        ar.reset()
        sq, sqk = ar.get(16 * 512, BF16)
        sq = sq.rearrange("p (c t) -> p c t", c=16)
        rms_to(sq, sqk, [xT[:, c, 0:T] for c in range(16)], ["xT"], on11, T, 16)
        for sg in tl["segs"]:
            c0, L, b = sg["c0"], sg["L"], sg["b"]
            for c in range(16):
                t_, tk = ft()
                tt(t_[:, 0:L], xT[:, c, c0:c0 + L], rstd[:, c0:c0 + L], ALU.mult, r=["xT", "rstd"], w=[tk])
                act(out_b[:, c, c0:c0 + L], t_[:, 0:L], AF.Identity, r=[tk, "A1", "A2", "modT"], w=[out_k],
                    bias=modT[:, l, shoff + c, b:b + 1], scale=Atab[:, l, c, b:b + 1])

    def merge_branch(l, k, ykT, ykeys, T, first, sgt, sgk, mt, mtk):
        for g0 in range(0, D, 512):
            vg_, wkg = load_w(wsrc("w_in", l, 0, KC, OFF_GATE + k * D + g0, 512), KC, 512)
            vb, wkb = load_w(wsrc("w_br", l, k * 1024, 8, g0, 512), 8, 512)
            for jj in range(4):
                j = g0 // 128 + jj
                ps, pk = psum()
                mms(ps[:, 0:T], [(vg_[:, kk_, jj * 128:(jj + 1) * 128], hb[:, kk_, 0:T]) for kk_ in range(KC)], r=[wkg, "hb"], w=[pk])
                act(sgt[:, 0:T], ps[:, 0:T], AF.Sigmoid, r=[pk], w=sgk)
                ps2, pk2 = psum()
                mms(ps2[:, 0:T], [(vb[:, kk_, jj * 128:(jj + 1) * 128], ykT[:, kk_, 0:T]) for kk_ in range(8)], r=[wkb] + list(ykeys), w=[pk2])
                if first:
                    tt(mgp[:, j, 0:T], sgt[:, 0:T], ps2[:, 0:T], ALU.mult, r=sgk + [pk2], w=["mg"])
                else:
                    tt(mt[:, 0:T], sgt[:, 0:T], ps2[:, 0:T], ALU.mult, r=sgk + [pk2], w=mtk)
                    tt(mgp[:, j, 0:T], mgp[:, j, 0:T], mt[:, 0:T], ALU.add, r=mtk + ["mg"], w=["mg"])

    def layer(tl, l):
        T, Cv, kind = tl["T"], tl["Cv"], tl["kind"]
        NT = T // 128
        segs = tl["segs"]
        nseg = len(segs)
        S.tag = "norm1"
        norm_mod(tl, l, A1, 0, hb, "hb")
        ar.reset()

        S.tag = "C.proj"
        uT, uk = ar.get(8 * 512, BF16)
        uT = uT.rearrange("p (c t) -> p c t", c=8)
        vtok, vk = ar.get(NT * 1024, BF16)
        vtok = vtok.rearrange("p (n c) -> p n c", n=NT)
        vg, vgk = ar.get(1024, F32)
        lnp, lnk = ar.get(2048, F32)
        st6, st6k = ar.get(16, F32)
        wst, wstk = ar.get(512, F32)
        bst, bstk = ar.get(512, F32)
        sgt, sgk = ar.get(512, BF16)
        mt, mtk = ar.get(512, F32)
        wsv = wst.rearrange("p (g t) -> p g t", g=4)
        if kind == "p":
            dma(QM, wsv, wsT_d[l], r=[], w=wstk, chan="misc")
            dma(QM, bst[0:1, :].rearrange("p (g t) -> p g t", g=4), bs_d[l].unsqueeze(0), r=[], w=bstk, chan="misc")
            tt(WT[:], wsv, C("triu").unsqueeze(1).broadcast_to([128, 4, 128]), ALU.mult, r=wstk + ["cst"], w=["WT"])
        else:
            A("dve", lambda h: h.memset(wst[:], 0.0), w=wstk)
            A("dve", lambda h: h.memset(bst[0:1, :], 0.0), w=bstk)
            for pb_ in (0, 64):
                dma(QM, wsv[pb_:pb_ + 32, :, pb_:pb_ + 32], wsT_d[l, 0:32, :, 0:32], r=[], w=wstk, chan="misc")
                dma(QM, bst[0:1, :].rearrange("p (g t) -> p g t", g=4)[:, :, pb_:pb_ + 32], bs_d[l, :, 0:32].unsqueeze(0), r=[], w=bstk, chan="misc")
            tt(WT[:], wsv, C("triu_s").unsqueeze(1).broadcast_to([128, 4, 128]), ALU.mult, r=wstk + ["cst"], w=["WT"])
        A("dve", lambda h: h.tensor_copy(out=brow[:], in_=bst[0:1, :].rearrange("p (g t) -> p g t", g=4)), r=bstk, w=["brow"])
        dma(QM, lnp.rearrange("p (a c) -> p a c", a=2), lnp_d[l:l + 1].broadcast_to([128, 2, 1024]) if False else
            lnp_d[l].unsqueeze(0).broadcast_to([128, 2, 1024]), r=[], w=lnk, chan="misc")

        def cb_u(j, ps, pk, m):
            act(uT[:, j, 0:T], ps, AF.Gelu, r=[pk], w=uk)
        linear_fm("w_in", l, 0, KC, OFF_U, 1024, hb, ["hb"], T, cb_u)

        def cb_v(tc, g0, g, ps, pk):
            act(vg[:, g0:g0 + g], ps, AF.Gelu, r=[pk], w=vgk)
            A("dve", lambda h: h.bn_stats(out=st6[:, (g0 // 512) * 6:(g0 // 512) * 6 + 6], in_=vg[:, g0:g0 + g]), r=vgk, w=st6k)
            if g0 + g == 1024:
                A("dve", lambda h: h.bn_aggr(out=st6[:, 12:14], in_=st6[:, 0:12]), r=st6k, w=st6k)
                act(st6[:, 14:15], st6[:, 13:14], AF.Sqrt, r=st6k, w=st6k, bias=EPS)
                A("dve", lambda h: h.reciprocal(out=st6[:, 14:15], in_=st6[:, 14:15]), r=st6k, w=st6k)
                ts(vg[:, :], vg[:, :], st6[:, 12:13], st6[:, 14:15], ALU.subtract, ALU.mult, r=vgk + st6k, w=vgk)
                tt(vg[:, :], vg[:, :], lnp[:, 0:1024], ALU.mult, r=vgk + lnk, w=vgk)
                tt(vg[:, :], vg[:, :], lnp[:, 1024:2048], ALU.add, r=vgk + lnk, w=vgk)
                A("dve", lambda h: h.tensor_copy(out=vtok[:, tc, :], in_=vg[:, :]), r=vgk, w=vk)
                if kind == "s":
                    dma(QM, vso[l, tc * 128:(tc + 1) * 128, :], vg[:, :], r=vgk, w=[], chan="vso")
        vw = []
        for g0 in (0, 512):
            vw.append(load_w(wsrc("w_in", l, 0, KC, OFF_V + g0, 512), KC, 512))
        for tc in range(NT):
            for gi, g0 in enumerate((0, 512)):
                v_, wk_ = vw[gi]
                ps, pk = psum()
                mms(ps[:, 0:512], [(hb[:, k, tc * 128:(tc + 1) * 128], v_[:, k, 0:512]) for k in range(KC)],
                    r=[wk_, "hb"], w=[pk])
                cb_v(tc, g0, 512, ps[:, 0:512], pk)
        S.tag = "C.mix"
        for ch in range(8):
            g = ch // 2
            ps, pk = psum()
            for tc in range(NT):
                mms(ps[:, tc * 128:(tc + 1) * 128],
                    [(vtok[:, tc, ch * 128:(ch + 1) * 128], WT[:, g, :]),
                     (onrow[0:1, :], brow[0:1, g, :])],
                    r=vk + ["WT", "brow", "onrow"], w=[pk])
            tt(uT[:, ch, 0:T], uT[:, ch, 0:T], ps[:, 0:T], ALU.mult, r=uk + [pk], w=uk)
        S.tag = "C.merge"
        merge_branch(l, 2, uT, uk, T, True, sgt, sgk, mt, mtk)

        S.tag = "A.i"
        ar.reset()
        qt, qk = ar.get(8 * 512, BF16)
        kt, kk = ar.get(8 * 512, BF16)
        khat, khk = ar.get(NT * 1024, BF16)
        vat, vak = ar.get(NT * 1024, BF16)
        gs, gsk = qt, qk
        oT, ok_ = ar.get(8 * 512, BF16)
        qt = qt.rearrange("p (c t) -> p c t", c=8)
        kt = kt.rearrange("p (c t) -> p c t", c=8)
        gs = gs.rearrange("p (c t) -> p c t", c=8)
        oT = oT.rearrange("p (c t) -> p c t", c=8)
        khat = khat.rearrange("p (n c) -> p n c", n=NT)
        vat = vat.rearrange("p (n c) -> p n c", n=NT)
        NB = T // 64
        dec, deck = ar.get(8 * 8, F32)
        em, emk = ar.get(8 * 8, F32)
        dec = dec.rearrange("p (h n) -> p h n", h=8)
        em = em.rearrange("p (h n) -> p h n", h=8)
        f1, f1k = ar.get(512, F32)
        f2, f2k = ar.get(512, F32)
        f3, f3k = ar.get(512, F32)
        f4, f4k = ar.get(512, F32)
        f5, f5k = ar.get(512, F32)
        khTs = [ar.get(512, BF16) for _ in range(2)]
        attb, attk = ar.get(8 * 64, BF16)
        attb = attb.rearrange("p (h t) -> p h t", h=8)
        Sst, Sk = ar.get(1024, F32)
        Sbf, Sbk = ar.get(1024, BF16)
        Sst = Sst.rearrange("p (h v) -> p h v", h=8)
        Sbf = Sbf.rearrange("p (h v) -> p h v", h=8)
        osq, osqk = ar.get(512, BF16)
        sgt, sgk = ar.get(512, BF16)
        mt, mtk = ar.get(512, F32)

        def cb_i(tc, g0, g, ps, pk):
            act(vat[:, tc, g0:g0 + g], ps, AF.Copy, r=[pk], w=vak)
        linear_tm("w_in", l, OFF_I, 1024, hb, ["hb"], T, cb_i)

        S.tag = "A.qf"
        mid, last = Cv // 2 - 1, Cv - 1

        def qf_tail(hd, khT, khTk):
            pst, ptk = psum()
            pstb = pst[:].bitcast(BF16)

            def trf(h):
                ins = None
                for tc in range(NT):
                    ins = h.transpose(out=pstb[:, tc * 128:(tc + 1) * 128], in_=khT[:, tc * 128:(tc + 1) * 128], identity=idb[:])
                return ins
            A("pe", trf, r=khTk + ["idb"], w=[ptk])
            act(khat[:, :, hd * 128:(hd + 1) * 128], pstb[:, 0:NT * 128].rearrange("p (n c) -> p n c", n=NT), AF.Copy,
                r=[ptk], w=khk)
        pend = None
        for hd in range(8):
            wt, wk, ch_ = wslot()
            v = wt[:, 0:KC * 256].rearrange("p (two k n) -> p two k n", k=KC, two=2)
            sq_, sqk_ = wsrc("w_in", l, 0, KC, OFF_Q + hd * 128, 128)
            sf_, _ = wsrc("w_in", l, 0, KC, OFF_F + hd * 128, 128)
            dma("sp", v[:, 0], sq_, r=list(sqk_), w=[wk], chan=ch_)
            dma("sp", v[:, 1], sf_, r=list(sqk_), w=[wk], chan=ch_)
            psq, pqk = psum()
            mms(psq[:, 0:T], [(v[:, 0, k, :], hb[:, k, 0:T]) for k in range(KC)], r=[wk, "hb"], w=[pqk])
            psf, pfk = psum()
            mms(psf[:, 0:T], [(v[:, 1, k, :], hb[:, k, 0:T]) for k in range(KC)], r=[wk, "hb"], w=[pfk])
            act(f1[:, 0:T], psq[:, 0:T], AF.Silu, r=[pqk], w=f1k)
            act(f2[:, 0:T], psf[:, 0:T], AF.Sigmoid, r=[pfk], w=f2k)
            ts(f2[:, 0:T], f2[:, 0:T], oml[:, l, hd:hd + 1], lb[:, l, hd:hd + 1], ALU.mult, ALU.add,
               r=f2k + ["oml", "lb"], w=f2k)
            act(f3[:, 0:T], f2[:, 0:T], AF.Ln, r=f2k, w=f3k)
            ts(f2[:, 0:T], f2[:, 0:T], -1.0, 1.0, ALU.mult, ALU.add, r=f2k, w=f2k)
            A("dve", lambda h: h.tensor_tensor_scan(out=f4[:, 0:T], data0=C("blkm")[:, 0:T], data1=f3[:, 0:T],
                                                     initial=0.0, op0=ALU.mult, op1=ALU.add),
              r=f3k + ["cst"], w=f4k)
            b3 = f4[:, 0:T].rearrange("p (n c) -> p n c", c=64)
            d3 = f3[:, 0:T].rearrange("p (n c) -> p n c", c=64)
            tt(d3, b3, b3[:, :, mid:mid + 1].broadcast_to([128, NB, 64]), ALU.subtract, r=f4k, w=f3k)
            act(dec[:, hd, 0:NB], b3[:, :, last], AF.Exp, r=f4k, w=deck)
            act(em[:, hd, 0:NB], b3[:, :, mid], AF.Exp, r=f4k, w=emk)
            act(f4[:, 0:T], f3[:, 0:T], AF.Exp, r=f3k, w=f4k)
            act(f5[:, 0:T], f3[:, 0:T], AF.Exp, r=f3k, w=f5k, scale=-1.0)
            tt(qt[:, hd, 0:T], f1[:, 0:T], f4[:, 0:T], ALU.mult, r=f1k + f4k, w=qk)
            tt(kt[:, hd, 0:T], f2[:, 0:T], f5[:, 0:T], ALU.mult, r=f2k + f5k, w=kk)
            e13 = f4[:, 0:T].rearrange("p (n c) -> p n c", c=64)
            khT, khTk = khTs[hd % 2]
            tt(khT[:, 0:T].rearrange("p (n c) -> p n c", c=64), kt[:, hd, 0:T].rearrange("p (n c) -> p n c", c=64),
               e13[:, :, last:last + 1].broadcast_to([128, NB, 64]), ALU.mult, r=kk + f4k, w=khTk)
            if pend is not None:
                qf_tail(*pend)
            pend = (hd, khT, khTk)
        qf_tail(*pend)

        S.tag = "A.rec"
        hst_in = None
        for sg in segs:
            c0, L, q = sg["c0"], sg["L"], sg["q"]
            Sv = Sst.rearrange("p h v -> p h v")
            if kind == "p":
                if tl["first"]:
                    A("dve", lambda h: h.memset(Sst[:], 0.0), w=Sk)
                else:
                    dma(QM, Sst, hp[l, q].rearrange("h k v -> k h v"), r=[], w=Sk, chan="hst")
            else:
                dma(QM, Sst, sh_in[l, q].rearrange("h k v -> k h v"), r=[], w=Sk, chan="hst")
            for bi in range(L // Cv):
                col = c0 + bi * Cv
                tc, pb, nb = col // 128, col % 128, col // 64
                psa, pak = psum()

                def attf(h, psa=psa, col=col, pb=pb):
                    ins = None
                    for hd in range(8):
                        ins = h.matmul(psa[pb:pb + Cv, hd * Cv:(hd + 1) * Cv], lhsT=kt[:, hd, col:col + Cv],
                                       rhs=qt[:, hd, col:col + Cv], start=True, stop=True)
                    return ins
                A("pe", attf, r=qk + kk, w=[pak])
                A("dve", lambda h, pb=pb: h.memset(attb[pb:pb + Cv, :, 0:Cv], 0.0), w=attk)
                A("dve", lambda h, pb=pb, psa=psa: h.copy_predicated(
                    out=attb[pb:pb + Cv, :, 0:Cv],
                    mask=C("maskA").bitcast(U32).rearrange("p (h t) -> p h t", h=8)[pb:pb + Cv, :, 0:Cv],
                    data=psa[pb:pb + Cv, 0:8 * Cv].rearrange("p (h t) -> p h t", h=8)),
                  r=[pak, "cst"] + attk, w=attk)
                tt(Sbf[:], Sst[:], em[:, :, nb:nb + 1].broadcast_to([128, 8, 128]), ALU.mult, r=Sk + emk, w=Sbk)
                pso, pok = psum()

                def of(h, pso=pso, col=col, pb=pb, tc=tc):
                    ins = None
                    for hd in range(8):
                        h.matmul(pso[:, hd * Cv:(hd + 1) * Cv], lhsT=Sbf[:, hd, :], rhs=qt[:, hd, col:col + Cv],
                                 start=True, stop=False)
                        ins = h.matmul(pso[:, hd * Cv:(hd + 1) * Cv], lhsT=vat[pb:pb + Cv, tc, hd * 128:(hd + 1) * 128],
                                       rhs=attb[pb:pb + Cv, hd, 0:Cv], start=False, stop=True)
                    return ins
                A("pe", of, r=Sbk + qk + vak + attk, w=[pok])
                act(oT[:, :, col:col + Cv], pso[:, 0:8 * Cv].rearrange("p (h t) -> p h t", h=8), AF.Copy, r=[pok], w=ok_)
                for half in range(2):
                    pss, psk = psum()

                    def sf(h, pss=pss, half=half, pb=pb, tc=tc):
                        ins = None
                        for hh in range(4):
                            hd = half * 4 + hh
                            ins = h.matmul(pss[:, hh * 128:(hh + 1) * 128], lhsT=khat[pb:pb + Cv, tc, hd * 128:(hd + 1) * 128],
                                           rhs=vat[pb:pb + Cv, tc, hd * 128:(hd + 1) * 128], start=True, stop=True)
                        return ins
                    A("pe", sf, r=khk + vak, w=[psk])
                    for hh in range(4):
                        hd = half * 4 + hh
                        stt(Sst[:, hd, :], Sst[:, hd, :], dec[:, hd, nb:nb + 1], pss[:, hh * 128:(hh + 1) * 128],
                            ALU.mult, ALU.add, r=Sk + deck + [psk], w=Sk)
            dst = (hp if kind == "p" else hs)[l, q].rearrange("h k v -> k h v")
            dma(QM, dst, Sst, r=Sk, w=[], chan="hst")
        S.tag = "A.gnorm"
        def cb_g(j, ps, pk, m):
            act(gs[:, j, 0:T], ps, AF.Silu, r=[pk], w=gsk)
        linear_fm("w_in", l, 0, KC, OFF_G, 1024, hb, ["hb"], T, cb_g)
        for hd in range(8):
            act(osq[:, 0:T], oT[:, hd, 0:T], AF.Square, r=ok_, w=osqk)
            ps, pk = psum()
            mms(ps[:, 0:T], [(on7[:], osq[:, 0:T])], r=osqk + ["ones"], w=[pk])
            act(f1[:, 0:T], ps[:, 0:T], AF.Sqrt, r=[pk], w=f1k, bias=EPS)
            A("dve", lambda h: h.reciprocal(out=f1[:, 0:T], in_=f1[:, 0:T]), r=f1k, w=f1k)
            tt(f2[:, 0:T], oT[:, hd, 0:T], f1[:, 0:T], ALU.mult, r=ok_ + f1k, w=f2k)
            stt(oT[:, hd, 0:T], f2[:, 0:T], C("gon", l, 8)[:, hd:hd + 1], gs[:, hd, 0:T], ALU.mult, ALU.mult,
                r=f2k + gsk + ["cst"], w=ok_)
        S.tag = "A.merge"
        merge_branch(l, 0, oT, ok_, T, False, sgt, sgk, mt, mtk)

        S.tag = "B.xbc"
        ar.reset()
        xbc, xbk = ar.get(12 * 512, BF16)
        xst, xsk = ar.get(NT * 1024, BF16)
        btk_, btkk = ar.get(NT * 256, BF16)
        ybT, ybk = ar.get(8 * 512, BF16)
        xbc = xbc.rearrange("p (c t) -> p c t", c=12)
        ybT = ybT.rearrange("p (c t) -> p c t", c=8)
        xst = xst.rearrange("p (n c) -> p n c", n=NT)
        btk_ = btk_.rearrange("p (n c) -> p n c", n=NT)
        xp_, xpk = ar.get(520, F32)
        acc, acck = ar.get(512, F32)
        dtt, dtk = ar.get(NT * 16, F32)
        att_, atk = ar.get(NT * 16, F32)
        dtt = dtt.rearrange("p (n h) -> p n h", n=NT)
        att_ = att_.rearrange("p (n h) -> p n h", n=NT)
        cst_in, csik = ar.get(4 * 36, F32)
        cst_out, csok = ar.get(4 * 36, F32)
        cst_in = cst_in.rearrange("p (q c j) -> p q c j", q=4, c=12)
        cst_out = cst_out.rearrange("p (q c j) -> p q c j", q=4, c=12)
        acs, acsk = ar.get(64, F32)
        edb, edbk = ar.get(2 * 16, F32)
        TriA, trak = ar.get(4 * 128, F32)
        Lm, lmk = ar.get(4 * 128, F32)
        ea4, ea4k = ar.get(4 * 128, BF16)
        Mb, mbk = ar.get(16 * 128, BF16)
        Ct, ctk = ar.get(16 * 128, BF16)
        CBT, cbtk = ar.get(256, F32)
        xdt, xdk = ar.get(1024, BF16)
        xw, xwk = ar.get(1024, BF16)
        y1, y1k = ar.get(128, F32)
        ST, STk = ar.get(1024, F32)
        Sb2, Sb2k = ar.get(2 * 1024, BF16)
        sgt, sgk = ar.get(512, BF16)
        mt, mtk = ar.get(512, F32)
        zst, zstk = ar.get(512, BF16)
        TriA = TriA.rearrange("p (h t) -> p h t", h=4)
        Lm = Lm.rearrange("p (h t) -> p h t", h=4)
        ea4 = ea4.rearrange("p (h t) -> p h t", h=4)
        ysq, ysqk = Mb.rearrange("p (c t) -> p c t", c=4), mbk
        Mb = Mb.rearrange("p (h t) -> p h t", h=16)
        Ct = Ct.rearrange("p (h t) -> p h t", h=16)
        CBT = CBT.rearrange("p (g t) -> p g t", g=2)
        xdt = xdt.rearrange("p (h q) -> p h q", h=16)
        xw = xw.rearrange("p (h q) -> p h q", h=16)
        ST = ST.rearrange("p (h q) -> p h q", h=16)
        Sb2 = Sb2.rearrange("p (s h q) -> p s h q", s=2, h=16)

        if kind == "s":
            dma(QM, cst_in, sc_in[l].rearrange("q p c j -> p q c j"), r=[], w=csik, chan="cvs")
            xp3 = xp_[:, 0:4 * 35].rearrange("p (q t) -> p q t", q=4)
        else:
            xp3 = xp_[:, 0:3 + T].rearrange("p (q t) -> p q t", q=1)
        Lc = segs[0]["L"]

        def cb_x(j, ps, pk, m):
            if kind == "s":
                A("dve", lambda h: h.tensor_copy(out=xp3[:, :, 0:3], in_=cst_in[:, :, j, :]), r=csik, w=xpk)
                act(xp3[:, :, 3:3 + Lc], ps.rearrange("p (q t) -> p q t", q=4)[:, :, 0:Lc], AF.Copy, r=[pk], w=xpk)
            else:
                if tl["first"]:
                    A("dve", lambda h: h.memset(xp3[:, 0, 0:3], 0.0), w=xpk)
                else:
                    A("dve", lambda h: h.tensor_copy(out=xp3[:, 0, 0:3], in_=convst[:, l, j, :]), r=["convst"], w=xpk)
                act(xp3[:, 0, 3:3 + Lc], ps, AF.Copy, r=[pk], w=xpk)
            nq = xp3.shape[1]
            a3 = acc[:, 0:nq * Lc].rearrange("p (q t) -> p q t", q=nq)
            cw = C("convw", l, 48)
            ts(a3, xp3[:, :, 0:Lc], cw[:, j:j + 1], C("convb", l, 12)[:, j:j + 1], ALU.mult, ALU.add,
               r=xpk + ["cst"], w=acck)
            for jj in range(1, 4):
                stt(a3, xp3[:, :, jj:jj + Lc], cw[:, jj * 12 + j:jj * 12 + j + 1], a3, ALU.mult, ALU.add,
                    r=xpk + acck + ["cst"], w=acck)
            if kind == "s":
                act(xbc[:, j, 0:T].rearrange("p (q t) -> p q t", q=4)[:, :, 0:Lc], a3, AF.Silu, r=acck, w=xbk)
                A("dve", lambda h: h.tensor_copy(out=cst_out[:, :, j, :], in_=xp3[:, :, Lc:Lc + 3]), r=xpk, w=csok)
            else:
                act(xbc[:, j, 0:T], a3[:, 0, :], AF.Silu, r=acck, w=xbk)
                A("dve", lambda h: h.tensor_copy(out=convst[:, l, j, :], in_=xp3[:, 0, Lc:Lc + 3]), r=xpk, w=["convst"])
        if kind == "s":
            A("dve", lambda h: h.memset(xbc[:, :, 0:T], 0.0), w=xbk)
        linear_fm("w_in", l, 0, KC, OFF_XBC, 1536, hb, ["hb"], T, cb_x)
        if kind == "s":
            dma(QM, cso[l].rearrange("q p c j -> p q c j"), cst_out, r=csok, w=[], chan="cvs")
        elif tl["last"]:
            dma(QM, cp[l, segs[0]["q"]], convst[:, l, :, :], r=["convst"], w=[], chan="cvs")

        S.tag = "B.dt_tr"
        vdt, wkdt = load_w(wsrc("w_in", l, 0, KC, OFF_DT, 16), KC, 16)
        for tc in range(NT):
            ps, pk = psum()
            mms(ps[:, 0:16], [(hb[:, k, tc * 128:(tc + 1) * 128], vdt[:, k, 0:16]) for k in range(KC)], r=[wkdt, "hb"], w=[pk])
            tt(dtt[:, tc, :], ps[:, 0:16], C("dtb", l, 16), ALU.add, r=[pk, "cst"], w=dtk)
            act(dtt[:, tc, :], dtt[:, tc, :], AF.Exp, r=dtk, w=dtk)
            act(dtt[:, tc, :], dtt[:, tc, :], AF.Ln, r=dtk, w=dtk, bias=1.0)
            tt(att_[:, tc, :], dtt[:, tc, :], aneg[:, l, :], ALU.mult, r=dtk + ["aneg"], w=atk)
        for tc in range(NT):
            pst, ptk = psum()
            pstb = pst[:].bitcast(BF16)

            def trx(h, pstb=pstb, tc=tc):
                ins = None
                for c in range(8):
                    ins = h.transpose(out=pstb[:, c * 128:(c + 1) * 128], in_=xbc[:, c, tc * 128:(tc + 1) * 128], identity=idb[:])
                return ins
            A("pe", trx, r=xbk + ["idb"], w=[ptk])
            act(xst[:, tc, :], pstb[:, 0:1024], AF.Copy, r=[ptk], w=xsk)
            pst2, ptk2 = psum()
            pstb2 = pst2[:].bitcast(BF16)

            def trb(h, pstb2=pstb2, tc=tc):
                ins = None
                for c in range(2):
                    ins = h.transpose(out=pstb2[:, c * 128:(c + 1) * 128], in_=xbc[:, 8 + c, tc * 128:(tc + 1) * 128], identity=idb[:])
                return ins
            A("pe", trb, r=xbk + ["idb"], w=[ptk2])
            act(btk_[:, tc, :], pstb2[:, 0:256], AF.Copy, r=[ptk2], w=btkk)

        S.tag = "B.ssd"
        tri = C("tri_p") if kind == "p" else C("tri_s")
        segb = C("segb_p") if kind == "p" else C("segb_s")
        negm = C("neg_p") if kind == "p" else C("neg_s")
        if kind == "p":
            if tl["first"]:
                A("dve", lambda h: h.memset(ST[:], 0.0), w=STk)
            else:
                dma(QM, ST, sp_o[l, segs[0]["q"]], r=[], w=STk, chan="sst")
            A("dve", lambda h: h.tensor_copy(out=Sb2[:, 0], in_=ST[:]), r=STk, w=Sb2k)
        for tc in range(NT):
            csegs = [sg for sg in segs if sg["c0"] // 128 == tc] if kind == "s" else [dict(q=segs[0]["q"], c0=tc * 128, L=128)]
            ps, pk = psum()
            mms(ps[:, 0:16], [(tri, att_[:, tc, :])], r=atk + ["cst"], w=[pk])
            mms(ps[:, 16:32], [(segb, att_[:, tc, :])], r=atk + ["cst"], w=[pk])
            A("dve", lambda h, ps=ps: h.tensor_copy(out=acs[:, 0:16], in_=ps[:, 0:16]), r=[pk], w=acsk)
            ts(acs[:, 16:32], ps[:, 0:16], -1.0, None, ALU.mult, None, r=[pk], w=acsk)
            tt(acs[:, 32:48], ps[:, 16:32], acs[:, 0:16], ALU.subtract, r=[pk] + acsk, w=acsk)
            act(acs[:, 48:64], acs[:, 32:48], AF.Exp, r=acsk, w=acsk)
            psc, pck = psum()
            for g in range(2):
                mms(psc[:, g * 128:(g + 1) * 128], [(xbc[:, 8 + g, tc * 128:(tc + 1) * 128], xbc[:, 10 + g, tc * 128:(tc + 1) * 128])],
                    r=xbk, w=[pck])
            act(CBT[:], psc[:, 0:256].rearrange("p (g t) -> p g t", g=2), AF.Copy, r=[pck], w=cbtk)
            for q4 in range(4):
                g = q4 // 2
                tt(TriA[:], tri.unsqueeze(1).broadcast_to([128, 4, 128]),
                   att_[:, tc, q4 * 4:(q4 + 1) * 4].unsqueeze(2).broadcast_to([128, 4, 128]), ALU.mult, r=atk + ["cst"], w=trak)
                psA, pAk = psum()
                mms(psA[:, 0:512], [(onf[:], TriA[:])], r=trak + ["ones"], w=[pAk])
                act(ea4[:], psA[:, 0:512].rearrange("p (h t) -> p h t", h=4), AF.Exp, r=[pAk], w=ea4k)
                tt(Ct[:, q4 * 4:(q4 + 1) * 4, :], ea4[:], xbc[:, 10 + g:11 + g, tc * 128:(tc + 1) * 128].broadcast_to([128, 4, 128]),
                   ALU.mult, r=ea4k + xbk, w=ctk)
                psB, pBk = psum()
                mms(psB[:, 0:512], [(onf[:], TriA[:]), (C("ident"), negm)], r=trak + ["ones", "cst"], w=[pBk])
                for hh in range(4):
                    hd = q4 * 4 + hh
                    act(Lm[:, hh, :], psB[:, hh * 128:(hh + 1) * 128], AF.Exp, r=[pBk] + acsk, w=lmk, bias=acs[:, 16 + hd:17 + hd])
                tt(Mb[:, q4 * 4:(q4 + 1) * 4, :], Lm[:], CBT[:, g:g + 1, :].broadcast_to([128, 4, 128]), ALU.mult, r=lmk + cbtk, w=mbk)
            tt(xdt[:], xst[:, tc, :].rearrange("p (h q) -> p h q", h=16), dtt[:, tc, :].unsqueeze(2).broadcast_to([128, 16, 64]),
               ALU.mult, r=xsk + dtk, w=xdk)
            tt(xw[:], xdt[:], acs[:, 48:64].unsqueeze(2).broadcast_to([128, 16, 64]), ALU.mult, r=xdk + acsk, w=xwk)
            for half in range(2):
                psy, pyk = psum()

                def yf(h, psy=psy, half=half, tc=tc, csegs=csegs):
                    ins = None
                    for cc in range(4):
                        hc = half * 4 + cc
                        for hh in range(2):
                            hd = hc * 2 + hh
                            o_ = psy[hh * 64:(hh + 1) * 64, cc * 128:(cc + 1) * 128]
                            h.matmul(o_, lhsT=xdt[:, hd, :], rhs=Mb[:, hd, :], start=True, stop=False)
                            for si, sg in enumerate(csegs):
                                lc0 = sg["c0"] % 128
                                ins = h.matmul(psy[hh * 64:(hh + 1) * 64, cc * 128 + lc0:cc * 128 + lc0 + sg["L"]],
                                               lhsT=Sb2[:, si if kind == "s" else 0, hd, :], rhs=Ct[:, hd, lc0:lc0 + sg["L"]],
                                               start=False, stop=(si == len(csegs) - 1))
                    return ins
                if kind == "s" and half == 0:
                    for si, sg in enumerate(csegs):
                        dma(QM, ST, ss_in[l, sg["q"]], r=[], w=STk, chan="sst")
                        A("dve", lambda h, si=si: h.tensor_copy(out=Sb2[:, si], in_=ST[:]), r=STk, w=Sb2k)
                A("pe", yf, r=xdk + mbk + Sb2k + ctk, w=[pyk])
                for cc in range(4):
                    hc = half * 4 + cc
                    stt(ybT[:, hc, tc * 128:(tc + 1) * 128], xbc[:, hc, tc * 128:(tc + 1) * 128], C("Dp", l, 8)[:, hc:hc + 1],
                        psy[:, cc * 128:(cc + 1) * 128], ALU.mult, ALU.add, r=xbk + [pyk, "cst"], w=ybk)
            for si, sg in enumerate(csegs):
                lc0, L = sg["c0"] % 128, sg["L"]
                if kind == "s":
                    dma(QM, ST, ss_in[l, sg["q"]], r=[], w=STk, chan="sst")
                psd, pdk = psum()
                sel = onf[:] if kind == "p" else C("sel_s%d" % si)
                mms(psd[:, 0:16], [(sel, att_[:, tc, :])], r=atk + ["ones", "cst"], w=[pdk])
                act(edb[:, 0:16], psd[:, 0:16], AF.Exp, r=[pdk], w=edbk)
                tt(ST[:], ST[:], edb[:, 0:16].unsqueeze(2).broadcast_to([128, 16, 64]), ALU.mult, r=STk + edbk, w=STk)
                for half in range(2):
                    pss, psk = psum()

                    def suf(h, pss=pss, half=half, lc0=lc0, L=L, tc=tc):
                        ins = None
                        for hh in range(8):
                            hd = half * 8 + hh
                            g = hd // 8
                            ins = h.matmul(pss[:, hh * 64:(hh + 1) * 64], lhsT=btk_[lc0:lc0 + L, tc, g * 128:(g + 1) * 128],
                                           rhs=xw[lc0:lc0 + L, hd, :], start=True, stop=True)
                        return ins
                    A("pe", suf, r=btkk + xwk, w=[psk])
                    tt(ST[:, half * 8:(half + 1) * 8, :], ST[:, half * 8:(half + 1) * 8, :],
                       pss[:, 0:512].rearrange("p (h q) -> p h q", h=8), ALU.add, r=STk + [psk], w=STk)
                if kind == "s":
                    dma(QM, sso[l, sg["q"]], ST, r=STk, w=[], chan="sst")
                else:
                    A("dve", lambda h: h.tensor_copy(out=Sb2[:, 0], in_=ST[:]), r=STk, w=Sb2k)
        if kind == "p":
            dma(QM, sp_o[l, segs[0]["q"]], ST, r=STk, w=[], chan="sst")
        S.tag = "B.znorm"
        def cb_z(j, ps, pk, m):
            act(zst[:, 0:T], ps, AF.Silu, r=[pk], w=zstk)
            tt(ybT[:, j, 0:T], ybT[:, j, 0:T], zst[:, 0:T], ALU.mult, r=ybk + zstk, w=ybk)
        linear_fm("w_in", l, 0, KC, OFF_Z, 1024, hb, ["hb"], T, cb_z)
        for g in range(2):
            rms_to(ysq, ysqk, [ybT[:, g * 4 + c, 0:T] for c in range(4)], ybk, on9, T, 4)
            for c in range(4):
                hc = g * 4 + c
                stt(ybT[:, hc, 0:T], ybT[:, hc, 0:T], C("gbn", l, 8)[:, hc:hc + 1], rstd[:, 0:T], ALU.mult, ALU.mult,
                    r=ybk + ["rstd", "cst"], w=ybk)
        S.tag = "B.merge"
        merge_branch(l, 1, ybT, ybk, T, False, sgt, sgk, mt, mtk)

        S.tag = "wout"
        mg, mgk = mgp, ["mg"]

        def cb_o(j, ps, pk, m):
            for sg in segs:
                c0, L, b = sg["c0"], sg["L"], sg["b"]
                stt(xT[:, j, c0:c0 + L], ps[:, c0:c0 + L], modT[:, l, 32 + j, b:b + 1], xT[:, j, c0:c0 + L], ALU.mult, ALU.add,
                    r=[pk, "modT", "xT"], w=["xT"])
        linear_fm("w_out", l, 0, KC, 0, D, mg, mgk, T, cb_o)

        S.tag = "ffn"
        norm_mod(tl, l, A2, 48, hb, "hb")
        ar.reset()
        hid, hidk = ar.get(32 * 512, BF16)
        hid = hid.rearrange("p (c t) -> p c t", c=32)
        rl = [ar.get(512, BF16) for _ in range(2)]
        for half in range(2):
            def cb_up(j, ps, pk, m, half=half):
                r_, rk_ = rl[j % 2]
                act(r_[:, 0:T], ps, AF.Relu, r=[pk], w=rk_)
                tt(hid[:, j, 0:T], r_[:, 0:T], r_[:, 0:T], ALU.mult, r=rk_, w=hidk)
            linear_fm("w_up", l, 0, KC, half * 4096, 4096, hb, ["hb"], T, cb_up)

            def cb_dn(j, ps, pk, m):
                for sg in segs:
                    c0, L, b = sg["c0"], sg["L"], sg["b"]
                    stt(xT[:, j, c0:c0 + L], ps[:, c0:c0 + L], modT[:, l, 80 + j, b:b + 1], xT[:, j, c0:c0 + L], ALU.mult, ALU.add,
                        r=[pk, "modT", "xT"], w=["xT"])
            linear_fm("w_dn", l, half * 4096, 32, 0, D, hid, hidk, T, cb_dn)

    for tl in tiles:
        T = tl["T"]
        if tl["kind"] == "p":
            q = tl["segs"][0]["q"]
            dma(QM, xT[:, :, 0:T], xp[q, :, tl["t0"]:tl["t0"] + T].rearrange("(c p) t -> p c t", p=128), r=[], w=["xT"], chan="xin")
        else:
            dma(QM, xT[:, :, 0:T], xs.rearrange("(c p) t -> p c t", p=128), r=[], w=["xT"], chan="xin")
        for l in range(depth):
            layer(tl, l)
        S.tag = "final"
        ar.reset()
        sq, sqk = ar.get(16 * 512, BF16)
        sq = sq.rearrange("p (c t) -> p c t", c=16)
        rms_to(sq, sqk, [xT[:, c, 0:T] for c in range(16)], ["xT"], on11, T, 16)
        for c in range(16):
            stt(xT[:, c, 0:T], xT[:, c, 0:T], C("fg")[:, c:c + 1], rstd[:, 0:T], ALU.mult, ALU.mult, r=["xT", "rstd", "cst"], w=["xT"])
        if tl["kind"] == "p":
            q = tl["segs"][0]["q"]
            dma(QM, yp[q, :, tl["t0"]:tl["t0"] + T].rearrange("(c p) t -> p c t", p=128), xT[:, :, 0:T], r=["xT"], w=[], chan="xin")
        else:
            dma(QM, ys.rearrange("(c p) t -> p c t", p=128), xT[:, :, 0:T], r=["xT"], w=[], chan="xin")

    S.emit()
    es.close()
    return nc


def _pack_cst(depth, P, lay, ncst):
    c = np.zeros((128, ncst), np.float32)

    def put(name, arr):
        o, n = lay[name]
        c[:, o:o + n] = np.asarray(arr, np.float32).reshape(128, n)

    def fm(v, nch):
        v = np.asarray(v)
        L = v.shape[0]
        return v.reshape(L, nch, 128).transpose(2, 0, 1).reshape(128, L * nch)

    put("ident", np.eye(128))
    put("n1g", fm(P["norm1_g"], 16))
    put("n2g", fm(P["norm2_g"], 16))
    put("fg", fm(P["final_g"][None], 16))
    put("bmod", fm(P["b_mod"], 96))
    put("lbraw", fm(P["hgrn_lb"], 8))
    put("gon", fm(P["hgrn_onorm_g"], 8))
    cw = np.asarray(P["ssm_conv_w"])
    put("convw", cw.reshape(depth, 4, 12, 128).transpose(3, 0, 1, 2).reshape(128, depth * 48))
    put("convb", fm(P["ssm_conv_b"], 12))
    put("dtb", np.broadcast_to(np.asarray(P["ssm_dt_bias"]).reshape(1, depth * 16), (128, depth * 16)))
    put("alog", np.broadcast_to(np.asarray(P["ssm_a_log"]).reshape(1, depth * 16), (128, depth * 16)))
    dp = np.repeat(np.asarray(P["ssm_d"]), 64, axis=1)
    put("Dp", fm(dp, 8))
    put("gbn", fm(P["ssm_onorm_g"], 8))
    p = np.arange(128)[:, None]
    t64 = np.arange(64)[None, :]
    put("maskA", np.tile(((p % 64) <= t64).astype(np.float32)[:, None, :], (1, 8, 1)))
    bm = np.ones(512, np.float32)
    bm[::64] = 0
    put("blkm", np.broadcast_to(bm, (128, 512)))
    t = np.arange(128)[None, :]
    tri_p = (p <= t).astype(np.float32)
    valid = lambda i: (i % 64) < 32
    same = (p // 64) == (t // 64)
    tri_s = (same & valid(p) & valid(t) & ((p % 64) <= (t % 64))).astype(np.float32)
    put("tri_p", tri_p)
    put("tri_s", tri_s)
    put("segb_p", np.ones((128, 128)))
    put("segb_s", (same & valid(p) & valid(t)).astype(np.float32))
    put("sel_s0", np.broadcast_to(((p // 64 == 0) & valid(p)).astype(np.float32), (128, 128)))
    put("sel_s1", np.broadcast_to(((p // 64 == 1) & valid(p)).astype(np.float32), (128, 128)))
    put("neg_p", np.tile(np.where(tri_p > 0, 0.0, NEG), (1, 4)))
    put("neg_s", np.tile(np.where(tri_s > 0, 0.0, NEG), (1, 4)))
    put("triu", tri_p)
    put("triu_s", tri_s)
    return c


def _host_inputs(cfg, inp, core):
    depth, LP, NPS = cfg["depth"], cfg["LP"], cfg["NPS"]
    lay, ncst = cst_layout(depth)
    f = lambda a: np.ascontiguousarray(np.asarray(a, np.float32))
    pb = slice(core * NPS, (core + 1) * NPS)
    sbs = slice(core * 4, (core + 1) * 4)
    m = {}
    m["xp"] = f(np.asarray(inp["x_prompt"])[pb].transpose(0, 2, 1))
    xs = np.zeros((D, 256), np.float32)
    xsl = np.asarray(inp["x_sample"])[sbs]
    for j in range(4):
        xs[:, 64 * j:64 * j + 32] = xsl[j].T
    m["xs"] = xs
    cc = np.concatenate([np.asarray(inp["c_prompt"])[pb], np.asarray(inp["c_sample"])[sbs]], 0)
    m["cT"] = f(cc.reshape(cc.shape[0], 16, 128).transpose(2, 1, 0))
    m["sh_in"] = f(np.asarray(inp["state_hgrn"])[:, sbs])
    m["ss_in"] = f(np.asarray(inp["state_ssm"])[:, sbs].transpose(0, 1, 4, 2, 3))
    sc = np.asarray(inp["state_conv"])[:, sbs]
    m["sc_in"] = f(sc.reshape(depth, 4, 3, 12, 128).transpose(0, 1, 4, 3, 2))
    for k_, n_ in (("w_mod", "w_mod"), ("w_in", "w_in"), ("w_branch", "w_br"), ("w_out", "w_out"), ("w_up", "w_up"), ("w_down", "w_dn")):
        m[n_] = f(inp[k_])
    m["cst"] = _pack_cst(depth, inp, lay, ncst)
    m["lnp"] = f(np.stack([np.asarray(inp["cmlp_ln_g"]), np.asarray(inp["cmlp_ln_b"])], 1))
    m["wsT"] = f(np.asarray(inp["cmlp_ws"]).transpose(0, 3, 1, 2))
    m["bs"] = f(inp["cmlp_bs"])
    return m


def run(cfg, inp, ncores=8):
    import time
    t0 = time.time()
    nc = build(cfg)
    t1 = time.time()
    in_maps = [_host_inputs(cfg, inp, c) for c in range(ncores)]
    t2 = time.time()
    res = run_bass_kernel_spmd(nc, in_maps, core_ids=list(range(ncores)))
    print("[kernel] build %.1fs host-layout %.1fs launch %.1fs" % (t1 - t0, t2 - t1, time.time() - t2), flush=True)
    depth, LP, NPS = cfg["depth"], cfg["LP"], cfg["NPS"]
    R = res.results
    cat = lambda fn, ax=0: np.concatenate([fn(r) for r in R], axis=ax)
    y_p = cat(lambda r: r["yp"].transpose(0, 2, 1))
    y_s = cat(lambda r: np.stack([r["ys"][:, 64 * j:64 * j + 32].T for j in range(4)], 0))
    h_p = cat(lambda r: r["hp"], 1)
    s_p = cat(lambda r: r["sp_o"].transpose(0, 1, 3, 4, 2), 1)
    c_p = cat(lambda r: r["cp"].transpose(0, 1, 4, 3, 2).reshape(depth, NPS, 3, 1536), 1)
    h_s = cat(lambda r: r["hs"], 1)
    s_s = cat(lambda r: r["sso"].transpose(0, 1, 3, 4, 2), 1)
    c_s = cat(lambda r: r["cso"].transpose(0, 1, 4, 3, 2).reshape(depth, 4, 3, 1536), 1)
    v_s = cat(lambda r: np.stack([r["vso"][:, 64 * j:64 * j + 32, :] for j in range(4)], 1), 1)
    outs = (y_p, y_s, h_p, s_p, c_p, h_s, s_s, c_s, v_s)
    return tuple(np.ascontiguousarray(o, dtype=np.float32) for o in outs)


def kernel(**inputs):
    cfg = dict(depth=4, LP=2048, NPS=2)
    return run(cfg, inputs, 8)
```

```python
import numpy as np
from contextlib import ExitStack
import concourse.bass as bass
import concourse.mybir as mybir
from concourse.bass_utils import run_bass_kernel_spmd

F32 = mybir.dt.float32
BF16 = mybir.dt.bfloat16
U32 = mybir.dt.uint32
AF = mybir.ActivationFunctionType
ALU = mybir.AluOpType

D = 2048
KC = 16
A_W = 1024
IN_TOTAL = 14864
OFF_Q, OFF_F, OFF_I, OFF_G, OFF_Z, OFF_XBC, OFF_DT, OFF_U, OFF_V, OFF_GATE = (
    0, 1024, 2048, 3072, 4096, 5120, 6656, 6672, 7696, 8720)
EPS = 1e-6
NEG = -30000.0


class Sched:
    ENGS = ("pe", "act", "dve", "pool", "sp")

    def __init__(self, nc):
        self.nc = nc
        self.ops = []
        self.last_w = {}
        self.readers = {}
        self.chan_last = {}
        self.chans = []
        self.tag = ""
        self.pe_log = None

    def add(self, eng, fn, r=(), w=(), dma=None):
        deps = set()
        for k in r:
            if k in self.last_w:
                deps.add(self.last_w[k])
        for k in w:
            if k in self.last_w:
                deps.add(self.last_w[k])
            deps.update(self.readers.get(k, ()))
        if dma is not None:
            if dma in self.chan_last:
                deps.add(self.chan_last[dma])
            else:
                self.chans.append(dma)
        i = len(self.ops)
        self.ops.append(dict(eng=eng, fn=fn, deps=deps, dma=dma, sig=False, tag=self.tag))
        for k in r:
            self.readers.setdefault(k, []).append(i)
        for k in w:
            self.last_w[k] = i
            self.readers[k] = []
        if dma is not None:
            self.chan_last[dma] = i
        return i

    def emit(self):
        nc = self.nc
        ops = self.ops
        for o in ops:
            for d in o["deps"]:
                ops[d]["sig"] = True
        cnt = {e: 0 for e in self.ENGS}
        ccnt = {c: 0 for c in self.chans}
        for o in ops:
            if o["dma"] is not None:
                ccnt[o["dma"]] += 16
                o["semk"] = ("c", o["dma"])
                o["val"] = ccnt[o["dma"]]
            elif o["sig"]:
                cnt[o["eng"]] += 1
                o["semk"] = ("e", o["eng"])
                o["val"] = cnt[o["eng"]]
        with ExitStack() as es:
            sems = {}
            for e in self.ENGS:
                sems[("e", e)] = es.enter_context(nc.semaphore("s_" + e))
            for c in self.chans:
                sems[("c", c)] = es.enter_context(nc.semaphore("d_" + str(c)))
            block = es.enter_context(nc.Block())
            per = {e: [o for o in ops if o["eng"] == e] for e in self.ENGS}
            final = [(("c", c), ccnt[c]) for c in self.chans]

            def run(e, h):
                waited = {}
                for o in per[e]:
                    need = {}
                    for d in o["deps"]:
                        p = ops[d]
                        k = p["semk"]
                        if p["val"] > need.get(k, 0):
                            need[k] = p["val"]
                    for k, v in need.items():
                        if waited.get(k, 0) < v:
                            h.wait_ge(sems[k], v)
                            waited[k] = v
                    if e == "pe" and self.pe_log is not None:
                        n0 = nc.n_instructions()
                        ins = o["fn"](h)
                        self.pe_log.append((o["tag"], nc.n_instructions() - n0))
                    else:
                        ins = o["fn"](h)
                    if o["dma"] is not None:
                        ins.then_inc(sems[o["semk"]], 16)
                    elif o["sig"]:
                        ins.then_inc(sems[o["semk"]], 1)
                if e == "sp":
                    for k, v in final:
                        if v > 0 and waited.get(k, 0) < v:
                            h.wait_ge(sems[k], v)

            @block.tensor
            def _(h):
                run("pe", h)

            @block.scalar
            def _(h):
                run("act", h)

            @block.vector
            def _(h):
                run("dve", h)

            @block.gpsimd
            def _(h):
                run("pool", h)

            @block.sync
            def _(h):
                run("sp", h)


def cst_layout(depth):
    items = [("ident", 128), ("n1g", depth * 16), ("n2g", depth * 16), ("fg", 16), ("bmod", depth * 96),
             ("lbraw", depth * 8), ("gon", depth * 8), ("convw", depth * 48), ("convb", depth * 12),
             ("dtb", depth * 16), ("alog", depth * 16), ("Dp", depth * 8), ("gbn", depth * 8),
             ("maskA", 8 * 64), ("blkm", 512),
             ("tri_p", 128), ("tri_s", 128), ("segb_p", 128), ("segb_s", 128),
             ("sel_s0", 128), ("sel_s1", 128), ("neg_p", 512), ("neg_s", 512), ("triu", 128), ("triu_s", 128)]
    lay = {}
    o = 0
    for n, c in items:
        lay[n] = (o, c)
        o += c
    return lay, o


def build(cfg):
    depth, LP, NPS = cfg["depth"], cfg["LP"], cfg["NPS"]
    NSS = 4
    TP = min(512, LP)
    lay, NCST = cst_layout(depth)
    nc = bass.Bass("TRN2", target_bir_lowering=False)

    def din(n, s, dt=F32):
        return nc.dram_tensor(n, list(s), dt, kind="ExternalInput").ap()

    def dout(n, s):
        return nc.dram_tensor(n, list(s), F32, kind="ExternalOutput").ap()

    xp = din("xp", [NPS, D, LP])
    xs = din("xs", [D, 256])
    cT = din("cT", [128, 16, NPS + 4])
    sh_in = din("sh_in", [depth, NSS, 8, 128, 128])
    ss_in = din("ss_in", [depth, NSS, 128, 16, 64])
    sc_in = din("sc_in", [depth, NSS, 128, 12, 3])
    w_mod = din("w_mod", [depth, D, 6 * D])
    w_in = din("w_in", [depth, D, IN_TOTAL])
    w_br = din("w_br", [depth, 3072, D])
    w_out = din("w_out", [depth, D, D])
    w_up = din("w_up", [depth, D, 4 * D])
    w_dn = din("w_dn", [depth, 4 * D, D])
    cst_d = din("cst", [128, NCST])
    lnp_d = din("lnp", [depth, 2, 1024])
    wsT_d = din("wsT", [depth, 128, 4, 128])
    bs_d = din("bs", [depth, 4, 128])

    yp = dout("yp", [NPS, D, LP])
    ys = dout("ys", [D, 256])
    hp = dout("hp", [depth, NPS, 8, 128, 128])
    sp_o = dout("sp_o", [depth, NPS, 128, 16, 64])
    cp = dout("cp", [depth, NPS, 128, 12, 3])
    hs = dout("hs", [depth, NSS, 8, 128, 128])
    sso = dout("sso", [depth, NSS, 128, 16, 64])
    cso = dout("cso", [depth, NSS, 128, 12, 3])
    vso = dout("vso", [depth, 256, 1024])

    WSPEC = {"w_in": (w_in, D, IN_TOTAL), "w_br": (w_br, 3072, D), "w_out": (w_out, D, D),
             "w_up": (w_up, D, 4 * D), "w_dn": (w_dn, 4 * D, D)}
    scr = {n_: nc.dram_tensor("scr_" + n_, [depth, k_, c_], BF16, kind="Internal").ap() for n_, (_, k_, c_) in WSPEC.items()}
    scr_keys = {}

    es = ExitStack()
    _n = [0]

    def sb(shape, dt, name=None):
        _n[0] += 1
        return es.enter_context(nc.sbuf_tensor(name or ("t%d" % _n[0]), list(shape), dt))

    S = Sched(nc)
    A = S.add
    QM = "pool"

    cst = sb([128, NCST], F32, "cst_sb")
    xT = sb([128, 16, 512], F32, "xT")
    hb = sb([128, 16, 512], BF16, "hb")
    W = [sb([128, 8192], BF16, "w%d" % i) for i in range(2)]
    NSEQ = NPS + NSS
    mgp = sb([128, 16, 512], BF16, "mg")
    modT = sb([128, depth, 96, NSEQ], F32, "modT")
    A1 = sb([128, depth, 16, NSEQ], F32, "A1")
    A2 = sb([128, depth, 16, NSEQ], F32, "A2")
    lb = sb([128, depth, 8], F32, "lb")
    oml = sb([128, depth, 8], F32, "oml")
    aneg = sb([128, depth, 16], F32, "aneg")
    idb = sb([128, 128], BF16, "idb")
    on11 = sb([128, 128], BF16, "on11")
    on7 = sb([128, 128], BF16, "on7")
    on9 = sb([128, 128], BF16, "on9")
    onf = sb([128, 128], F32, "onf")
    onrow = sb([1, 128], BF16, "onrow")
    WT = sb([128, 4, 128], BF16, "WT")
    brow = sb([1, 4, 128], BF16, "brow")
    convst = sb([128, depth, 12, 3], F32, "convst")
    csb = sb([128, 16, NSEQ], BF16, "csb")
    rstd = sb([128, 512], F32, "rstd")
    ftmp = [sb([128, 512], F32, "ftmp%d" % i) for i in range(2)]
    PS = [es.enter_context(nc.psum_tensor("ps%d" % i, [128, 512], F32)) for i in range(8)]
    ARN = 34 * 1024
    AR = sb([128, ARN], BF16, "arena")

    def C(name, l=None, n=None):
        o, c = lay[name]
        if l is None:
            return cst[:, o:o + c]
        return cst[:, o + l * n:o + (l + 1) * n]

    class Arena:
        def __init__(self):
            self.o = 0

        def reset(self):
            self.o = 0

        def get(self, nel, dt):
            nb = nel * (2 if dt == F32 else 1)
            nb = (nb + 15) // 16 * 16
            a = self.o
            self.o += nb
            assert self.o <= AR_TOP, ("arena overflow", self.o)
            ap = AR[:, a:a + nb]
            if dt == F32:
                ap = ap.bitcast(F32)
            ap = ap[:, 0:nel]
            keys = [("ar", g) for g in range(a // 1024, (a + nb - 1) // 1024 + 1)]
            return ap, keys

    ar = Arena()
    AR_TOP = ARN - 1536
    SGT = AR[:, AR_TOP:AR_TOP + 512]
    SGK = [("ar", g) for g in range(AR_TOP // 1024, (AR_TOP + 511) // 1024 + 1)]
    MT = AR[:, AR_TOP + 512:ARN].bitcast(F32)
    MTK = [("ar", g) for g in range((AR_TOP + 512) // 1024, (ARN - 1) // 1024 + 1)]

    def drain(gen, n=None):
        if gen is None:
            return
        i = 0
        while n is None or i < n:
            try:
                next(gen)
            except StopIteration:
                return
            i += 1
    _ps = [0]

    def psum():
        i = _ps[0] % 8
        _ps[0] += 1
        return PS[i], ("ps", i)

    _wi = [0]

    def wslot():
        i = _wi[0] % 2
        _wi[0] += 1
        return W[i], ("w", i), "wd%d" % i

    _ft = [0]

    def ft():
        i = _ft[0] % 2
        _ft[0] += 1
        return ftmp[i], ("ft", i)

    def act(out, in_, func, r, w, bias=None, scale=None):
        kw = {}
        if bias is not None:
            kw["bias"] = bias
        if scale is not None:
            kw["scale"] = scale
        return A("act", lambda h: h.activation(out=out, in_=in_, func=func, **kw), r=r, w=w)

    def tt(out, in0, in1, op, r, w, eng="dve"):
        return A(eng, lambda h: h.tensor_tensor(out=out, in0=in0, in1=in1, op=op), r=r, w=w)

    def ts(out, in0, s1, s2, op0, op1, r, w):
        if op1 is None:
            return A("dve", lambda h: h.tensor_scalar(out=out, in0=in0, scalar1=s1, scalar2=None, op0=op0), r=r, w=w)
        return A("dve", lambda h: h.tensor_scalar(out=out, in0=in0, scalar1=s1, scalar2=s2, op0=op0, op1=op1), r=r, w=w)

    def stt(out, in0, sc, in1, op0, op1, r, w):
        return A("dve", lambda h: h.scalar_tensor_tensor(out=out, in0=in0, scalar=sc, in1=in1, op0=op0, op1=op1), r=r, w=w)

    def dma(eng, out, in_, r, w, chan):
        return A(eng, lambda h: h.dma_start(out=out, in_=in_), r=r, w=w, dma=chan)

    def mms(ps_ap, pairs, r, w):
        def f(h):
            ins = None
            n = len(pairs)
            for i, (l_, r_) in enumerate(pairs):
                ins = h.matmul(ps_ap, lhsT=l_, rhs=r_, start=(i == 0), stop=(i == n - 1))
            return ins
        return A("pe", f, r=r, w=w)

    def load_w(src, kc, ncols, eng="sp"):
        src_ap, rkeys = src
        wt, wk, ch = wslot()
        v = wt[:, 0:kc * ncols].rearrange("p (k n) -> p k n", k=kc)
        dma(eng, v, src_ap, r=list(rkeys), w=[wk], chan=ch)
        return v, wk

    def wsrc(name, l, row0, kc, col0, ncols):
        return (scr[name][l, row0:row0 + kc * 128, col0:col0 + ncols].rearrange("(k p) n -> p k n", p=128),
                scr_keys[(name, l)])

    def linear_fm(name, l, row0, kc, col0, ncols, xin, xkeys, T, cb, gc=None, after_group=None):
        gc = gc or (8192 // kc)
        for g0 in range(0, ncols, gc):
            g = min(gc, ncols - g0)
            v, wk = load_w(wsrc(name, l, row0, kc, col0 + g0, g), kc, g)
            for j in range(0, g, 128):
                m = min(128, g - j)
                ps, pk = psum()
                mms(ps[0:m, 0:T], [(v[:, k, j:j + m], xin[:, k, 0:T]) for k in range(kc)],
                    r=[wk] + list(xkeys), w=[pk])
                cb((g0 + j) // 128, ps[0:m, 0:T], pk, m)
            if after_group is not None:
                after_group()

    def linear_tm(name, l, col0, ncols, xin, xkeys, T, cb, after_group=None):
        for g0 in range(0, ncols, 512):
            g = min(512, ncols - g0)
            v, wk = load_w(wsrc(name, l, 0, KC, col0 + g0, g), KC, g)
            for tc in range(T // 128):
                ps, pk = psum()
                mms(ps[:, 0:g], [(xin[:, k, tc * 128:(tc + 1) * 128], v[:, k, 0:g]) for k in range(KC)],
                    r=[wk] + list(xkeys), w=[pk])
                cb(tc, g0, g, ps[:, 0:g], pk)
            if after_group is not None:
                after_group()

    S.tag = "prologue"
    if cfg.get("pe_log") is not None:
        S.pe_log = cfg["pe_log"]
    dma("sp", cst[:], cst_d, r=[], w=["cst"], chan="cst")
    ctmp = sb([128, 16, NSEQ], F32, "ctmp")
    dma("sp", ctmp[:], cT, r=[], w=["ctmp"], chan="ct")
    act(csb[:], ctmp[:], AF.Silu, r=["ctmp"], w=["csb"])
    A("dve", lambda h: h.tensor_copy(out=idb[:], in_=C("ident")), r=["cst"], w=["idb"])
    A("dve", lambda h: h.memset(on11[:], 1.0 / 2048), w=["on11"])
    A("dve", lambda h: h.memset(on7[:], 1.0 / 128), w=["on7"])
    A("dve", lambda h: h.memset(on9[:], 1.0 / 512), w=["on9"])
    A("dve", lambda h: h.memset(onf[:], 1.0), w=["onf"])
    A("dve", lambda h: h.memset(onrow[:], 1.0), w=["onrow"])
    A("dve", lambda h: h.memset(convst[:], 0.0), w=["convst"])
    lbe = sb([128, depth, 8], F32, "lbe")
    lbs_ = sb([128, 8], F32, "lbs")
    act(lbe[:], C("lbraw").rearrange("p (l h) -> p l h", l=depth), AF.Exp, r=["cst"], w=["lbe"])
    A("dve", lambda h: h.tensor_copy(out=lbs_[:], in_=lbe[:, 0, :]), r=["lbe"], w=["lbs"])
    for l in range(1, depth):
        tt(lbs_[:], lbs_[:], lbe[:, l, :], ALU.add, r=["lbe", "lbs"], w=["lbs"])
    A("dve", lambda h: h.reciprocal(out=lbs_[:], in_=lbs_[:]), r=["lbs"], w=["lbs"])
    A("dve", lambda h: h.memset(lb[:], 0.0), w=["lb"])
    for l in range(1, depth):
        tt(lbe[:, l, :], lbe[:, l, :], lbs_[:], ALU.mult, r=["lbe", "lbs"], w=["lbe"])
        tt(lb[:, l, :], lb[:, l - 1, :], lbe[:, l, :], ALU.add, r=["lbe", "lb"], w=["lb"])
    ts(oml[:], lb[:], -1.0, 1.0, ALU.mult, ALU.add, r=["lb"], w=["oml"])
    act(aneg[:], C("alog").rearrange("p (l h) -> p l h", l=depth), AF.Exp, r=["cst"], w=["aneg"])
    ts(aneg[:], aneg[:], -1.0, None, ALU.mult, None, r=["aneg"], w=["aneg"])

    _cv = [0]

    def convert_layer(l):
        for n_, (wd_, k_, c_) in WSPEC.items():
            keys = []
            for r0 in range(0, k_, 512):
                key = ("scr", n_, l, r0)
                keys.append(key)
                ch = "cv%d" % (_cv[0] % 8)
                _cv[0] += 1
                dma("pool", scr[n_][l, r0:r0 + 512, :], wd_[l, r0:r0 + 512, :], r=[], w=[key], chan=ch)
            scr_keys[(n_, l)] = keys
    convert_layer(0)
    for l in range(depth):
        for g0 in range(0, 6 * D, 512):
            v, wk = load_w((w_mod[l, :, g0:g0 + 512].rearrange("(k p) n -> p k n", p=128), []), KC, 512, eng="pool")
            ps, pk = psum()
            for j in range(4):
                mms(ps[:, j * NSEQ:(j + 1) * NSEQ], [(v[:, k, j * 128:(j + 1) * 128], csb[:, k, :]) for k in range(KC)],
                    r=[wk, "csb"], w=[pk])
            c0 = g0 // 128
            tt(modT[:, l, c0:c0 + 4, :], ps[:, 0:4 * NSEQ].rearrange("p (j b) -> p j b", j=4),
               C("bmod", l, 96)[:, c0:c0 + 4].unsqueeze(2).broadcast_to([128, 4, NSEQ]), ALU.add,
               r=[pk, "cst"], w=["modT"])
        stt(A1[:, l, :, :], modT[:, l, 16:32, :], 1.0, C("n1g", l, 16).unsqueeze(2).broadcast_to([128, 16, NSEQ]),
            ALU.add, ALU.mult, r=["modT", "cst"], w=["A1"])
        stt(A2[:, l, :, :], modT[:, l, 64:80, :], 1.0, C("n2g", l, 16).unsqueeze(2).broadcast_to([128, 16, NSEQ]),
            ALU.add, ALU.mult, r=["modT", "cst"], w=["A2"])

    for l in range(1, depth):
        convert_layer(l)

    tiles = []
    for p in range(NPS):
        for t0 in range(0, LP, TP):
            tiles.append(dict(kind="p", T=TP, Cv=64, t0=t0, first=(t0 == 0), last=(t0 + TP >= LP),
                              segs=[dict(b=p, q=p, c0=0, L=TP)]))
    tiles.append(dict(kind="s", T=256, Cv=32, t0=0, first=True, last=True,
                      segs=[dict(b=NPS + j, q=j, c0=64 * j, L=32) for j in range(NSS)]))

    def rms_to(sq_ap, sq_keys, src_chunks, src_keys, ones, T, nch):
        ps, pk = psum()
        for c in range(nch):
            if c % 2 == 0:
                act(sq_ap[:, c, 0:T], src_chunks[c], AF.Square, r=src_keys, w=sq_keys)
            else:
                tt(sq_ap[:, c, 0:T], src_chunks[c], src_chunks[c], ALU.mult, r=src_keys, w=sq_keys)
        mms(ps[:, 0:T], [(ones[:], sq_ap[:, c, 0:T]) for c in range(nch)], r=sq_keys + ["ones"], w=[pk])
        act(rstd[:, 0:T], ps[:, 0:T], AF.Sqrt, r=[pk], w=["rstd"], bias=EPS)
        A("dve", lambda h: h.reciprocal(out=rstd[:, 0:T], in_=rstd[:, 0:T]), r=["rstd"], w=["rstd"])

    def norm_mod(tl, l, Atab, shoff, out_b, out_k):
        T = tl["T"]
        ar.reset()
        sq, sqk = ar.get(16 * 512, BF16)
        sq = sq.rearrange("p (c t) -> p c t", c=16)
        rms_to(sq, sqk, [xT[:, c, 0:T] for c in range(16)], ["xT"], on11, T, 16)
        for sg in tl["segs"]:
            c0, L, b = sg["c0"], sg["L"], sg["b"]
            for c in range(16):
                t_, tk = ft()
                tt(t_[:, 0:L], xT[:, c, c0:c0 + L], rstd[:, c0:c0 + L], ALU.mult, r=["xT", "rstd"], w=[tk])
                act(out_b[:, c, c0:c0 + L], t_[:, 0:L], AF.Identity, r=[tk, "A1", "A2", "modT"], w=[out_k],
                    bias=modT[:, l, shoff + c, b:b + 1], scale=Atab[:, l, c, b:b + 1])

    def merge_branch(l, k, ykT, ykeys, T, first, tag):
        sgt, sgk, mt, mtk = SGT, SGK, MT, MTK
        for g0 in range(0, D, 512):
            t_ = S.tag
            S.tag = tag
            vg_, wkg = load_w(wsrc("w_in", l, 0, KC, OFF_GATE + k * D + g0, 512), KC, 512)
            vb, wkb = load_w(wsrc("w_br", l, k * 1024, 8, g0, 512), 8, 512)
            S.tag = t_
            for jj in range(4):
                t_ = S.tag
                S.tag = tag
                j = g0 // 128 + jj
                ps, pk = psum()
                mms(ps[:, 0:T], [(vg_[:, kk_, jj * 128:(jj + 1) * 128], hb[:, kk_, 0:T]) for kk_ in range(KC)], r=[wkg, "hb"], w=[pk])
                act(sgt[:, 0:T], ps[:, 0:T], AF.Sigmoid, r=[pk], w=sgk)
                ps2, pk2 = psum()
                mms(ps2[:, 0:T], [(vb[:, kk_, jj * 128:(jj + 1) * 128], ykT[:, kk_, 0:T]) for kk_ in range(8)], r=[wkb] + list(ykeys), w=[pk2])
                if first:
                    tt(mgp[:, j, 0:T], sgt[:, 0:T], ps2[:, 0:T], ALU.mult, r=sgk + [pk2], w=["mg"])
                else:
                    tt(mt[:, 0:T], sgt[:, 0:T], ps2[:, 0:T], ALU.mult, r=sgk + [pk2], w=mtk)
                    tt(mgp[:, j, 0:T], mgp[:, j, 0:T], mt[:, 0:T], ALU.add, r=mtk + ["mg"], w=["mg"])
                S.tag = t_
            yield

    def layer(tl, l):
        T, Cv, kind = tl["T"], tl["Cv"], tl["kind"]
        NT = T // 128
        segs = tl["segs"]
        nseg = len(segs)
        S.tag = "norm1"
        norm_mod(tl, l, A1, 0, hb, "hb")
        ar.reset()

        S.tag = "C.proj"
        uT, uk = ar.get(8 * 512, BF16)
        uT = uT.rearrange("p (c t) -> p c t", c=8)
        vtok, vk = ar.get(NT * 1024, BF16)
        vtok = vtok.rearrange("p (n c) -> p n c", n=NT)
        vg, vgk = ar.get(1024, F32)
        lnp, lnk = ar.get(2048, F32)
        st6, st6k = ar.get(16, F32)
        wst, wstk = ar.get(512, F32)
        bst, bstk = ar.get(512, F32)
        wsv = wst.rearrange("p (g t) -> p g t", g=4)
        if kind == "p":
            dma(QM, wsv, wsT_d[l], r=[], w=wstk, chan="misc")
            dma(QM, bst[0:1, :].rearrange("p (g t) -> p g t", g=4), bs_d[l].unsqueeze(0), r=[], w=bstk, chan="misc")
            tt(WT[:], wsv, C("triu").unsqueeze(1).broadcast_to([128, 4, 128]), ALU.mult, r=wstk + ["cst"], w=["WT"])
        else:
            A("dve", lambda h: h.memset(wst[:], 0.0), w=wstk)
            A("dve", lambda h: h.memset(bst[0:1, :], 0.0), w=bstk)
            for pb_ in (0, 64):
                dma(QM, wsv[pb_:pb_ + 32, :, pb_:pb_ + 32], wsT_d[l, 0:32, :, 0:32], r=[], w=wstk, chan="misc")
                dma(QM, bst[0:1, :].rearrange("p (g t) -> p g t", g=4)[:, :, pb_:pb_ + 32], bs_d[l, :, 0:32].unsqueeze(0), r=[], w=bstk, chan="misc")
            tt(WT[:], wsv, C("triu_s").unsqueeze(1).broadcast_to([128, 4, 128]), ALU.mult, r=wstk + ["cst"], w=["WT"])
        A("dve", lambda h: h.tensor_copy(out=brow[:], in_=bst[0:1, :].rearrange("p (g t) -> p g t", g=4)), r=bstk, w=["brow"])
        dma(QM, lnp.rearrange("p (a c) -> p a c", a=2), lnp_d[l:l + 1].broadcast_to([128, 2, 1024]) if False else
            lnp_d[l].unsqueeze(0).broadcast_to([128, 2, 1024]), r=[], w=lnk, chan="misc")

        def cb_u(j, ps, pk, m):
            act(uT[:, j, 0:T], ps, AF.Gelu, r=[pk], w=uk)
        linear_fm("w_in", l, 0, KC, OFF_U, 1024, hb, ["hb"], T, cb_u)

        def cb_v(tc, g0, g, ps, pk):
            act(vg[:, g0:g0 + g], ps, AF.Gelu, r=[pk], w=vgk)
            A("dve", lambda h: h.bn_stats(out=st6[:, (g0 // 512) * 6:(g0 // 512) * 6 + 6], in_=vg[:, g0:g0 + g]), r=vgk, w=st6k)
            if g0 + g == 1024:
                A("dve", lambda h: h.bn_aggr(out=st6[:, 12:14], in_=st6[:, 0:12]), r=st6k, w=st6k)
                act(st6[:, 14:15], st6[:, 13:14], AF.Sqrt, r=st6k, w=st6k, bias=EPS)
                A("dve", lambda h: h.reciprocal(out=st6[:, 14:15], in_=st6[:, 14:15]), r=st6k, w=st6k)
                ts(vg[:, :], vg[:, :], st6[:, 12:13], st6[:, 14:15], ALU.subtract, ALU.mult, r=vgk + st6k, w=vgk)
                tt(vg[:, :], vg[:, :], lnp[:, 0:1024], ALU.mult, r=vgk + lnk, w=vgk)
                tt(vg[:, :], vg[:, :], lnp[:, 1024:2048], ALU.add, r=vgk + lnk, w=vgk)
                A("dve", lambda h: h.tensor_copy(out=vtok[:, tc, :], in_=vg[:, :]), r=vgk, w=vk)
                if kind == "s":
                    dma(QM, vso[l, tc * 128:(tc + 1) * 128, :], vg[:, :], r=vgk, w=[], chan="vso")
        vw = []
        for g0 in (0, 512):
            vw.append(load_w(wsrc("w_in", l, 0, KC, OFF_V + g0, 512), KC, 512))
        for tc in range(NT):
            for gi, g0 in enumerate((0, 512)):
                v_, wk_ = vw[gi]
                ps, pk = psum()
                mms(ps[:, 0:512], [(hb[:, k, tc * 128:(tc + 1) * 128], v_[:, k, 0:512]) for k in range(KC)],
                    r=[wk_, "hb"], w=[pk])
                cb_v(tc, g0, 512, ps[:, 0:512], pk)
        S.tag = "C.mix"
        for ch in range(8):
            g = ch // 2
            ps, pk = psum()
            for tc in range(NT):
                mms(ps[:, tc * 128:(tc + 1) * 128],
                    [(vtok[:, tc, ch * 128:(ch + 1) * 128], WT[:, g, :]),
                     (onrow[0:1, :], brow[0:1, g, :])],
                    r=vk + ["WT", "brow", "onrow"], w=[pk])
            tt(uT[:, ch, 0:T], uT[:, ch, 0:T], ps[:, 0:T], ALU.mult, r=uk + [pk], w=uk)
        cmg = merge_branch(l, 2, uT, uk, T, True, "C.merge")

        S.tag = "A.i"
        ar.reset()
        oT, ok_ = ar.get(8 * 512, BF16)
        qt, qk = ar.get(8 * 512, BF16)
        kt, kk = ar.get(8 * 512, BF16)
        khat, khk = ar.get(NT * 1024, BF16)
        vat, vak = ar.get(NT * 1024, BF16)
        gs, gsk = qt, qk
        qt = qt.rearrange("p (c t) -> p c t", c=8)
        kt = kt.rearrange("p (c t) -> p c t", c=8)
        gs = gs.rearrange("p (c t) -> p c t", c=8)
        oT = oT.rearrange("p (c t) -> p c t", c=8)
        khat = khat.rearrange("p (n c) -> p n c", n=NT)
        vat = vat.rearrange("p (n c) -> p n c", n=NT)
        NB = T // 64
        dec, deck = ar.get(8 * 8, F32)
        em, emk = ar.get(8 * 8, F32)
        dec = dec.rearrange("p (h n) -> p h n", h=8)
        em = em.rearrange("p (h n) -> p h n", h=8)
        f1, f1k = ar.get(512, F32)
        f2, f2k = ar.get(512, F32)
        f3, f3k = ar.get(512, F32)
        f4, f4k = ar.get(512, F32)
        f5, f5k = ar.get(512, F32)
        khTs = [ar.get(512, BF16) for _ in range(2)]
        attb, attk = ar.get(8 * 64, BF16)
        attb = attb.rearrange("p (h t) -> p h t", h=8)
        Sst, Sk = ar.get(1024, F32)
        Sbf, Sbk = ar.get(1024, BF16)
        Sst = Sst.rearrange("p (h v) -> p h v", h=8)
        Sbf = Sbf.rearrange("p (h v) -> p h v", h=8)
        osq, osqk = ar.get(512, BF16)

        def cb_i(tc, g0, g, ps, pk):
            act(vat[:, tc, g0:g0 + g], ps, AF.Copy, r=[pk], w=vak)
        linear_tm("w_in", l, OFF_I, 1024, hb, ["hb"], T, cb_i, after_group=lambda: drain(cmg, 1))

        S.tag = "A.qf"
        mid, last = Cv // 2 - 1, Cv - 1

        def qf_tail(hd, khT, khTk):
            pst, ptk = psum()
            pstb = pst[:].bitcast(BF16)

            def trf(h):
                ins = None
                for tc in range(NT):
                    ins = h.transpose(out=pstb[:, tc * 128:(tc + 1) * 128], in_=khT[:, tc * 128:(tc + 1) * 128], identity=idb[:])
                return ins
            A("pe", trf, r=khTk + ["idb"], w=[ptk])
            act(khat[:, :, hd * 128:(hd + 1) * 128], pstb[:, 0:NT * 128].rearrange("p (n c) -> p n c", n=NT), AF.Copy,
                r=[ptk], w=khk)
        pend = None
        for hd in range(8):
            wt, wk, ch_ = wslot()
            v = wt[:, 0:KC * 256].rearrange("p (two k n) -> p two k n", k=KC, two=2)
            sq_, sqk_ = wsrc("w_in", l, 0, KC, OFF_Q + hd * 128, 128)
            sf_, _ = wsrc("w_in", l, 0, KC, OFF_F + hd * 128, 128)
            dma("sp", v[:, 0], sq_, r=list(sqk_), w=[wk], chan=ch_)
            dma("sp", v[:, 1], sf_, r=list(sqk_), w=[wk], chan=ch_)
            psq, pqk = psum()
            mms(psq[:, 0:T], [(v[:, 0, k, :], hb[:, k, 0:T]) for k in range(KC)], r=[wk, "hb"], w=[pqk])
            psf, pfk = psum()
            mms(psf[:, 0:T], [(v[:, 1, k, :], hb[:, k, 0:T]) for k in range(KC)], r=[wk, "hb"], w=[pfk])
            act(f1[:, 0:T], psq[:, 0:T], AF.Silu, r=[pqk], w=f1k)
            act(f2[:, 0:T], psf[:, 0:T], AF.Sigmoid, r=[pfk], w=f2k)
            ts(f2[:, 0:T], f2[:, 0:T], oml[:, l, hd:hd + 1], lb[:, l, hd:hd + 1], ALU.mult, ALU.add,
               r=f2k + ["oml", "lb"], w=f2k)
            act(f3[:, 0:T], f2[:, 0:T], AF.Ln, r=f2k, w=f3k)
            ts(f2[:, 0:T], f2[:, 0:T], -1.0, 1.0, ALU.mult, ALU.add, r=f2k, w=f2k)
            A("dve", lambda h: h.tensor_tensor_scan(out=f4[:, 0:T], data0=C("blkm")[:, 0:T], data1=f3[:, 0:T],
                                                     initial=0.0, op0=ALU.mult, op1=ALU.add),
              r=f3k + ["cst"], w=f4k)
            b3 = f4[:, 0:T].rearrange("p (n c) -> p n c", c=64)
            d3 = f3[:, 0:T].rearrange("p (n c) -> p n c", c=64)
            tt(d3, b3, b3[:, :, mid:mid + 1].broadcast_to([128, NB, 64]), ALU.subtract, r=f4k, w=f3k)
            act(dec[:, hd, 0:NB], b3[:, :, last], AF.Exp, r=f4k, w=deck)
            act(em[:, hd, 0:NB], b3[:, :, mid], AF.Exp, r=f4k, w=emk)
            act(f4[:, 0:T], f3[:, 0:T], AF.Exp, r=f3k, w=f4k)
            act(f5[:, 0:T], f3[:, 0:T], AF.Exp, r=f3k, w=f5k, scale=-1.0)
            tt(qt[:, hd, 0:T], f1[:, 0:T], f4[:, 0:T], ALU.mult, r=f1k + f4k, w=qk)
            tt(kt[:, hd, 0:T], f2[:, 0:T], f5[:, 0:T], ALU.mult, r=f2k + f5k, w=kk)
            e13 = f4[:, 0:T].rearrange("p (n c) -> p n c", c=64)
            khT, khTk = khTs[hd % 2]
            tt(khT[:, 0:T].rearrange("p (n c) -> p n c", c=64), kt[:, hd, 0:T].rearrange("p (n c) -> p n c", c=64),
               e13[:, :, last:last + 1].broadcast_to([128, NB, 64]), ALU.mult, r=kk + f4k, w=khTk)
            if pend is not None:
                qf_tail(*pend)
            pend = (hd, khT, khTk)
            if hd in (1, 3):
                drain(cmg, 1)
        qf_tail(*pend)
        drain(cmg)

        S.tag = "A.rec"
        hst_in = None
        for sg in segs:
            c0, L, q = sg["c0"], sg["L"], sg["q"]
            Sv = Sst.rearrange("p h v -> p h v")
            if kind == "p":
                if tl["first"]:
                    A("dve", lambda h: h.memset(Sst[:], 0.0), w=Sk)
                else:
                    dma(QM, Sst, hp[l, q].rearrange("h k v -> k h v"), r=[], w=Sk, chan="hst")
            else:
                dma(QM, Sst, sh_in[l, q].rearrange("h k v -> k h v"), r=[], w=Sk, chan="hst")
            for bi in range(L // Cv):
                col = c0 + bi * Cv
                tc, pb, nb = col // 128, col % 128, col // 64
                psa, pak = psum()

                def attf(h, psa=psa, col=col, pb=pb):
                    ins = None
                    for hd in range(8):
                        ins = h.matmul(psa[pb:pb + Cv, hd * Cv:(hd + 1) * Cv], lhsT=kt[:, hd, col:col + Cv],
                                       rhs=qt[:, hd, col:col + Cv], start=True, stop=True)
                    return ins
                A("pe", attf, r=qk + kk, w=[pak])
                A("dve", lambda h, pb=pb: h.memset(attb[pb:pb + Cv, :, 0:Cv], 0.0), w=attk)
                A("dve", lambda h, pb=pb, psa=psa: h.copy_predicated(
                    out=attb[pb:pb + Cv, :, 0:Cv],
                    mask=C("maskA").bitcast(U32).rearrange("p (h t) -> p h t", h=8)[pb:pb + Cv, :, 0:Cv],
                    data=psa[pb:pb + Cv, 0:8 * Cv].rearrange("p (h t) -> p h t", h=8)),
                  r=[pak, "cst"] + attk, w=attk)
                tt(Sbf[:], Sst[:], em[:, :, nb:nb + 1].broadcast_to([128, 8, 128]), ALU.mult, r=Sk + emk, w=Sbk)
                pso, pok = psum()

                def of(h, pso=pso, col=col, pb=pb, tc=tc):
                    ins = None
                    for hd in range(8):
                        h.matmul(pso[:, hd * Cv:(hd + 1) * Cv], lhsT=Sbf[:, hd, :], rhs=qt[:, hd, col:col + Cv],
                                 start=True, stop=False)
                        ins = h.matmul(pso[:, hd * Cv:(hd + 1) * Cv], lhsT=vat[pb:pb + Cv, tc, hd * 128:(hd + 1) * 128],
                                       rhs=attb[pb:pb + Cv, hd, 0:Cv], start=False, stop=True)
                    return ins
                A("pe", of, r=Sbk + qk + vak + attk, w=[pok])
                act(oT[:, :, col:col + Cv], pso[:, 0:8 * Cv].rearrange("p (h t) -> p h t", h=8), AF.Copy, r=[pok], w=ok_)
                for half in range(2):
                    pss, psk = psum()

                    def sf(h, pss=pss, half=half, pb=pb, tc=tc):
                        ins = None
                        for hh in range(4):
                            hd = half * 4 + hh
                            ins = h.matmul(pss[:, hh * 128:(hh + 1) * 128], lhsT=khat[pb:pb + Cv, tc, hd * 128:(hd + 1) * 128],
                                           rhs=vat[pb:pb + Cv, tc, hd * 128:(hd + 1) * 128], start=True, stop=True)
                        return ins
                    A("pe", sf, r=khk + vak, w=[psk])
                    for hh in range(4):
                        hd = half * 4 + hh
                        stt(Sst[:, hd, :], Sst[:, hd, :], dec[:, hd, nb:nb + 1], pss[:, hh * 128:(hh + 1) * 128],
                            ALU.mult, ALU.add, r=Sk + deck + [psk], w=Sk)
            dst = (hp if kind == "p" else hs)[l, q].rearrange("h k v -> k h v")
            dma(QM, dst, Sst, r=Sk, w=[], chan="hst")
        S.tag = "A.gnorm"
        def cb_g(j, ps, pk, m):
            act(gs[:, j, 0:T], ps, AF.Silu, r=[pk], w=gsk)
        linear_fm("w_in", l, 0, KC, OFF_G, 1024, hb, ["hb"], T, cb_g)
        for hd in range(8):
            act(osq[:, 0:T], oT[:, hd, 0:T], AF.Square, r=ok_, w=osqk)
            ps, pk = psum()
            mms(ps[:, 0:T], [(on7[:], osq[:, 0:T])], r=osqk + ["ones"], w=[pk])
            act(f1[:, 0:T], ps[:, 0:T], AF.Sqrt, r=[pk], w=f1k, bias=EPS)
            A("dve", lambda h: h.reciprocal(out=f1[:, 0:T], in_=f1[:, 0:T]), r=f1k, w=f1k)
            tt(f2[:, 0:T], oT[:, hd, 0:T], f1[:, 0:T], ALU.mult, r=ok_ + f1k, w=f2k)
            stt(oT[:, hd, 0:T], f2[:, 0:T], C("gon", l, 8)[:, hd:hd + 1], gs[:, hd, 0:T], ALU.mult, ALU.mult,
                r=f2k + gsk + ["cst"], w=ok_)
        amg = merge_branch(l, 0, oT, ok_, T, False, "A.merge")

        S.tag = "B.xbc"
        ar.reset()
        ybT, ybk = ar.get(8 * 512, BF16)
        xbc, xbk = ar.get(12 * 512, BF16)
        xst, xsk = ar.get(NT * 1024, BF16)
        btk_, btkk = ar.get(NT * 256, BF16)
        xbc = xbc.rearrange("p (c t) -> p c t", c=12)
        ybT = ybT.rearrange("p (c t) -> p c t", c=8)
        xst = xst.rearrange("p (n c) -> p n c", n=NT)
        btk_ = btk_.rearrange("p (n c) -> p n c", n=NT)
        xp_, xpk = ar.get(520, F32)
        acc, acck = ar.get(512, F32)
        dtt, dtk = ar.get(NT * 16, F32)
        att_, atk = ar.get(NT * 16, F32)
        dtt = dtt.rearrange("p (n h) -> p n h", n=NT)
        att_ = att_.rearrange("p (n h) -> p n h", n=NT)
        cst_in, csik = ar.get(4 * 36, F32)
        cst_out, csok = ar.get(4 * 36, F32)
        cst_in = cst_in.rearrange("p (q c j) -> p q c j", q=4, c=12)
        cst_out = cst_out.rearrange("p (q c j) -> p q c j", q=4, c=12)
        acs, acsk = ar.get(64, F32)
        edb, edbk = ar.get(2 * 16, F32)
        TriA, trak = ar.get(4 * 128, F32)
        Lm, lmk = ar.get(4 * 128, F32)
        ea4, ea4k = ar.get(4 * 128, BF16)
        Mb, mbk = ar.get(16 * 128, BF16)
        Ct, ctk = ar.get(16 * 128, BF16)
        CBT, cbtk = ar.get(256, F32)
        xdt, xdk = ar.get(1024, BF16)
        xw, xwk = ar.get(1024, BF16)
        y1, y1k = ar.get(128, F32)
        ST, STk = ar.get(1024, F32)
        Sb2, Sb2k = ar.get(2 * 1024, BF16)
        zst, zstk = ar.get(512, BF16)
        TriA = TriA.rearrange("p (h t) -> p h t", h=4)
        Lm = Lm.rearrange("p (h t) -> p h t", h=4)
        ea4 = ea4.rearrange("p (h t) -> p h t", h=4)
        ysq, ysqk = Mb.rearrange("p (c t) -> p c t", c=4), mbk
        Mb = Mb.rearrange("p (h t) -> p h t", h=16)
        Ct = Ct.rearrange("p (h t) -> p h t", h=16)
        CBT = CBT.rearrange("p (g t) -> p g t", g=2)
        xdt = xdt.rearrange("p (h q) -> p h q", h=16)
        xw = xw.rearrange("p (h q) -> p h q", h=16)
        ST = ST.rearrange("p (h q) -> p h q", h=16)
        Sb2 = Sb2.rearrange("p (s h q) -> p s h q", s=2, h=16)

        if kind == "s":
            dma(QM, cst_in, sc_in[l].rearrange("q p c j -> p q c j"), r=[], w=csik, chan="cvs")
            xp3 = xp_[:, 0:4 * 35].rearrange("p (q t) -> p q t", q=4)
        else:
            xp3 = xp_[:, 0:3 + T].rearrange("p (q t) -> p q t", q=1)
        Lc = segs[0]["L"]

        def cb_x(j, ps, pk, m):
            if kind == "s":
                A("dve", lambda h: h.tensor_copy(out=xp3[:, :, 0:3], in_=cst_in[:, :, j, :]), r=csik, w=xpk)
                act(xp3[:, :, 3:3 + Lc], ps.rearrange("p (q t) -> p q t", q=4)[:, :, 0:Lc], AF.Copy, r=[pk], w=xpk)
            else:
                if tl["first"]:
                    A("dve", lambda h: h.memset(xp3[:, 0, 0:3], 0.0), w=xpk)
                else:
                    A("dve", lambda h: h.tensor_copy(out=xp3[:, 0, 0:3], in_=convst[:, l, j, :]), r=["convst"], w=xpk)
                act(xp3[:, 0, 3:3 + Lc], ps, AF.Copy, r=[pk], w=xpk)
            nq = xp3.shape[1]
            a3 = acc[:, 0:nq * Lc].rearrange("p (q t) -> p q t", q=nq)
            cw = C("convw", l, 48)
            ts(a3, xp3[:, :, 0:Lc], cw[:, j:j + 1], C("convb", l, 12)[:, j:j + 1], ALU.mult, ALU.add,
               r=xpk + ["cst"], w=acck)
            for jj in range(1, 4):
                stt(a3, xp3[:, :, jj:jj + Lc], cw[:, jj * 12 + j:jj * 12 + j + 1], a3, ALU.mult, ALU.add,
                    r=xpk + acck + ["cst"], w=acck)
            if kind == "s":
                act(xbc[:, j, 0:T].rearrange("p (q t) -> p q t", q=4)[:, :, 0:Lc], a3, AF.Silu, r=acck, w=xbk)
                A("dve", lambda h: h.tensor_copy(out=cst_out[:, :, j, :], in_=xp3[:, :, Lc:Lc + 3]), r=xpk, w=csok)
            else:
                act(xbc[:, j, 0:T], a3[:, 0, :], AF.Silu, r=acck, w=xbk)
                A("dve", lambda h: h.tensor_copy(out=convst[:, l, j, :], in_=xp3[:, 0, Lc:Lc + 3]), r=xpk, w=["convst"])
        if kind == "s":
            A("dve", lambda h: h.memset(xbc[:, :, 0:T], 0.0), w=xbk)
        linear_fm("w_in", l, 0, KC, OFF_XBC, 1536, hb, ["hb"], T, cb_x, after_group=lambda: drain(amg, 1))
        if kind == "s":
            dma(QM, cso[l].rearrange("q p c j -> p q c j"), cst_out, r=csok, w=[], chan="cvs")
        elif tl["last"]:
            dma(QM, cp[l, segs[0]["q"]], convst[:, l, :, :], r=["convst"], w=[], chan="cvs")

        S.tag = "B.dt_tr"
        vdt, wkdt = load_w(wsrc("w_in", l, 0, KC, OFF_DT, 16), KC, 16)
        for tc in range(NT):
            ps, pk = psum()
            mms(ps[:, 0:16], [(hb[:, k, tc * 128:(tc + 1) * 128], vdt[:, k, 0:16]) for k in range(KC)], r=[wkdt, "hb"], w=[pk])
            tt(dtt[:, tc, :], ps[:, 0:16], C("dtb", l, 16), ALU.add, r=[pk, "cst"], w=dtk)
            act(dtt[:, tc, :], dtt[:, tc, :], AF.Exp, r=dtk, w=dtk)
            act(dtt[:, tc, :], dtt[:, tc, :], AF.Ln, r=dtk, w=dtk, bias=1.0)
            tt(att_[:, tc, :], dtt[:, tc, :], aneg[:, l, :], ALU.mult, r=dtk + ["aneg"], w=atk)
        for tc in range(NT):
            pst, ptk = psum()
            pstb = pst[:].bitcast(BF16)

            def trx(h, pstb=pstb, tc=tc):
                ins = None
                for c in range(8):
                    ins = h.transpose(out=pstb[:, c * 128:(c + 1) * 128], in_=xbc[:, c, tc * 128:(tc + 1) * 128], identity=idb[:])
                return ins
            A("pe", trx, r=xbk + ["idb"], w=[ptk])
            act(xst[:, tc, :], pstb[:, 0:1024], AF.Copy, r=[ptk], w=xsk)
            pst2, ptk2 = psum()
            pstb2 = pst2[:].bitcast(BF16)

            def trb(h, pstb2=pstb2, tc=tc):
                ins = None
                for c in range(2):
                    ins = h.transpose(out=pstb2[:, c * 128:(c + 1) * 128], in_=xbc[:, 8 + c, tc * 128:(tc + 1) * 128], identity=idb[:])
                return ins
            A("pe", trb, r=xbk + ["idb"], w=[ptk2])
            act(btk_[:, tc, :], pstb2[:, 0:256], AF.Copy, r=[ptk2], w=btkk)

        drain(amg)
        S.tag = "B.ssd"
        tri = C("tri_p") if kind == "p" else C("tri_s")
        segb = C("segb_p") if kind == "p" else C("segb_s")
        negm = C("neg_p") if kind == "p" else C("neg_s")
        if kind == "p":
            if tl["first"]:
                A("dve", lambda h: h.memset(ST[:], 0.0), w=STk)
            else:
                dma(QM, ST, sp_o[l, segs[0]["q"]], r=[], w=STk, chan="sst")
            A("dve", lambda h: h.tensor_copy(out=Sb2[:, 0], in_=ST[:]), r=STk, w=Sb2k)
        for tc in range(NT):
            csegs = [sg for sg in segs if sg["c0"] // 128 == tc] if kind == "s" else [dict(q=segs[0]["q"], c0=tc * 128, L=128)]
            ps, pk = psum()
            mms(ps[:, 0:16], [(tri, att_[:, tc, :])], r=atk + ["cst"], w=[pk])
            mms(ps[:, 16:32], [(segb, att_[:, tc, :])], r=atk + ["cst"], w=[pk])
            A("dve", lambda h, ps=ps: h.tensor_copy(out=acs[:, 0:16], in_=ps[:, 0:16]), r=[pk], w=acsk)
            ts(acs[:, 16:32], ps[:, 0:16], -1.0, None, ALU.mult, None, r=[pk], w=acsk)
            tt(acs[:, 32:48], ps[:, 16:32], acs[:, 0:16], ALU.subtract, r=[pk] + acsk, w=acsk)
            act(acs[:, 48:64], acs[:, 32:48], AF.Exp, r=acsk, w=acsk)
            psc, pck = psum()
            for g in range(2):
                mms(psc[:, g * 128:(g + 1) * 128], [(xbc[:, 8 + g, tc * 128:(tc + 1) * 128], xbc[:, 10 + g, tc * 128:(tc + 1) * 128])],
                    r=xbk, w=[pck])
            act(CBT[:], psc[:, 0:256].rearrange("p (g t) -> p g t", g=2), AF.Copy, r=[pck], w=cbtk)
            for q4 in range(4):
                g = q4 // 2
                tt(TriA[:], tri.unsqueeze(1).broadcast_to([128, 4, 128]),
                   att_[:, tc, q4 * 4:(q4 + 1) * 4].unsqueeze(2).broadcast_to([128, 4, 128]), ALU.mult, r=atk + ["cst"], w=trak)
                psA, pAk = psum()
                mms(psA[:, 0:512], [(onf[:], TriA[:])], r=trak + ["ones"], w=[pAk])
                act(ea4[:], psA[:, 0:512].rearrange("p (h t) -> p h t", h=4), AF.Exp, r=[pAk], w=ea4k)
                tt(Ct[:, q4 * 4:(q4 + 1) * 4, :], ea4[:], xbc[:, 10 + g:11 + g, tc * 128:(tc + 1) * 128].broadcast_to([128, 4, 128]),
                   ALU.mult, r=ea4k + xbk, w=ctk)
                psB, pBk = psum()
                mms(psB[:, 0:512], [(onf[:], TriA[:]), (C("ident"), negm)], r=trak + ["ones", "cst"], w=[pBk])
                for hh in range(4):
                    hd = q4 * 4 + hh
                    act(Lm[:, hh, :], psB[:, hh * 128:(hh + 1) * 128], AF.Exp, r=[pBk] + acsk, w=lmk, bias=acs[:, 16 + hd:17 + hd])
                tt(Mb[:, q4 * 4:(q4 + 1) * 4, :], Lm[:], CBT[:, g:g + 1, :].broadcast_to([128, 4, 128]), ALU.mult, r=lmk + cbtk, w=mbk)
            tt(xdt[:], xst[:, tc, :].rearrange("p (h q) -> p h q", h=16), dtt[:, tc, :].unsqueeze(2).broadcast_to([128, 16, 64]),
               ALU.mult, r=xsk + dtk, w=xdk)
            tt(xw[:], xdt[:], acs[:, 48:64].unsqueeze(2).broadcast_to([128, 16, 64]), ALU.mult, r=xdk + acsk, w=xwk)
            for half in range(2):
                psy, pyk = psum()

                def yf(h, psy=psy, half=half, tc=tc, csegs=csegs):
                    ins = None
                    for cc in range(4):
                        hc = half * 4 + cc
                        for hh in range(2):
                            hd = hc * 2 + hh
                            o_ = psy[hh * 64:(hh + 1) * 64, cc * 128:(cc + 1) * 128]
                            h.matmul(o_, lhsT=xdt[:, hd, :], rhs=Mb[:, hd, :], start=True, stop=False)
                            for si, sg in enumerate(csegs):
                                lc0 = sg["c0"] % 128
                                ins = h.matmul(psy[hh * 64:(hh + 1) * 64, cc * 128 + lc0:cc * 128 + lc0 + sg["L"]],
                                               lhsT=Sb2[:, si if kind == "s" else 0, hd, :], rhs=Ct[:, hd, lc0:lc0 + sg["L"]],
                                               start=False, stop=(si == len(csegs) - 1))
                    return ins
                if kind == "s" and half == 0:
                    for si, sg in enumerate(csegs):
                        dma(QM, ST, ss_in[l, sg["q"]], r=[], w=STk, chan="sst")
                        A("dve", lambda h, si=si: h.tensor_copy(out=Sb2[:, si], in_=ST[:]), r=STk, w=Sb2k)
                A("pe", yf, r=xdk + mbk + Sb2k + ctk, w=[pyk])
                for cc in range(4):
                    hc = half * 4 + cc
                    stt(ybT[:, hc, tc * 128:(tc + 1) * 128], xbc[:, hc, tc * 128:(tc + 1) * 128], C("Dp", l, 8)[:, hc:hc + 1],
                        psy[:, cc * 128:(cc + 1) * 128], ALU.mult, ALU.add, r=xbk + [pyk, "cst"], w=ybk)
            for si, sg in enumerate(csegs):
                lc0, L = sg["c0"] % 128, sg["L"]
                if kind == "s":
                    dma(QM, ST, ss_in[l, sg["q"]], r=[], w=STk, chan="sst")
                psd, pdk = psum()
                sel = onf[:] if kind == "p" else C("sel_s%d" % si)
                mms(psd[:, 0:16], [(sel, att_[:, tc, :])], r=atk + ["ones", "cst"], w=[pdk])
                act(edb[:, 0:16], psd[:, 0:16], AF.Exp, r=[pdk], w=edbk)
                tt(ST[:], ST[:], edb[:, 0:16].unsqueeze(2).broadcast_to([128, 16, 64]), ALU.mult, r=STk + edbk, w=STk)
                for half in range(2):
                    pss, psk = psum()

                    def suf(h, pss=pss, half=half, lc0=lc0, L=L, tc=tc):
                        ins = None
                        for hh in range(8):
                            hd = half * 8 + hh
                            g = hd // 8
                            ins = h.matmul(pss[:, hh * 64:(hh + 1) * 64], lhsT=btk_[lc0:lc0 + L, tc, g * 128:(g + 1) * 128],
                                           rhs=xw[lc0:lc0 + L, hd, :], start=True, stop=True)
                        return ins
                    A("pe", suf, r=btkk + xwk, w=[psk])
                    tt(ST[:, half * 8:(half + 1) * 8, :], ST[:, half * 8:(half + 1) * 8, :],
                       pss[:, 0:512].rearrange("p (h q) -> p h q", h=8), ALU.add, r=STk + [psk], w=STk)
                if kind == "s":
                    dma(QM, sso[l, sg["q"]], ST, r=STk, w=[], chan="sst")
                else:
                    A("dve", lambda h: h.tensor_copy(out=Sb2[:, 0], in_=ST[:]), r=STk, w=Sb2k)
        if kind == "p":
            dma(QM, sp_o[l, segs[0]["q"]], ST, r=STk, w=[], chan="sst")
        S.tag = "B.znorm"
        def cb_z(j, ps, pk, m):
            act(zst[:, 0:T], ps, AF.Silu, r=[pk], w=zstk)
            tt(ybT[:, j, 0:T], ybT[:, j, 0:T], zst[:, 0:T], ALU.mult, r=ybk + zstk, w=ybk)
        linear_fm("w_in", l, 0, KC, OFF_Z, 1024, hb, ["hb"], T, cb_z)
        for g in range(2):
            rms_to(ysq, ysqk, [ybT[:, g * 4 + c, 0:T] for c in range(4)], ybk, on9, T, 4)
            for c in range(4):
                hc = g * 4 + c
                stt(ybT[:, hc, 0:T], ybT[:, hc, 0:T], C("gbn", l, 8)[:, hc:hc + 1], rstd[:, 0:T], ALU.mult, ALU.mult,
                    r=ybk + ["rstd", "cst"], w=ybk)
        drain(merge_branch(l, 1, ybT, ybk, T, False, "B.merge"))

        S.tag = "wout"
        mg, mgk = mgp, ["mg"]

        def cb_o(j, ps, pk, m):
            for sg in segs:
                c0, L, b = sg["c0"], sg["L"], sg["b"]
                stt(xT[:, j, c0:c0 + L], ps[:, c0:c0 + L], modT[:, l, 32 + j, b:b + 1], xT[:, j, c0:c0 + L], ALU.mult, ALU.add,
                    r=[pk, "modT", "xT"], w=["xT"])
        linear_fm("w_out", l, 0, KC, 0, D, mg, mgk, T, cb_o)

        S.tag = "ffn"
        norm_mod(tl, l, A2, 48, hb, "hb")
        ar.reset()
        hid, hidk = ar.get(32 * 512, BF16)
        hid = hid.rearrange("p (c t) -> p c t", c=32)
        rl = [ar.get(512, BF16) for _ in range(2)]
        for half in range(2):
            def cb_up(j, ps, pk, m, half=half):
                r_, rk_ = rl[j % 2]
                act(r_[:, 0:T], ps, AF.Relu, r=[pk], w=rk_)
                tt(hid[:, j, 0:T], r_[:, 0:T], r_[:, 0:T], ALU.mult, r=rk_, w=hidk)
            linear_fm("w_up", l, 0, KC, half * 4096, 4096, hb, ["hb"], T, cb_up)

            def cb_dn(j, ps, pk, m):
                for sg in segs:
                    c0, L, b = sg["c0"], sg["L"], sg["b"]
                    stt(xT[:, j, c0:c0 + L], ps[:, c0:c0 + L], modT[:, l, 80 + j, b:b + 1], xT[:, j, c0:c0 + L], ALU.mult, ALU.add,
                        r=[pk, "modT", "xT"], w=["xT"])
            linear_fm("w_dn", l, half * 4096, 32, 0, D, hid, hidk, T, cb_dn)

    for tl in tiles:
        T = tl["T"]
        if tl["kind"] == "p":
            q = tl["segs"][0]["q"]
            dma(QM, xT[:, :, 0:T], xp[q, :, tl["t0"]:tl["t0"] + T].rearrange("(c p) t -> p c t", p=128), r=[], w=["xT"], chan="xin")
        else:
            dma(QM, xT[:, :, 0:T], xs.rearrange("(c p) t -> p c t", p=128), r=[], w=["xT"], chan="xin")
        for l in range(depth):
            layer(tl, l)
        S.tag = "final"
        ar.reset()
        sq, sqk = ar.get(16 * 512, BF16)
        sq = sq.rearrange("p (c t) -> p c t", c=16)
        rms_to(sq, sqk, [xT[:, c, 0:T] for c in range(16)], ["xT"], on11, T, 16)
        for c in range(16):
            stt(xT[:, c, 0:T], xT[:, c, 0:T], C("fg")[:, c:c + 1], rstd[:, 0:T], ALU.mult, ALU.mult, r=["xT", "rstd", "cst"], w=["xT"])
        if tl["kind"] == "p":
            q = tl["segs"][0]["q"]
            dma(QM, yp[q, :, tl["t0"]:tl["t0"] + T].rearrange("(c p) t -> p c t", p=128), xT[:, :, 0:T], r=["xT"], w=[], chan="xin")
        else:
            dma(QM, ys.rearrange("(c p) t -> p c t", p=128), xT[:, :, 0:T], r=["xT"], w=[], chan="xin")

    S.emit()
    es.close()
    return nc


def _pack_cst(depth, P, lay, ncst):
    c = np.zeros((128, ncst), np.float32)

    def put(name, arr):
        o, n = lay[name]
        c[:, o:o + n] = np.asarray(arr, np.float32).reshape(128, n)

    def fm(v, nch):
        v = np.asarray(v)
        L = v.shape[0]
        return v.reshape(L, nch, 128).transpose(2, 0, 1).reshape(128, L * nch)

    put("ident", np.eye(128))
    put("n1g", fm(P["norm1_g"], 16))
    put("n2g", fm(P["norm2_g"], 16))
    put("fg", fm(P["final_g"][None], 16))
    put("bmod", fm(P["b_mod"], 96))
    put("lbraw", fm(P["hgrn_lb"], 8))
    put("gon", fm(P["hgrn_onorm_g"], 8))
    cw = np.asarray(P["ssm_conv_w"])
    put("convw", cw.reshape(depth, 4, 12, 128).transpose(3, 0, 1, 2).reshape(128, depth * 48))
    put("convb", fm(P["ssm_conv_b"], 12))
    put("dtb", np.broadcast_to(np.asarray(P["ssm_dt_bias"]).reshape(1, depth * 16), (128, depth * 16)))
    put("alog", np.broadcast_to(np.asarray(P["ssm_a_log"]).reshape(1, depth * 16), (128, depth * 16)))
    dp = np.repeat(np.asarray(P["ssm_d"]), 64, axis=1)
    put("Dp", fm(dp, 8))
    put("gbn", fm(P["ssm_onorm_g"], 8))
    p = np.arange(128)[:, None]
    t64 = np.arange(64)[None, :]
    put("maskA", np.tile(((p % 64) <= t64).astype(np.float32)[:, None, :], (1, 8, 1)))
    bm = np.ones(512, np.float32)
    bm[::64] = 0
    put("blkm", np.broadcast_to(bm, (128, 512)))
    t = np.arange(128)[None, :]
    tri_p = (p <= t).astype(np.float32)
    valid = lambda i: (i % 64) < 32
    same = (p // 64) == (t // 64)
    tri_s = (same & valid(p) & valid(t) & ((p % 64) <= (t % 64))).astype(np.float32)
    put("tri_p", tri_p)
    put("tri_s", tri_s)
    put("segb_p", np.ones((128, 128)))
    put("segb_s", (same & valid(p) & valid(t)).astype(np.float32))
    put("sel_s0", np.broadcast_to(((p // 64 == 0) & valid(p)).astype(np.float32), (128, 128)))
    put("sel_s1", np.broadcast_to(((p // 64 == 1) & valid(p)).astype(np.float32), (128, 128)))
    put("neg_p", np.tile(np.where(tri_p > 0, 0.0, NEG), (1, 4)))
    put("neg_s", np.tile(np.where(tri_s > 0, 0.0, NEG), (1, 4)))
    put("triu", tri_p)
    put("triu_s", tri_s)
    return c


def _host_inputs(cfg, inp, core):
    depth, LP, NPS = cfg["depth"], cfg["LP"], cfg["NPS"]
    lay, ncst = cst_layout(depth)
    f = lambda a: np.ascontiguousarray(np.asarray(a, np.float32))
    pb = slice(core * NPS, (core + 1) * NPS)
    sbs = slice(core * 4, (core + 1) * 4)
    m = {}
    m["xp"] = f(np.asarray(inp["x_prompt"])[pb].transpose(0, 2, 1))
    xs = np.zeros((D, 256), np.float32)
    xsl = np.asarray(inp["x_sample"])[sbs]
    for j in range(4):
        xs[:, 64 * j:64 * j + 32] = xsl[j].T
    m["xs"] = xs
    cc = np.concatenate([np.asarray(inp["c_prompt"])[pb], np.asarray(inp["c_sample"])[sbs]], 0)
    m["cT"] = f(cc.reshape(cc.shape[0], 16, 128).transpose(2, 1, 0))
    m["sh_in"] = f(np.asarray(inp["state_hgrn"])[:, sbs])
    m["ss_in"] = f(np.asarray(inp["state_ssm"])[:, sbs].transpose(0, 1, 4, 2, 3))
    sc = np.asarray(inp["state_conv"])[:, sbs]
    m["sc_in"] = f(sc.reshape(depth, 4, 3, 12, 128).transpose(0, 1, 4, 3, 2))
    for k_, n_ in (("w_mod", "w_mod"), ("w_in", "w_in"), ("w_branch", "w_br"), ("w_out", "w_out"), ("w_up", "w_up"), ("w_down", "w_dn")):
        m[n_] = f(inp[k_])
    m["cst"] = _pack_cst(depth, inp, lay, ncst)
    m["lnp"] = f(np.stack([np.asarray(inp["cmlp_ln_g"]), np.asarray(inp["cmlp_ln_b"])], 1))
    m["wsT"] = f(np.asarray(inp["cmlp_ws"]).transpose(0, 3, 1, 2))
    m["bs"] = f(inp["cmlp_bs"])
    return m


def run(cfg, inp, ncores=8):
    import time
    t0 = time.time()
    nc = build(cfg)
    t1 = time.time()
    in_maps = [_host_inputs(cfg, inp, c) for c in range(ncores)]
    t2 = time.time()
    res = run_bass_kernel_spmd(nc, in_maps, core_ids=list(range(ncores)))
    print("[kernel] build %.1fs host-layout %.1fs launch %.1fs" % (t1 - t0, t2 - t1, time.time() - t2), flush=True)
    depth, LP, NPS = cfg["depth"], cfg["LP"], cfg["NPS"]
    R = res.results
    cat = lambda fn, ax=0: np.concatenate([fn(r) for r in R], axis=ax)
    y_p = cat(lambda r: r["yp"].transpose(0, 2, 1))
    y_s = cat(lambda r: np.stack([r["ys"][:, 64 * j:64 * j + 32].T for j in range(4)], 0))
    h_p = cat(lambda r: r["hp"], 1)
    s_p = cat(lambda r: r["sp_o"].transpose(0, 1, 3, 4, 2), 1)
    c_p = cat(lambda r: r["cp"].transpose(0, 1, 4, 3, 2).reshape(depth, NPS, 3, 1536), 1)
    h_s = cat(lambda r: r["hs"], 1)
    s_s = cat(lambda r: r["sso"].transpose(0, 1, 3, 4, 2), 1)
    c_s = cat(lambda r: r["cso"].transpose(0, 1, 4, 3, 2).reshape(depth, 4, 3, 1536), 1)
    v_s = cat(lambda r: np.stack([r["vso"][:, 64 * j:64 * j + 32, :] for j in range(4)], 1), 1)
    outs = (y_p, y_s, h_p, s_p, c_p, h_s, s_s, c_s, v_s)
    return tuple(np.ascontiguousarray(o, dtype=np.float32) for o in outs)


def kernel(**inputs):
    cfg = dict(depth=4, LP=2048, NPS=2)
    return run(cfg, inputs, 8)
```

```python
import numpy as np
from contextlib import ExitStack
import concourse.bass as bass
import concourse.mybir as mybir
from concourse.bass_utils import run_bass_kernel_spmd

F32 = mybir.dt.float32
BF16 = mybir.dt.bfloat16
U32 = mybir.dt.uint32
AF = mybir.ActivationFunctionType
ALU = mybir.AluOpType

D = 2048
KC = 16
A_W = 1024
IN_TOTAL = 14864
OFF_Q, OFF_F, OFF_I, OFF_G, OFF_Z, OFF_XBC, OFF_DT, OFF_U, OFF_V, OFF_GATE = (
    0, 1024, 2048, 3072, 4096, 5120, 6656, 6672, 7696, 8720)
EPS = 1e-6
NEG = -30000.0


class Sched:
    ENGS = ("pe", "act", "dve", "pool", "sp")

    def __init__(self, nc):
        self.nc = nc
        self.ops = []
        self.last_w = {}
        self.readers = {}
        self.chan_last = {}
        self.chans = []
        self.tag = ""
        self.pe_log = None

    def add(self, eng, fn, r=(), w=(), dma=None):
        deps = set()
        for k in r:
            if k in self.last_w:
                deps.add(self.last_w[k])
        for k in w:
            if k in self.last_w:
                deps.add(self.last_w[k])
            deps.update(self.readers.get(k, ()))
        if dma is not None:
            if dma in self.chan_last:
                deps.add(self.chan_last[dma])
            else:
                self.chans.append(dma)
        i = len(self.ops)
        self.ops.append(dict(eng=eng, fn=fn, deps=deps, dma=dma, sig=False, tag=self.tag))
        for k in r:
            self.readers.setdefault(k, []).append(i)
        for k in w:
            self.last_w[k] = i
            self.readers[k] = []
        if dma is not None:
            self.chan_last[dma] = i
        return i

    def emit(self):
        nc = self.nc
        ops = self.ops
        for o in ops:
            for d in o["deps"]:
                ops[d]["sig"] = True
        cnt = {e: 0 for e in self.ENGS}
        ccnt = {c: 0 for c in self.chans}
        for o in ops:
            if o["dma"] is not None:
                ccnt[o["dma"]] += 16
                o["semk"] = ("c", o["dma"])
                o["val"] = ccnt[o["dma"]]
            elif o["sig"]:
                cnt[o["eng"]] += 1
                o["semk"] = ("e", o["eng"])
                o["val"] = cnt[o["eng"]]
        with ExitStack() as es:
            sems = {}
            for e in self.ENGS:
                sems[("e", e)] = es.enter_context(nc.semaphore("s_" + e))
            for c in self.chans:
                sems[("c", c)] = es.enter_context(nc.semaphore("d_" + str(c)))
            block = es.enter_context(nc.Block())
            per = {e: [o for o in ops if o["eng"] == e] for e in self.ENGS}
            final = [(("c", c), ccnt[c]) for c in self.chans]

            def run(e, h):
                waited = {}
                for o in per[e]:
                    need = {}
                    for d in o["deps"]:
                        p = ops[d]
                        k = p["semk"]
                        if p["val"] > need.get(k, 0):
                            need[k] = p["val"]
                    for k, v in need.items():
                        if waited.get(k, 0) < v:
                            h.wait_ge(sems[k], v)
                            waited[k] = v
                    if e == "pe" and self.pe_log is not None:
                        n0 = nc.n_instructions()
                        ins = o["fn"](h)
                        self.pe_log.append((o["tag"], nc.n_instructions() - n0))
                    else:
                        ins = o["fn"](h)
                    if o["dma"] is not None:
                        ins.then_inc(sems[o["semk"]], 16)
                    elif o["sig"]:
                        ins.then_inc(sems[o["semk"]], 1)
                if e == "sp":
                    for k, v in final:
                        if v > 0 and waited.get(k, 0) < v:
                            h.wait_ge(sems[k], v)

            @block.tensor
            def _(h):
                run("pe", h)

            @block.scalar
            def _(h):
                run("act", h)

            @block.vector
            def _(h):
                run("dve", h)

            @block.gpsimd
            def _(h):
                run("pool", h)

            @block.sync
            def _(h):
                run("sp", h)


def cst_layout(depth):
    items = [("ident", 128), ("n1g", depth * 16), ("n2g", depth * 16), ("fg", 16), ("bmod", depth * 96),
             ("lbraw", depth * 8), ("gon", depth * 8), ("convw", depth * 48), ("convb", depth * 12),
             ("dtb", depth * 16), ("alog", depth * 16), ("Dp", depth * 8), ("gbn", depth * 8),
             ("maskA", 8 * 64), ("blkm", 512),
             ("tri_p", 128), ("tri_s", 128), ("segb_p", 128), ("segb_s", 128),
             ("sel_s0", 128), ("sel_s1", 128), ("neg_p", 512), ("neg_s", 512), ("triu", 128), ("triu_s", 128)]
    lay = {}
    o = 0
    for n, c in items:
        lay[n] = (o, c)
        o += c
    return lay, o


def build(cfg):
    depth, LP, NPS = cfg["depth"], cfg["LP"], cfg["NPS"]
    NSS = 4
    TP = min(512, LP)
    lay, NCST = cst_layout(depth)
    nc = bass.Bass("TRN2", target_bir_lowering=False)

    def din(n, s, dt=F32):
        return nc.dram_tensor(n, list(s), dt, kind="ExternalInput").ap()

    def dout(n, s):
        return nc.dram_tensor(n, list(s), F32, kind="ExternalOutput").ap()

    xp = din("xp", [NPS, D, LP])
    xs = din("xs", [D, 256])
    cT = din("cT", [128, 16, NPS + 4])
    sh_in = din("sh_in", [depth, NSS, 8, 128, 128])
    ss_in = din("ss_in", [depth, NSS, 128, 16, 64])
    sc_in = din("sc_in", [depth, NSS, 128, 12, 3])
    w_mod = din("w_mod", [depth, D, 6 * D])
    w_in = din("w_in", [depth, D, IN_TOTAL])
    w_br = din("w_br", [depth, 3072, D])
    w_out = din("w_out", [depth, D, D])
    w_up = din("w_up", [depth, D, 4 * D])
    w_dn = din("w_dn", [depth, 4 * D, D])
    cst_d = din("cst", [128, NCST])
    lnp_d = din("lnp", [depth, 2, 1024])
    wsT_d = din("wsT", [depth, 128, 4, 128])
    bs_d = din("bs", [depth, 4, 128])

    yp = dout("yp", [NPS, D, LP])
    ys = dout("ys", [D, 256])
    hp = dout("hp", [depth, NPS, 8, 128, 128])
    sp_o = dout("sp_o", [depth, NPS, 128, 16, 64])
    cp = dout("cp", [depth, NPS, 128, 12, 3])
    hs = dout("hs", [depth, NSS, 8, 128, 128])
    sso = dout("sso", [depth, NSS, 128, 16, 64])
    cso = dout("cso", [depth, NSS, 128, 12, 3])
    vso = dout("vso", [depth, 256, 1024])

    WSPEC = {"w_in": (w_in, D, IN_TOTAL), "w_br": (w_br, 3072, D), "w_out": (w_out, D, D),
             "w_up": (w_up, D, 4 * D), "w_dn": (w_dn, 4 * D, D)}
    scr = {n_: nc.dram_tensor("scr_" + n_, [depth, k_, c_], BF16, kind="Internal").ap() for n_, (_, k_, c_) in WSPEC.items()}
    scr_keys = {}

    es = ExitStack()
    _n = [0]

    def sb(shape, dt, name=None):
        _n[0] += 1
        return es.enter_context(nc.sbuf_tensor(name or ("t%d" % _n[0]), list(shape), dt))

    S = Sched(nc)
    A = S.add
    QM = "pool"

    cst = sb([128, NCST], F32, "cst_sb")
    xT = sb([128, 16, 512], F32, "xT")
    hb = sb([128, 16, 512], BF16, "hb")
    NW = 4
    WEL = 4096
    W = [sb([128, WEL], BF16, "w%d" % i) for i in range(NW)]
    NSEQ = NPS + NSS
    mgp = sb([128, 16, 512], BF16, "mg")
    modT = sb([128, depth, 96, NSEQ], F32, "modT")
    A1 = sb([128, depth, 16, NSEQ], F32, "A1")
    A2 = sb([128, depth, 16, NSEQ], F32, "A2")
    lb = sb([128, depth, 8], F32, "lb")
    oml = sb([128, depth, 8], F32, "oml")
    aneg = sb([128, depth, 16], F32, "aneg")
    idb = sb([128, 128], BF16, "idb")
    on11 = sb([128, 128], BF16, "on11")
    on7 = sb([128, 128], BF16, "on7")
    on9 = sb([128, 128], BF16, "on9")
    onf = sb([128, 128], F32, "onf")
    onrow = sb([1, 128], BF16, "onrow")
    WT = sb([128, 4, 128], BF16, "WT")
    brow = sb([1, 4, 128], BF16, "brow")
    convst = sb([128, depth, 12, 3], F32, "convst")
    csb = sb([128, 16, NSEQ], BF16, "csb")
    rstd = sb([128, 512], F32, "rstd")
    ftmp = [sb([128, 512], F32, "ftmp%d" % i) for i in range(2)]
    PS = [es.enter_context(nc.psum_tensor("ps%d" % i, [128, 512], F32)) for i in range(8)]
    ARN = 34 * 1024
    AR = sb([128, ARN], BF16, "arena")

    def C(name, l=None, n=None):
        o, c = lay[name]
        if l is None:
            return cst[:, o:o + c]
        return cst[:, o + l * n:o + (l + 1) * n]

    class Arena:
        def __init__(self):
            self.o = 0

        def reset(self):
            self.o = 0

        def get(self, nel, dt):
            nb = nel * (2 if dt == F32 else 1)
            nb = (nb + 15) // 16 * 16
            a = self.o
            self.o += nb
            assert self.o <= AR_TOP, ("arena overflow", self.o)
            ap = AR[:, a:a + nb]
            if dt == F32:
                ap = ap.bitcast(F32)
            ap = ap[:, 0:nel]
            keys = [("ar", g) for g in range(a // 1024, (a + nb - 1) // 1024 + 1)]
            return ap, keys

    ar = Arena()
    AR_TOP = ARN - 1536
    SGT = AR[:, AR_TOP:AR_TOP + 512]
    SGK = [("ar", g) for g in range(AR_TOP // 1024, (AR_TOP + 511) // 1024 + 1)]
    MT = AR[:, AR_TOP + 512:ARN].bitcast(F32)
    MTK = [("ar", g) for g in range((AR_TOP + 512) // 1024, (ARN - 1) // 1024 + 1)]

    def drain(gen, n=None):
        if gen is None:
            return
        i = 0
        while n is None or i < n:
            try:
                next(gen)
            except StopIteration:
                return
            i += 1
    _ps = [0]

    def psum():
        i = _ps[0] % 8
        _ps[0] += 1
        return PS[i], ("ps", i)

    _wi = [0]

    def wslot():
        i = _wi[0] % NW
        _wi[0] += 1
        return W[i], ("w", i), "wd%d" % i

    _ft = [0]

    def ft():
        i = _ft[0] % 2
        _ft[0] += 1
        return ftmp[i], ("ft", i)

    def act(out, in_, func, r, w, bias=None, scale=None):
        kw = {}
        if bias is not None:
            kw["bias"] = bias
        if scale is not None:
            kw["scale"] = scale
        return A("act", lambda h: h.activation(out=out, in_=in_, func=func, **kw), r=r, w=w)

    def tt(out, in0, in1, op, r, w, eng="dve"):
        return A(eng, lambda h: h.tensor_tensor(out=out, in0=in0, in1=in1, op=op), r=r, w=w)

    def ts(out, in0, s1, s2, op0, op1, r, w):
        if op1 is None:
            return A("dve", lambda h: h.tensor_scalar(out=out, in0=in0, scalar1=s1, scalar2=None, op0=op0), r=r, w=w)
        return A("dve", lambda h: h.tensor_scalar(out=out, in0=in0, scalar1=s1, scalar2=s2, op0=op0, op1=op1), r=r, w=w)

    def stt(out, in0, sc, in1, op0, op1, r, w):
        return A("dve", lambda h: h.scalar_tensor_tensor(out=out, in0=in0, scalar=sc, in1=in1, op0=op0, op1=op1), r=r, w=w)

    def dma(eng, out, in_, r, w, chan):
        return A(eng, lambda h: h.dma_start(out=out, in_=in_), r=r, w=w, dma=chan)

    def mms(ps_ap, pairs, r, w):
        def f(h):
            ins = None
            n = len(pairs)
            for i, (l_, r_) in enumerate(pairs):
                ins = h.matmul(ps_ap, lhsT=l_, rhs=r_, start=(i == 0), stop=(i == n - 1))
            return ins
        return A("pe", f, r=r, w=w)

    def load_w(src, kc, ncols, eng="sp"):
        src_ap, rkeys = src
        wt, wk, ch = wslot()
        v = wt[:, 0:kc * ncols].rearrange("p (k n) -> p k n", k=kc)
        dma(eng, v, src_ap, r=list(rkeys), w=[wk], chan=ch)
        return v, wk

    def wsrc(name, l, row0, kc, col0, ncols):
        return (scr[name][l, row0:row0 + kc * 128, col0:col0 + ncols].rearrange("(k p) n -> p k n", p=128),
                scr_keys[(name, l)])

    def linear_fm(name, l, row0, kc, col0, ncols, xin, xkeys, T, cb, gc=None, after_group=None):
        gc = gc or (WEL // kc)
        for g0 in range(0, ncols, gc):
            g = min(gc, ncols - g0)
            v, wk = load_w(wsrc(name, l, row0, kc, col0 + g0, g), kc, g)
            for j in range(0, g, 128):
                m = min(128, g - j)
                ps, pk = psum()
                mms(ps[0:m, 0:T], [(v[:, k, j:j + m], xin[:, k, 0:T]) for k in range(kc)],
                    r=[wk] + list(xkeys), w=[pk])
                cb((g0 + j) // 128, ps[0:m, 0:T], pk, m)
            if after_group is not None:
                after_group()

    def linear_tm(name, l, col0, ncols, xin, xkeys, T, cb, after_group=None):
        for g0 in range(0, ncols, 256):
            g = min(256, ncols - g0)
            v, wk = load_w(wsrc(name, l, 0, KC, col0 + g0, g), KC, g)
            for tc in range(T // 128):
                ps, pk = psum()
                mms(ps[:, 0:g], [(xin[:, k, tc * 128:(tc + 1) * 128], v[:, k, 0:g]) for k in range(KC)],
                    r=[wk] + list(xkeys), w=[pk])
                cb(tc, g0, g, ps[:, 0:g], pk)
            if after_group is not None:
                after_group()

    S.tag = "prologue"
    if cfg.get("pe_log") is not None:
        S.pe_log = cfg["pe_log"]
    dma("sp", cst[:], cst_d, r=[], w=["cst"], chan="cst")
    ctmp = sb([128, 16, NSEQ], F32, "ctmp")
    dma("sp", ctmp[:], cT, r=[], w=["ctmp"], chan="ct")
    act(csb[:], ctmp[:], AF.Silu, r=["ctmp"], w=["csb"])
    A("dve", lambda h: h.tensor_copy(out=idb[:], in_=C("ident")), r=["cst"], w=["idb"])
    A("dve", lambda h: h.memset(on11[:], 1.0 / 2048), w=["on11"])
    A("dve", lambda h: h.memset(on7[:], 1.0 / 128), w=["on7"])
    A("dve", lambda h: h.memset(on9[:], 1.0 / 512), w=["on9"])
    A("dve", lambda h: h.memset(onf[:], 1.0), w=["onf"])
    A("dve", lambda h: h.memset(onrow[:], 1.0), w=["onrow"])
    A("dve", lambda h: h.memset(convst[:], 0.0), w=["convst"])
    lbe = sb([128, depth, 8], F32, "lbe")
    lbs_ = sb([128, 8], F32, "lbs")
    act(lbe[:], C("lbraw").rearrange("p (l h) -> p l h", l=depth), AF.Exp, r=["cst"], w=["lbe"])
    A("dve", lambda h: h.tensor_copy(out=lbs_[:], in_=lbe[:, 0, :]), r=["lbe"], w=["lbs"])
    for l in range(1, depth):
        tt(lbs_[:], lbs_[:], lbe[:, l, :], ALU.add, r=["lbe", "lbs"], w=["lbs"])
    A("dve", lambda h: h.reciprocal(out=lbs_[:], in_=lbs_[:]), r=["lbs"], w=["lbs"])
    A("dve", lambda h: h.memset(lb[:], 0.0), w=["lb"])
    for l in range(1, depth):
        tt(lbe[:, l, :], lbe[:, l, :], lbs_[:], ALU.mult, r=["lbe", "lbs"], w=["lbe"])
        tt(lb[:, l, :], lb[:, l - 1, :], lbe[:, l, :], ALU.add, r=["lbe", "lb"], w=["lb"])
    ts(oml[:], lb[:], -1.0, 1.0, ALU.mult, ALU.add, r=["lb"], w=["oml"])
    act(aneg[:], C("alog").rearrange("p (l h) -> p l h", l=depth), AF.Exp, r=["cst"], w=["aneg"])
    ts(aneg[:], aneg[:], -1.0, None, ALU.mult, None, r=["aneg"], w=["aneg"])

    _cv = [0]

    def convert(n_, l):
        wd_, k_, c_ = WSPEC[n_]
        keys = []
        for r0 in range(0, k_, 512):
            key = ("scr", n_, l, r0)
            keys.append(key)
            ch = "cv%d" % (_cv[0] % 8)
            _cv[0] += 1
            dma("pool", scr[n_][l, r0:r0 + 512, :], wd_[l, r0:r0 + 512, :], r=[], w=[key], chan=ch)
        scr_keys[(n_, l)] = keys
    convert("w_in", 0)
    for l in range(depth):
        for g0 in range(0, 6 * D, 256):
            v, wk = load_w((w_mod[l, :, g0:g0 + 256].rearrange("(k p) n -> p k n", p=128), []), KC, 256, eng="pool")
            ps, pk = psum()
            for j in range(2):
                mms(ps[:, j * NSEQ:(j + 1) * NSEQ], [(v[:, k, j * 128:(j + 1) * 128], csb[:, k, :]) for k in range(KC)],
                    r=[wk, "csb"], w=[pk])
            c0 = g0 // 128
            tt(modT[:, l, c0:c0 + 2, :], ps[:, 0:2 * NSEQ].rearrange("p (j b) -> p j b", j=2),
               C("bmod", l, 96)[:, c0:c0 + 2].unsqueeze(2).broadcast_to([128, 2, NSEQ]), ALU.add,
               r=[pk, "cst"], w=["modT"])
        stt(A1[:, l, :, :], modT[:, l, 16:32, :], 1.0, C("n1g", l, 16).unsqueeze(2).broadcast_to([128, 16, NSEQ]),
            ALU.add, ALU.mult, r=["modT", "cst"], w=["A1"])
        stt(A2[:, l, :, :], modT[:, l, 64:80, :], 1.0, C("n2g", l, 16).unsqueeze(2).broadcast_to([128, 16, NSEQ]),
            ALU.add, ALU.mult, r=["modT", "cst"], w=["A2"])
    for l in range(depth):
        for n_ in WSPEC:
            if (n_, l) != ("w_in", 0):
                convert(n_, l)

    tiles = []
    for p in range(NPS):
        for t0 in range(0, LP, TP):
            tiles.append(dict(kind="p", T=TP, Cv=64, t0=t0, first=(t0 == 0), last=(t0 + TP >= LP),
                              segs=[dict(b=p, q=p, c0=0, L=TP)]))
    tiles.append(dict(kind="s", T=256, Cv=32, t0=0, first=True, last=True,
                      segs=[dict(b=NPS + j, q=j, c0=64 * j, L=32) for j in range(NSS)]))

    def rms_to(sq_ap, sq_keys, src_chunks, src_keys, ones, T, nch):
        ps, pk = psum()
        for c in range(nch):
            if c % 2 == 0:
                act(sq_ap[:, c, 0:T], src_chunks[c], AF.Square, r=src_keys, w=sq_keys)
            else:
                tt(sq_ap[:, c, 0:T], src_chunks[c], src_chunks[c], ALU.mult, r=src_keys, w=sq_keys)
        mms(ps[:, 0:T], [(ones[:], sq_ap[:, c, 0:T]) for c in range(nch)], r=sq_keys + ["ones"], w=[pk])
        act(rstd[:, 0:T], ps[:, 0:T], AF.Sqrt, r=[pk], w=["rstd"], bias=EPS)
        A("dve", lambda h: h.reciprocal(out=rstd[:, 0:T], in_=rstd[:, 0:T]), r=["rstd"], w=["rstd"])

    def norm_mod(tl, l, Atab, shoff, out_b, out_k):
        T = tl["T"]
        ar.reset()
        sq, sqk = ar.get(16 * 512, BF16)
        sq = sq.rearrange("p (c t) -> p c t", c=16)
        rms_to(sq, sqk, [xT[:, c, 0:T] for c in range(16)], ["xT"], on11, T, 16)
        for sg in tl["segs"]:
            c0, L, b = sg["c0"], sg["L"], sg["b"]
            for c in range(16):
                t_, tk = ft()
                tt(t_[:, 0:L], xT[:, c, c0:c0 + L], rstd[:, c0:c0 + L], ALU.mult, r=["xT", "rstd"], w=[tk])
                act(out_b[:, c, c0:c0 + L], t_[:, 0:L], AF.Identity, r=[tk, "A1", "A2", "modT"], w=[out_k],
                    bias=modT[:, l, shoff + c, b:b + 1], scale=Atab[:, l, c, b:b + 1])

    def merge_branch(l, k, ykT, ykeys, T, first, tag):
        sgt, sgk, mt, mtk = SGT, SGK, MT, MTK
        for g0 in range(0, D, 256):
            t_ = S.tag
            S.tag = tag
            vg_, wkg = load_w(wsrc("w_in", l, 0, KC, OFF_GATE + k * D + g0, 256), KC, 256)
            vb, wkb = load_w(wsrc("w_br", l, k * 1024, 8, g0, 256), 8, 256)
            S.tag = t_
            for jj in range(2):
                t_ = S.tag
                S.tag = tag
                j = g0 // 128 + jj
                ps, pk = psum()
                mms(ps[:, 0:T], [(vg_[:, kk_, jj * 128:(jj + 1) * 128], hb[:, kk_, 0:T]) for kk_ in range(KC)], r=[wkg, "hb"], w=[pk])
                act(sgt[:, 0:T], ps[:, 0:T], AF.Sigmoid, r=[pk], w=sgk)
                ps2, pk2 = psum()
                mms(ps2[:, 0:T], [(vb[:, kk_, jj * 128:(jj + 1) * 128], ykT[:, kk_, 0:T]) for kk_ in range(8)], r=[wkb] + list(ykeys), w=[pk2])
                if first:
                    tt(mgp[:, j, 0:T], sgt[:, 0:T], ps2[:, 0:T], ALU.mult, r=sgk + [pk2], w=["mg"])
                else:
                    tt(mt[:, 0:T], sgt[:, 0:T], ps2[:, 0:T], ALU.mult, r=sgk + [pk2], w=mtk)
                    tt(mgp[:, j, 0:T], mgp[:, j, 0:T], mt[:, 0:T], ALU.add, r=mtk + ["mg"], w=["mg"])
                S.tag = t_
            yield

    def layer(tl, l):
        T, Cv, kind = tl["T"], tl["Cv"], tl["kind"]
        NT = T // 128
        segs = tl["segs"]
        nseg = len(segs)
        S.tag = "norm1"
        norm_mod(tl, l, A1, 0, hb, "hb")
        ar.reset()

        S.tag = "C.proj"
        uT, uk = ar.get(8 * 512, BF16)
        uT = uT.rearrange("p (c t) -> p c t", c=8)
        vtok, vk = ar.get(NT * 1024, BF16)
        vtok = vtok.rearrange("p (n c) -> p n c", n=NT)
        vg, vgk = ar.get(1024, F32)
        lnp, lnk = ar.get(2048, F32)
        st6, st6k = ar.get(32, F32)
        wst, wstk = ar.get(512, F32)
        bst, bstk = ar.get(512, F32)
        wsv = wst.rearrange("p (g t) -> p g t", g=4)
        if kind == "p":
            dma(QM, wsv, wsT_d[l], r=[], w=wstk, chan="misc")
            dma(QM, bst[0:1, :].rearrange("p (g t) -> p g t", g=4), bs_d[l].unsqueeze(0), r=[], w=bstk, chan="misc")
            tt(WT[:], wsv, C("triu").unsqueeze(1).broadcast_to([128, 4, 128]), ALU.mult, r=wstk + ["cst"], w=["WT"])
        else:
            A("dve", lambda h: h.memset(wst[:], 0.0), w=wstk)
            A("dve", lambda h: h.memset(bst[0:1, :], 0.0), w=bstk)
            for pb_ in (0, 64):
                dma(QM, wsv[pb_:pb_ + 32, :, pb_:pb_ + 32], wsT_d[l, 0:32, :, 0:32], r=[], w=wstk, chan="misc")
                dma(QM, bst[0:1, :].rearrange("p (g t) -> p g t", g=4)[:, :, pb_:pb_ + 32], bs_d[l, :, 0:32].unsqueeze(0), r=[], w=bstk, chan="misc")
            tt(WT[:], wsv, C("triu_s").unsqueeze(1).broadcast_to([128, 4, 128]), ALU.mult, r=wstk + ["cst"], w=["WT"])
        A("dve", lambda h: h.tensor_copy(out=brow[:], in_=bst[0:1, :].rearrange("p (g t) -> p g t", g=4)), r=bstk, w=["brow"])
        dma(QM, lnp.rearrange("p (a c) -> p a c", a=2), lnp_d[l:l + 1].broadcast_to([128, 2, 1024]) if False else
            lnp_d[l].unsqueeze(0).broadcast_to([128, 2, 1024]), r=[], w=lnk, chan="misc")

        def cb_u(j, ps, pk, m):
            act(uT[:, j, 0:T], ps, AF.Gelu, r=[pk], w=uk)
        linear_fm("w_in", l, 0, KC, OFF_U, 1024, hb, ["hb"], T, cb_u)

        def cb_v(tc, g0, g, ps, pk):
            act(vg[:, g0:g0 + g], ps, AF.Gelu, r=[pk], w=vgk)
            A("dve", lambda h: h.bn_stats(out=st6[:, (g0 // 256) * 6:(g0 // 256) * 6 + 6], in_=vg[:, g0:g0 + g]), r=vgk, w=st6k)
            if g0 + g == 1024:
                A("dve", lambda h: h.bn_aggr(out=st6[:, 24:26], in_=st6[:, 0:24]), r=st6k, w=st6k)
                act(st6[:, 26:27], st6[:, 25:26], AF.Sqrt, r=st6k, w=st6k, bias=EPS)
                A("dve", lambda h: h.reciprocal(out=st6[:, 26:27], in_=st6[:, 26:27]), r=st6k, w=st6k)
                ts(vg[:, :], vg[:, :], st6[:, 24:25], st6[:, 26:27], ALU.subtract, ALU.mult, r=vgk + st6k, w=vgk)
                tt(vg[:, :], vg[:, :], lnp[:, 0:1024], ALU.mult, r=vgk + lnk, w=vgk)
                tt(vg[:, :], vg[:, :], lnp[:, 1024:2048], ALU.add, r=vgk + lnk, w=vgk)
                A("dve", lambda h: h.tensor_copy(out=vtok[:, tc, :], in_=vg[:, :]), r=vgk, w=vk)
                if kind == "s":
                    dma(QM, vso[l, tc * 128:(tc + 1) * 128, :], vg[:, :], r=vgk, w=[], chan="vso")
        vw = []
        for g0 in (0, 256, 512, 768):
            vw.append(load_w(wsrc("w_in", l, 0, KC, OFF_V + g0, 256), KC, 256))
        for tc in range(NT):
            for gi, g0 in enumerate((0, 256, 512, 768)):
                v_, wk_ = vw[gi]
                ps, pk = psum()
                mms(ps[:, 0:256], [(hb[:, k, tc * 128:(tc + 1) * 128], v_[:, k, 0:256]) for k in range(KC)],
                    r=[wk_, "hb"], w=[pk])
                cb_v(tc, g0, 256, ps[:, 0:256], pk)
        S.tag = "C.mix"
        for ch in range(8):
            g = ch // 2
            ps, pk = psum()
            for tc in range(NT):
                mms(ps[:, tc * 128:(tc + 1) * 128],
                    [(vtok[:, tc, ch * 128:(ch + 1) * 128], WT[:, g, :]),
                     (onrow[0:1, :], brow[0:1, g, :])],
                    r=vk + ["WT", "brow", "onrow"], w=[pk])
            tt(uT[:, ch, 0:T], uT[:, ch, 0:T], ps[:, 0:T], ALU.mult, r=uk + [pk], w=uk)
        cmg = merge_branch(l, 2, uT, uk, T, True, "C.merge")

        S.tag = "A.i"
        ar.reset()
        oT, ok_ = ar.get(8 * 512, BF16)
        qt, qk = ar.get(8 * 512, BF16)
        kt, kk = ar.get(8 * 512, BF16)
        khat, khk = ar.get(NT * 1024, BF16)
        vat, vak = ar.get(NT * 1024, BF16)
        gs, gsk = qt, qk
        qt = qt.rearrange("p (c t) -> p c t", c=8)
        kt = kt.rearrange("p (c t) -> p c t", c=8)
        gs = gs.rearrange("p (c t) -> p c t", c=8)
        oT = oT.rearrange("p (c t) -> p c t", c=8)
        khat = khat.rearrange("p (n c) -> p n c", n=NT)
        vat = vat.rearrange("p (n c) -> p n c", n=NT)
        NB = T // 64
        dec, deck = ar.get(8 * 8, F32)
        em, emk = ar.get(8 * 8, F32)
        dec = dec.rearrange("p (h n) -> p h n", h=8)
        em = em.rearrange("p (h n) -> p h n", h=8)
        f1, f1k = ar.get(512, F32)
        f2, f2k = ar.get(512, F32)
        f3, f3k = ar.get(512, F32)
        f4, f4k = ar.get(512, F32)
        f5, f5k = ar.get(512, F32)
        khTs = [ar.get(512, BF16) for _ in range(2)]
        attb, attk = ar.get(8 * 64, BF16)
        attb = attb.rearrange("p (h t) -> p h t", h=8)
        Sst, Sk = ar.get(1024, F32)
        Sbf, Sbk = ar.get(1024, BF16)
        Sst = Sst.rearrange("p (h v) -> p h v", h=8)
        Sbf = Sbf.rearrange("p (h v) -> p h v", h=8)
        osq, osqk = ar.get(512, BF16)

        def cb_i(tc, g0, g, ps, pk):
            act(vat[:, tc, g0:g0 + g], ps, AF.Copy, r=[pk], w=vak)
        linear_tm("w_in", l, OFF_I, 1024, hb, ["hb"], T, cb_i, after_group=lambda: drain(cmg, 1))

        S.tag = "A.qf"
        mid, last = Cv // 2 - 1, Cv - 1

        def qf_tail(hd, khT, khTk):
            pst, ptk = psum()
            pstb = pst[:].bitcast(BF16)

            def trf(h):
                ins = None
                for tc in range(NT):
                    ins = h.transpose(out=pstb[:, tc * 128:(tc + 1) * 128], in_=khT[:, tc * 128:(tc + 1) * 128], identity=idb[:])
                return ins
            A("pe", trf, r=khTk + ["idb"], w=[ptk])
            act(khat[:, :, hd * 128:(hd + 1) * 128], pstb[:, 0:NT * 128].rearrange("p (n c) -> p n c", n=NT), AF.Copy,
                r=[ptk], w=khk)
        pend = None
        for hd in range(8):
            wt, wk, ch_ = wslot()
            v = wt[:, 0:KC * 256].rearrange("p (two k n) -> p two k n", k=KC, two=2)
            sq_, sqk_ = wsrc("w_in", l, 0, KC, OFF_Q + hd * 128, 128)
            sf_, _ = wsrc("w_in", l, 0, KC, OFF_F + hd * 128, 128)
            dma("sp", v[:, 0], sq_, r=list(sqk_), w=[wk], chan=ch_)
            dma("sp", v[:, 1], sf_, r=list(sqk_), w=[wk], chan=ch_)
            psq, pqk = psum()
            mms(psq[:, 0:T], [(v[:, 0, k, :], hb[:, k, 0:T]) for k in range(KC)], r=[wk, "hb"], w=[pqk])
            psf, pfk = psum()
            mms(psf[:, 0:T], [(v[:, 1, k, :], hb[:, k, 0:T]) for k in range(KC)], r=[wk, "hb"], w=[pfk])
            act(f1[:, 0:T], psq[:, 0:T], AF.Silu, r=[pqk], w=f1k)
            act(f2[:, 0:T], psf[:, 0:T], AF.Sigmoid, r=[pfk], w=f2k)
            ts(f2[:, 0:T], f2[:, 0:T], oml[:, l, hd:hd + 1], lb[:, l, hd:hd + 1], ALU.mult, ALU.add,
               r=f2k + ["oml", "lb"], w=f2k)
            act(f3[:, 0:T], f2[:, 0:T], AF.Ln, r=f2k, w=f3k)
            ts(f2[:, 0:T], f2[:, 0:T], -1.0, 1.0, ALU.mult, ALU.add, r=f2k, w=f2k)
            A("dve", lambda h: h.tensor_tensor_scan(out=f4[:, 0:T], data0=C("blkm")[:, 0:T], data1=f3[:, 0:T],
                                                     initial=0.0, op0=ALU.mult, op1=ALU.add),
              r=f3k + ["cst"], w=f4k)
            b3 = f4[:, 0:T].rearrange("p (n c) -> p n c", c=64)
            d3 = f3[:, 0:T].rearrange("p (n c) -> p n c", c=64)
            tt(d3, b3, b3[:, :, mid:mid + 1].broadcast_to([128, NB, 64]), ALU.subtract, r=f4k, w=f3k)
            act(dec[:, hd, 0:NB], b3[:, :, last], AF.Exp, r=f4k, w=deck)
            act(em[:, hd, 0:NB], b3[:, :, mid], AF.Exp, r=f4k, w=emk)
            act(f4[:, 0:T], f3[:, 0:T], AF.Exp, r=f3k, w=f4k)
            act(f5[:, 0:T], f3[:, 0:T], AF.Exp, r=f3k, w=f5k, scale=-1.0)
            tt(qt[:, hd, 0:T], f1[:, 0:T], f4[:, 0:T], ALU.mult, r=f1k + f4k, w=qk)
            tt(kt[:, hd, 0:T], f2[:, 0:T], f5[:, 0:T], ALU.mult, r=f2k + f5k, w=kk)
            e13 = f4[:, 0:T].rearrange("p (n c) -> p n c", c=64)
            khT, khTk = khTs[hd % 2]
            tt(khT[:, 0:T].rearrange("p (n c) -> p n c", c=64), kt[:, hd, 0:T].rearrange("p (n c) -> p n c", c=64),
               e13[:, :, last:last + 1].broadcast_to([128, NB, 64]), ALU.mult, r=kk + f4k, w=khTk)
            if pend is not None:
                qf_tail(*pend)
            pend = (hd, khT, khTk)
            if hd % 2 == 1:
                drain(cmg, 1)
        qf_tail(*pend)
        drain(cmg)

        S.tag = "A.rec"
        hst_in = None
        for sg in segs:
            c0, L, q = sg["c0"], sg["L"], sg["q"]
            Sv = Sst.rearrange("p h v -> p h v")
            if kind == "p":
                if tl["first"]:
                    A("dve", lambda h: h.memset(Sst[:], 0.0), w=Sk)
                else:
                    dma(QM, Sst, hp[l, q].rearrange("h k v -> k h v"), r=[], w=Sk, chan="hst")
            else:
                dma(QM, Sst, sh_in[l, q].rearrange("h k v -> k h v"), r=[], w=Sk, chan="hst")
            for bi in range(L // Cv):
                col = c0 + bi * Cv
                tc, pb, nb = col // 128, col % 128, col // 64
                psa, pak = psum()

                def attf(h, psa=psa, col=col, pb=pb):
                    ins = None
                    for hd in range(8):
                        ins = h.matmul(psa[pb:pb + Cv, hd * Cv:(hd + 1) * Cv], lhsT=kt[:, hd, col:col + Cv],
                                       rhs=qt[:, hd, col:col + Cv], start=True, stop=True)
                    return ins
                A("pe", attf, r=qk + kk, w=[pak])
                A("dve", lambda h, pb=pb: h.memset(attb[pb:pb + Cv, :, 0:Cv], 0.0), w=attk)
                A("dve", lambda h, pb=pb, psa=psa: h.copy_predicated(
                    out=attb[pb:pb + Cv, :, 0:Cv],
                    mask=C("maskA").bitcast(U32).rearrange("p (h t) -> p h t", h=8)[pb:pb + Cv, :, 0:Cv],
                    data=psa[pb:pb + Cv, 0:8 * Cv].rearrange("p (h t) -> p h t", h=8)),
                  r=[pak, "cst"] + attk, w=attk)
                tt(Sbf[:], Sst[:], em[:, :, nb:nb + 1].broadcast_to([128, 8, 128]), ALU.mult, r=Sk + emk, w=Sbk)
                pso, pok = psum()

                def of(h, pso=pso, col=col, pb=pb, tc=tc):
                    ins = None
                    for hd in range(8):
                        h.matmul(pso[:, hd * Cv:(hd + 1) * Cv], lhsT=Sbf[:, hd, :], rhs=qt[:, hd, col:col + Cv],
                                 start=True, stop=False)
                        ins = h.matmul(pso[:, hd * Cv:(hd + 1) * Cv], lhsT=vat[pb:pb + Cv, tc, hd * 128:(hd + 1) * 128],
                                       rhs=attb[pb:pb + Cv, hd, 0:Cv], start=False, stop=True)
                    return ins
                A("pe", of, r=Sbk + qk + vak + attk, w=[pok])
                act(oT[:, :, col:col + Cv], pso[:, 0:8 * Cv].rearrange("p (h t) -> p h t", h=8), AF.Copy, r=[pok], w=ok_)
                for half in range(2):
                    pss, psk = psum()

                    def sf(h, pss=pss, half=half, pb=pb, tc=tc):
                        ins = None
                        for hh in range(4):
                            hd = half * 4 + hh
                            ins = h.matmul(pss[:, hh * 128:(hh + 1) * 128], lhsT=khat[pb:pb + Cv, tc, hd * 128:(hd + 1) * 128],
                                           rhs=vat[pb:pb + Cv, tc, hd * 128:(hd + 1) * 128], start=True, stop=True)
                        return ins
                    A("pe", sf, r=khk + vak, w=[psk])
                    for hh in range(4):
                        hd = half * 4 + hh
                        stt(Sst[:, hd, :], Sst[:, hd, :], dec[:, hd, nb:nb + 1], pss[:, hh * 128:(hh + 1) * 128],
                            ALU.mult, ALU.add, r=Sk + deck + [psk], w=Sk)
            dst = (hp if kind == "p" else hs)[l, q].rearrange("h k v -> k h v")
            dma(QM, dst, Sst, r=Sk, w=[], chan="hst")
        S.tag = "A.gnorm"
        def cb_g(j, ps, pk, m):
            act(gs[:, j, 0:T], ps, AF.Silu, r=[pk], w=gsk)
        linear_fm("w_in", l, 0, KC, OFF_G, 1024, hb, ["hb"], T, cb_g)
        for hd in range(8):
            act(osq[:, 0:T], oT[:, hd, 0:T], AF.Square, r=ok_, w=osqk)
            ps, pk = psum()
            mms(ps[:, 0:T], [(on7[:], osq[:, 0:T])], r=osqk + ["ones"], w=[pk])
            act(f1[:, 0:T], ps[:, 0:T], AF.Sqrt, r=[pk], w=f1k, bias=EPS)
            A("dve", lambda h: h.reciprocal(out=f1[:, 0:T], in_=f1[:, 0:T]), r=f1k, w=f1k)
            tt(f2[:, 0:T], oT[:, hd, 0:T], f1[:, 0:T], ALU.mult, r=ok_ + f1k, w=f2k)
            stt(oT[:, hd, 0:T], f2[:, 0:T], C("gon", l, 8)[:, hd:hd + 1], gs[:, hd, 0:T], ALU.mult, ALU.mult,
                r=f2k + gsk + ["cst"], w=ok_)
        amg = merge_branch(l, 0, oT, ok_, T, False, "A.merge")

        S.tag = "B.xbc"
        ar.reset()
        ybT, ybk = ar.get(8 * 512, BF16)
        xbc, xbk = ar.get(12 * 512, BF16)
        xst, xsk = ar.get(NT * 1024, BF16)
        btk_, btkk = ar.get(NT * 256, BF16)
        xbc = xbc.rearrange("p (c t) -> p c t", c=12)
        ybT = ybT.rearrange("p (c t) -> p c t", c=8)
        xst = xst.rearrange("p (n c) -> p n c", n=NT)
        btk_ = btk_.rearrange("p (n c) -> p n c", n=NT)
        xp_, xpk = ar.get(520, F32)
        acc, acck = ar.get(512, F32)
        dtt, dtk = ar.get(NT * 16, F32)
        att_, atk = ar.get(NT * 16, F32)
        dtt = dtt.rearrange("p (n h) -> p n h", n=NT)
        att_ = att_.rearrange("p (n h) -> p n h", n=NT)
        cst_in, csik = ar.get(4 * 36, F32)
        cst_out, csok = ar.get(4 * 36, F32)
        cst_in = cst_in.rearrange("p (q c j) -> p q c j", q=4, c=12)
        cst_out = cst_out.rearrange("p (q c j) -> p q c j", q=4, c=12)
        acs, acsk = ar.get(64, F32)
        edb, edbk = ar.get(2 * 16, F32)
        TriA, trak = ar.get(4 * 128, F32)
        Lm, lmk = ar.get(4 * 128, F32)
        ea4, ea4k = ar.get(4 * 128, BF16)
        Mb, mbk = ar.get(16 * 128, BF16)
        Ct, ctk = ar.get(16 * 128, BF16)
        CBT, cbtk = ar.get(256, F32)
        xdt, xdk = ar.get(1024, BF16)
        xw, xwk = ar.get(1024, BF16)
        y1, y1k = ar.get(128, F32)
        ST, STk = ar.get(1024, F32)
        Sb2, Sb2k = ar.get(2 * 1024, BF16)
        zst, zstk = ar.get(512, BF16)
        TriA = TriA.rearrange("p (h t) -> p h t", h=4)
        Lm = Lm.rearrange("p (h t) -> p h t", h=4)
        ea4 = ea4.rearrange("p (h t) -> p h t", h=4)
        ysq, ysqk = Mb.rearrange("p (c t) -> p c t", c=4), mbk
        Mb = Mb.rearrange("p (h t) -> p h t", h=16)
        Ct = Ct.rearrange("p (h t) -> p h t", h=16)
        CBT = CBT.rearrange("p (g t) -> p g t", g=2)
        xdt = xdt.rearrange("p (h q) -> p h q", h=16)
        xw = xw.rearrange("p (h q) -> p h q", h=16)
        ST = ST.rearrange("p (h q) -> p h q", h=16)
        Sb2 = Sb2.rearrange("p (s h q) -> p s h q", s=2, h=16)

        if kind == "s":
            dma(QM, cst_in, sc_in[l].rearrange("q p c j -> p q c j"), r=[], w=csik, chan="cvs")
            xp3 = xp_[:, 0:4 * 35].rearrange("p (q t) -> p q t", q=4)
        else:
            xp3 = xp_[:, 0:3 + T].rearrange("p (q t) -> p q t", q=1)
        Lc = segs[0]["L"]

        def cb_x(j, ps, pk, m):
            if kind == "s":
                A("dve", lambda h: h.tensor_copy(out=xp3[:, :, 0:3], in_=cst_in[:, :, j, :]), r=csik, w=xpk)
                act(xp3[:, :, 3:3 + Lc], ps.rearrange("p (q t) -> p q t", q=4)[:, :, 0:Lc], AF.Copy, r=[pk], w=xpk)
            else:
                if tl["first"]:
                    A("dve", lambda h: h.memset(xp3[:, 0, 0:3], 0.0), w=xpk)
                else:
                    A("dve", lambda h: h.tensor_copy(out=xp3[:, 0, 0:3], in_=convst[:, l, j, :]), r=["convst"], w=xpk)
                act(xp3[:, 0, 3:3 + Lc], ps, AF.Copy, r=[pk], w=xpk)
            nq = xp3.shape[1]
            a3 = acc[:, 0:nq * Lc].rearrange("p (q t) -> p q t", q=nq)
            cw = C("convw", l, 48)
            ts(a3, xp3[:, :, 0:Lc], cw[:, j:j + 1], C("convb", l, 12)[:, j:j + 1], ALU.mult, ALU.add,
               r=xpk + ["cst"], w=acck)
            for jj in range(1, 4):
                stt(a3, xp3[:, :, jj:jj + Lc], cw[:, jj * 12 + j:jj * 12 + j + 1], a3, ALU.mult, ALU.add,
                    r=xpk + acck + ["cst"], w=acck)
            if kind == "s":
                act(xbc[:, j, 0:T].rearrange("p (q t) -> p q t", q=4)[:, :, 0:Lc], a3, AF.Silu, r=acck, w=xbk)
                A("dve", lambda h: h.tensor_copy(out=cst_out[:, :, j, :], in_=xp3[:, :, Lc:Lc + 3]), r=xpk, w=csok)
            else:
                act(xbc[:, j, 0:T], a3[:, 0, :], AF.Silu, r=acck, w=xbk)
                A("dve", lambda h: h.tensor_copy(out=convst[:, l, j, :], in_=xp3[:, 0, Lc:Lc + 3]), r=xpk, w=["convst"])
        if kind == "s":
            A("dve", lambda h: h.memset(xbc[:, :, 0:T], 0.0), w=xbk)
        linear_fm("w_in", l, 0, KC, OFF_XBC, 1536, hb, ["hb"], T, cb_x, after_group=lambda: drain(amg, 1))
        if kind == "s":
            dma(QM, cso[l].rearrange("q p c j -> p q c j"), cst_out, r=csok, w=[], chan="cvs")
        elif tl["last"]:
            dma(QM, cp[l, segs[0]["q"]], convst[:, l, :, :], r=["convst"], w=[], chan="cvs")

        S.tag = "B.dt_tr"
        vdt, wkdt = load_w(wsrc("w_in", l, 0, KC, OFF_DT, 16), KC, 16)
        for tc in range(NT):
            ps, pk = psum()
            mms(ps[:, 0:16], [(hb[:, k, tc * 128:(tc + 1) * 128], vdt[:, k, 0:16]) for k in range(KC)], r=[wkdt, "hb"], w=[pk])
            tt(dtt[:, tc, :], ps[:, 0:16], C("dtb", l, 16), ALU.add, r=[pk, "cst"], w=dtk)
            act(dtt[:, tc, :], dtt[:, tc, :], AF.Exp, r=dtk, w=dtk)
            act(dtt[:, tc, :], dtt[:, tc, :], AF.Ln, r=dtk, w=dtk, bias=1.0)
            tt(att_[:, tc, :], dtt[:, tc, :], aneg[:, l, :], ALU.mult, r=dtk + ["aneg"], w=atk)
        for tc in range(NT):
            pst, ptk = psum()
            pstb = pst[:].bitcast(BF16)

            def trx(h, pstb=pstb, tc=tc):
                ins = None
                for c in range(8):
                    ins = h.transpose(out=pstb[:, c * 128:(c + 1) * 128], in_=xbc[:, c, tc * 128:(tc + 1) * 128], identity=idb[:])
                return ins
            A("pe", trx, r=xbk + ["idb"], w=[ptk])
            act(xst[:, tc, :], pstb[:, 0:1024], AF.Copy, r=[ptk], w=xsk)
            pst2, ptk2 = psum()
            pstb2 = pst2[:].bitcast(BF16)

            def trb(h, pstb2=pstb2, tc=tc):
                ins = None
                for c in range(2):
                    ins = h.transpose(out=pstb2[:, c * 128:(c + 1) * 128], in_=xbc[:, 8 + c, tc * 128:(tc + 1) * 128], identity=idb[:])
                return ins
            A("pe", trb, r=xbk + ["idb"], w=[ptk2])
            act(btk_[:, tc, :], pstb2[:, 0:256], AF.Copy, r=[ptk2], w=btkk)

        drain(amg)
        S.tag = "B.ssd"
        tri = C("tri_p") if kind == "p" else C("tri_s")
        segb = C("segb_p") if kind == "p" else C("segb_s")
        negm = C("neg_p") if kind == "p" else C("neg_s")
        if kind == "p":
            if tl["first"]:
                A("dve", lambda h: h.memset(ST[:], 0.0), w=STk)
            else:
                dma(QM, ST, sp_o[l, segs[0]["q"]], r=[], w=STk, chan="sst")
            A("dve", lambda h: h.tensor_copy(out=Sb2[:, 0], in_=ST[:]), r=STk, w=Sb2k)
        for tc in range(NT):
            csegs = [sg for sg in segs if sg["c0"] // 128 == tc] if kind == "s" else [dict(q=segs[0]["q"], c0=tc * 128, L=128)]
            ps, pk = psum()
            mms(ps[:, 0:16], [(tri, att_[:, tc, :])], r=atk + ["cst"], w=[pk])
            mms(ps[:, 16:32], [(segb, att_[:, tc, :])], r=atk + ["cst"], w=[pk])
            A("dve", lambda h, ps=ps: h.tensor_copy(out=acs[:, 0:16], in_=ps[:, 0:16]), r=[pk], w=acsk)
            ts(acs[:, 16:32], ps[:, 0:16], -1.0, None, ALU.mult, None, r=[pk], w=acsk)
            tt(acs[:, 32:48], ps[:, 16:32], acs[:, 0:16], ALU.subtract, r=[pk] + acsk, w=acsk)
            act(acs[:, 48:64], acs[:, 32:48], AF.Exp, r=acsk, w=acsk)
            psc, pck = psum()
            for g in range(2):
                mms(psc[:, g * 128:(g + 1) * 128], [(xbc[:, 8 + g, tc * 128:(tc + 1) * 128], xbc[:, 10 + g, tc * 128:(tc + 1) * 128])],
                    r=xbk, w=[pck])
            act(CBT[:], psc[:, 0:256].rearrange("p (g t) -> p g t", g=2), AF.Copy, r=[pck], w=cbtk)
            for q4 in range(4):
                g = q4 // 2
                tt(TriA[:], tri.unsqueeze(1).broadcast_to([128, 4, 128]),
                   att_[:, tc, q4 * 4:(q4 + 1) * 4].unsqueeze(2).broadcast_to([128, 4, 128]), ALU.mult, r=atk + ["cst"], w=trak)
                psA, pAk = psum()
                mms(psA[:, 0:512], [(onf[:], TriA[:])], r=trak + ["ones"], w=[pAk])
                act(ea4[:], psA[:, 0:512].rearrange("p (h t) -> p h t", h=4), AF.Exp, r=[pAk], w=ea4k)
                tt(Ct[:, q4 * 4:(q4 + 1) * 4, :], ea4[:], xbc[:, 10 + g:11 + g, tc * 128:(tc + 1) * 128].broadcast_to([128, 4, 128]),
                   ALU.mult, r=ea4k + xbk, w=ctk)
                psB, pBk = psum()
                mms(psB[:, 0:512], [(onf[:], TriA[:]), (C("ident"), negm)], r=trak + ["ones", "cst"], w=[pBk])
                for hh in range(4):
                    hd = q4 * 4 + hh
                    act(Lm[:, hh, :], psB[:, hh * 128:(hh + 1) * 128], AF.Exp, r=[pBk] + acsk, w=lmk, bias=acs[:, 16 + hd:17 + hd])
                tt(Mb[:, q4 * 4:(q4 + 1) * 4, :], Lm[:], CBT[:, g:g + 1, :].broadcast_to([128, 4, 128]), ALU.mult, r=lmk + cbtk, w=mbk)
            tt(xdt[:], xst[:, tc, :].rearrange("p (h q) -> p h q", h=16), dtt[:, tc, :].unsqueeze(2).broadcast_to([128, 16, 64]),
               ALU.mult, r=xsk + dtk, w=xdk)
            tt(xw[:], xdt[:], acs[:, 48:64].unsqueeze(2).broadcast_to([128, 16, 64]), ALU.mult, r=xdk + acsk, w=xwk)
            for half in range(2):
                psy, pyk = psum()

                def yf(h, psy=psy, half=half, tc=tc, csegs=csegs):
                    ins = None
                    for cc in range(4):
                        hc = half * 4 + cc
                        for hh in range(2):
                            hd = hc * 2 + hh
                            o_ = psy[hh * 64:(hh + 1) * 64, cc * 128:(cc + 1) * 128]
                            h.matmul(o_, lhsT=xdt[:, hd, :], rhs=Mb[:, hd, :], start=True, stop=False)
                            for si, sg in enumerate(csegs):
                                lc0 = sg["c0"] % 128
                                ins = h.matmul(psy[hh * 64:(hh + 1) * 64, cc * 128 + lc0:cc * 128 + lc0 + sg["L"]],
                                               lhsT=Sb2[:, si if kind == "s" else 0, hd, :], rhs=Ct[:, hd, lc0:lc0 + sg["L"]],
                                               start=False, stop=(si == len(csegs) - 1))
                    return ins
                if kind == "s" and half == 0:
                    for si, sg in enumerate(csegs):
                        dma(QM, ST, ss_in[l, sg["q"]], r=[], w=STk, chan="sst")
                        A("dve", lambda h, si=si: h.tensor_copy(out=Sb2[:, si], in_=ST[:]), r=STk, w=Sb2k)
                A("pe", yf, r=xdk + mbk + Sb2k + ctk, w=[pyk])
                for cc in range(4):
                    hc = half * 4 + cc
                    stt(ybT[:, hc, tc * 128:(tc + 1) * 128], xbc[:, hc, tc * 128:(tc + 1) * 128], C("Dp", l, 8)[:, hc:hc + 1],
                        psy[:, cc * 128:(cc + 1) * 128], ALU.mult, ALU.add, r=xbk + [pyk, "cst"], w=ybk)
            for si, sg in enumerate(csegs):
                lc0, L = sg["c0"] % 128, sg["L"]
                if kind == "s":
                    dma(QM, ST, ss_in[l, sg["q"]], r=[], w=STk, chan="sst")
                psd, pdk = psum()
                sel = onf[:] if kind == "p" else C("sel_s%d" % si)
                mms(psd[:, 0:16], [(sel, att_[:, tc, :])], r=atk + ["ones", "cst"], w=[pdk])
                act(edb[:, 0:16], psd[:, 0:16], AF.Exp, r=[pdk], w=edbk)
                tt(ST[:], ST[:], edb[:, 0:16].unsqueeze(2).broadcast_to([128, 16, 64]), ALU.mult, r=STk + edbk, w=STk)
                for half in range(2):
                    pss, psk = psum()

                    def suf(h, pss=pss, half=half, lc0=lc0, L=L, tc=tc):
                        ins = None
                        for hh in range(8):
                            hd = half * 8 + hh
                            g = hd // 8
                            ins = h.matmul(pss[:, hh * 64:(hh + 1) * 64], lhsT=btk_[lc0:lc0 + L, tc, g * 128:(g + 1) * 128],
                                           rhs=xw[lc0:lc0 + L, hd, :], start=True, stop=True)
                        return ins
                    A("pe", suf, r=btkk + xwk, w=[psk])
                    tt(ST[:, half * 8:(half + 1) * 8, :], ST[:, half * 8:(half + 1) * 8, :],
                       pss[:, 0:512].rearrange("p (h q) -> p h q", h=8), ALU.add, r=STk + [psk], w=STk)
                if kind == "s":
                    dma(QM, sso[l, sg["q"]], ST, r=STk, w=[], chan="sst")
                else:
                    A("dve", lambda h: h.tensor_copy(out=Sb2[:, 0], in_=ST[:]), r=STk, w=Sb2k)
        if kind == "p":
            dma(QM, sp_o[l, segs[0]["q"]], ST, r=STk, w=[], chan="sst")
        S.tag = "B.znorm"
        def cb_z(j, ps, pk, m):
            act(zst[:, 0:T], ps, AF.Silu, r=[pk], w=zstk)
            tt(ybT[:, j, 0:T], ybT[:, j, 0:T], zst[:, 0:T], ALU.mult, r=ybk + zstk, w=ybk)
        linear_fm("w_in", l, 0, KC, OFF_Z, 1024, hb, ["hb"], T, cb_z)
        for g in range(2):
            rms_to(ysq, ysqk, [ybT[:, g * 4 + c, 0:T] for c in range(4)], ybk, on9, T, 4)
            for c in range(4):
                hc = g * 4 + c
                stt(ybT[:, hc, 0:T], ybT[:, hc, 0:T], C("gbn", l, 8)[:, hc:hc + 1], rstd[:, 0:T], ALU.mult, ALU.mult,
                    r=ybk + ["rstd", "cst"], w=ybk)
        drain(merge_branch(l, 1, ybT, ybk, T, False, "B.merge"))

        S.tag = "wout"
        mg, mgk = mgp, ["mg"]

        def cb_o(j, ps, pk, m):
            for sg in segs:
                c0, L, b = sg["c0"], sg["L"], sg["b"]
                stt(xT[:, j, c0:c0 + L], ps[:, c0:c0 + L], modT[:, l, 32 + j, b:b + 1], xT[:, j, c0:c0 + L], ALU.mult, ALU.add,
                    r=[pk, "modT", "xT"], w=["xT"])
        linear_fm("w_out", l, 0, KC, 0, D, mg, mgk, T, cb_o)

        S.tag = "ffn"
        norm_mod(tl, l, A2, 48, hb, "hb")
        ar.reset()
        hid, hidk = ar.get(16 * 512, BF16)
        hid = hid.rearrange("p (c t) -> p c t", c=16)
        rl = [ar.get(512, BF16) for _ in range(2)]
        for qtr in range(4):
            def cb_up(j, ps, pk, m):
                r_, rk_ = rl[j % 2]
                act(r_[:, 0:T], ps, AF.Relu, r=[pk], w=rk_)
                tt(hid[:, j, 0:T], r_[:, 0:T], r_[:, 0:T], ALU.mult, r=rk_, w=hidk)
            linear_fm("w_up", l, 0, KC, qtr * 2048, 2048, hb, ["hb"], T, cb_up)

            def cb_dn(j, ps, pk, m):
                for sg in segs:
                    c0, L, b = sg["c0"], sg["L"], sg["b"]
                    stt(xT[:, j, c0:c0 + L], ps[:, c0:c0 + L], modT[:, l, 80 + j, b:b + 1], xT[:, j, c0:c0 + L], ALU.mult, ALU.add,
                        r=[pk, "modT", "xT"], w=["xT"])
            linear_fm("w_dn", l, qtr * 2048, 16, 0, D, hid, hidk, T, cb_dn)

    for tl in tiles:
        T = tl["T"]
        if tl["kind"] == "p":
            q = tl["segs"][0]["q"]
            dma(QM, xT[:, :, 0:T], xp[q, :, tl["t0"]:tl["t0"] + T].rearrange("(c p) t -> p c t", p=128), r=[], w=["xT"], chan="xin")
        else:
            dma(QM, xT[:, :, 0:T], xs.rearrange("(c p) t -> p c t", p=128), r=[], w=["xT"], chan="xin")
        for l in range(depth):
            layer(tl, l)
        S.tag = "final"
        ar.reset()
        sq, sqk = ar.get(16 * 512, BF16)
        sq = sq.rearrange("p (c t) -> p c t", c=16)
        rms_to(sq, sqk, [xT[:, c, 0:T] for c in range(16)], ["xT"], on11, T, 16)
        for c in range(16):
            stt(xT[:, c, 0:T], xT[:, c, 0:T], C("fg")[:, c:c + 1], rstd[:, 0:T], ALU.mult, ALU.mult, r=["xT", "rstd", "cst"], w=["xT"])
        if tl["kind"] == "p":
            q = tl["segs"][0]["q"]
            dma(QM, yp[q, :, tl["t0"]:tl["t0"] + T].rearrange("(c p) t -> p c t", p=128), xT[:, :, 0:T], r=["xT"], w=[], chan="xin")
        else:
            dma(QM, ys.rearrange("(c p) t -> p c t", p=128), xT[:, :, 0:T], r=["xT"], w=[], chan="xin")

    S.emit()
    es.close()
    return nc


def _pack_cst(depth, P, lay, ncst):
    c = np.zeros((128, ncst), np.float32)

    def put(name, arr):
        o, n = lay[name]
        c[:, o:o + n] = np.asarray(arr, np.float32).reshape(128, n)

    def fm(v, nch):
        v = np.asarray(v)
        L = v.shape[0]
        return v.reshape(L, nch, 128).transpose(2, 0, 1).reshape(128, L * nch)

    put("ident", np.eye(128))
    put("n1g", fm(P["norm1_g"], 16))
    put("n2g", fm(P["norm2_g"], 16))
    put("fg", fm(P["final_g"][None], 16))
    put("bmod", fm(P["b_mod"], 96))
    put("lbraw", fm(P["hgrn_lb"], 8))
    put("gon", fm(P["hgrn_onorm_g"], 8))
    cw = np.asarray(P["ssm_conv_w"])
    put("convw", cw.reshape(depth, 4, 12, 128).transpose(3, 0, 1, 2).reshape(128, depth * 48))
    put("convb", fm(P["ssm_conv_b"], 12))
    put("dtb", np.broadcast_to(np.asarray(P["ssm_dt_bias"]).reshape(1, depth * 16), (128, depth * 16)))
    put("alog", np.broadcast_to(np.asarray(P["ssm_a_log"]).reshape(1, depth * 16), (128, depth * 16)))
    dp = np.repeat(np.asarray(P["ssm_d"]), 64, axis=1)
    put("Dp", fm(dp, 8))
    put("gbn", fm(P["ssm_onorm_g"], 8))
    p = np.arange(128)[:, None]
    t64 = np.arange(64)[None, :]
    put("maskA", np.tile(((p % 64) <= t64).astype(np.float32)[:, None, :], (1, 8, 1)))
    bm = np.ones(512, np.float32)
    bm[::64] = 0
    put("blkm", np.broadcast_to(bm, (128, 512)))
    t = np.arange(128)[None, :]
    tri_p = (p <= t).astype(np.float32)
    valid = lambda i: (i % 64) < 32
    same = (p // 64) == (t // 64)
    tri_s = (same & valid(p) & valid(t) & ((p % 64) <= (t % 64))).astype(np.float32)
    put("tri_p", tri_p)
    put("tri_s", tri_s)
    put("segb_p", np.ones((128, 128)))
    put("segb_s", (same & valid(p) & valid(t)).astype(np.float32))
    put("sel_s0", np.broadcast_to(((p // 64 == 0) & valid(p)).astype(np.float32), (128, 128)))
    put("sel_s1", np.broadcast_to(((p // 64 == 1) & valid(p)).astype(np.float32), (128, 128)))
    put("neg_p", np.tile(np.where(tri_p > 0, 0.0, NEG), (1, 4)))
    put("neg_s", np.tile(np.where(tri_s > 0, 0.0, NEG), (1, 4)))
    put("triu", tri_p)
    put("triu_s", tri_s)
    return c


def _host_inputs(cfg, inp, core):
    depth, LP, NPS = cfg["depth"], cfg["LP"], cfg["NPS"]
    lay, ncst = cst_layout(depth)
    f = lambda a: np.ascontiguousarray(np.asarray(a, np.float32))
    pb = slice(core * NPS, (core + 1) * NPS)
    sbs = slice(core * 4, (core + 1) * 4)
    m = {}
    m["xp"] = f(np.asarray(inp["x_prompt"])[pb].transpose(0, 2, 1))
    xs = np.zeros((D, 256), np.float32)
    xsl = np.asarray(inp["x_sample"])[sbs]
    for j in range(4):
        xs[:, 64 * j:64 * j + 32] = xsl[j].T
    m["xs"] = xs
    cc = np.concatenate([np.asarray(inp["c_prompt"])[pb], np.asarray(inp["c_sample"])[sbs]], 0)
    m["cT"] = f(cc.reshape(cc.shape[0], 16, 128).transpose(2, 1, 0))
    m["sh_in"] = f(np.asarray(inp["state_hgrn"])[:, sbs])
    m["ss_in"] = f(np.asarray(inp["state_ssm"])[:, sbs].transpose(0, 1, 4, 2, 3))
    sc = np.asarray(inp["state_conv"])[:, sbs]
    m["sc_in"] = f(sc.reshape(depth, 4, 3, 12, 128).transpose(0, 1, 4, 3, 2))
    for k_, n_ in (("w_mod", "w_mod"), ("w_in", "w_in"), ("w_branch", "w_br"), ("w_out", "w_out"), ("w_up", "w_up"), ("w_down", "w_dn")):
        m[n_] = f(inp[k_])
    m["cst"] = _pack_cst(depth, inp, lay, ncst)
    m["lnp"] = f(np.stack([np.asarray(inp["cmlp_ln_g"]), np.asarray(inp["cmlp_ln_b"])], 1))
    m["wsT"] = f(np.asarray(inp["cmlp_ws"]).transpose(0, 3, 1, 2))
    m["bs"] = f(inp["cmlp_bs"])
    return m


def run(cfg, inp, ncores=8):
    import time
    t0 = time.time()
    nc = build(cfg)
    t1 = time.time()
    in_maps = [_host_inputs(cfg, inp, c) for c in range(ncores)]
    t2 = time.time()
    res = run_bass_kernel_spmd(nc, in_maps, core_ids=list(range(ncores)))
    print("[kernel] build %.1fs host-layout %.1fs launch %.1fs" % (t1 - t0, t2 - t1, time.time() - t2), flush=True)
    depth, LP, NPS = cfg["depth"], cfg["LP"], cfg["NPS"]
    R = res.results
    cat = lambda fn, ax=0: np.concatenate([fn(r) for r in R], axis=ax)
    y_p = cat(lambda r: r["yp"].transpose(0, 2, 1))
    y_s = cat(lambda r: np.stack([r["ys"][:, 64 * j:64 * j + 32].T for j in range(4)], 0))
    h_p = cat(lambda r: r["hp"], 1)
    s_p = cat(lambda r: r["sp_o"].transpose(0, 1, 3, 4, 2), 1)
    c_p = cat(lambda r: r["cp"].transpose(0, 1, 4, 3, 2).reshape(depth, NPS, 3, 1536), 1)
    h_s = cat(lambda r: r["hs"], 1)
    s_s = cat(lambda r: r["sso"].transpose(0, 1, 3, 4, 2), 1)
    c_s = cat(lambda r: r["cso"].transpose(0, 1, 4, 3, 2).reshape(depth, 4, 3, 1536), 1)
    v_s = cat(lambda r: np.stack([r["vso"][:, 64 * j:64 * j + 32, :] for j in range(4)], 1), 1)
    outs = (y_p, y_s, h_p, s_p, c_p, h_s, s_s, c_s, v_s)
    return tuple(np.ascontiguousarray(o, dtype=np.float32) for o in outs)


def kernel(**inputs):
    cfg = dict(depth=4, LP=2048, NPS=2)
    return run(cfg, inputs, 8)
```
